# Optimizing a Trainium2 kernel written in Bass

```python
import math
import jax, jax.numpy as jnp
from jax import lax
import numpy as np

D_MODEL = 1024
BATCH = 8
SEQ = 2048
DEPTH = 2

HEAD_DIM = 64
N_DIFF_HEADS = 4
DIFF_V_DIM = 2 * HEAD_DIM
DIFF_QK = N_DIFF_HEADS * 2 * HEAD_DIM
DIFF_WIDTH = N_DIFF_HEADS * DIFF_V_DIM
N_FOX_HEADS = 8
FOX_WIDTH = N_FOX_HEADS * HEAD_DIM
N_BUCKETS = 32
MAX_DISTANCE = 128
Q_BLOCK = 128
N_KEYS = 128
N_EXPERTS = N_KEYS * N_KEYS
PEER_HEADS = 8
PEER_TOPK = 16
PEER_QDIM = 256
PEER_HALF = PEER_QDIM // 2
TOKEN_BLOCK = 128
EPS = 1e-6
IN_SIZES = (DIFF_QK, DIFF_QK, DIFF_WIDTH, FOX_WIDTH, FOX_WIDTH, FOX_WIDTH, N_FOX_HEADS, D_MODEL, D_MODEL)
IN_WIDTH = sum(IN_SIZES)

kernel_name = "hybrid_diffattn_fox_peer"


def rms_norm(x, g):
    xf = x.astype(jnp.float32)
    y = xf * lax.rsqrt(jnp.mean(xf * xf, axis=-1, keepdims=True) + EPS)
    return (y * g.astype(jnp.float32)).astype(x.dtype)


def t5_bucket(dist):
    n = jnp.maximum(dist, 0)
    max_exact = N_BUCKETS // 2
    nf = jnp.maximum(n, 1).astype(jnp.float32)
    large = max_exact + (jnp.log(nf / max_exact) / math.log(MAX_DISTANCE / max_exact)
                         * (N_BUCKETS - max_exact)).astype(jnp.int32)
    large = jnp.minimum(large, N_BUCKETS - 1)
    return jnp.where(n < max_exact, n, large)


def diff_attention(q, k, v, rel_bias, lam, lam_init, subln_g):
    B, S = q.shape[0], q.shape[1]
    scale = HEAD_DIM ** -0.5
    lf = lam.astype(jnp.float32)
    lam_full = jnp.exp(jnp.sum(lf[0] * lf[1])) - jnp.exp(jnp.sum(lf[2] * lf[3])) + lam_init
    outs = []
    for blk in range(S // Q_BLOCK):
        q0 = blk * Q_BLOCK
        end = q0 + Q_BLOCK
        logits = jnp.einsum('bqhmd,bkhmd->bhmqk', q[:, q0:end], k[:, :end]).astype(jnp.float32) * scale
        dist = (q0 + jnp.arange(Q_BLOCK))[:, None] - jnp.arange(end)[None, :]
        bias = jnp.transpose(rel_bias[t5_bucket(dist)], (2, 0, 1))[:, None]
        logits = jnp.where(dist >= 0, logits + bias.astype(jnp.float32), -jnp.inf)
        probs = jax.nn.softmax(logits, axis=-1)
        weights = probs[:, :, 0] - lam_full * probs[:, :, 1]
        outs.append(jnp.einsum('bhqk,bkhd->bqhd', weights.astype(v.dtype), v[:, :end]))
    o = jnp.concatenate(outs, axis=1)
    o = rms_norm(o, subln_g) * (1.0 - lam_init)
    return o.reshape(B, S, DIFF_WIDTH)


def forgetting_attention(q, k, v, f_logit):
    B, S = q.shape[0], q.shape[1]
    scale = HEAD_DIM ** -0.5
    c = jnp.swapaxes(jnp.cumsum(jax.nn.log_sigmoid(f_logit.astype(jnp.float32)), axis=1), 1, 2)
    outs = []
    for blk in range(S // Q_BLOCK):
        q0 = blk * Q_BLOCK
        end = q0 + Q_BLOCK
        logits = jnp.einsum('bqhd,bkhd->bhqk', q[:, q0:end], k[:, :end]).astype(jnp.float32) * scale
        decay = c[:, :, q0:end, None] - c[:, :, None, :end]
        dist = (q0 + jnp.arange(Q_BLOCK))[:, None] - jnp.arange(end)[None, :]
        logits = jnp.where(dist >= 0, logits + decay, -jnp.inf)
        probs = jax.nn.softmax(logits, axis=-1)
        outs.append(jnp.einsum('bhqk,bkhd->bqhd', probs.astype(v.dtype), v[:, :end]))
    return jnp.concatenate(outs, axis=1).reshape(B, S, FOX_WIDTH)


def peer_ffn(h, w_query, sub_keys, expert_u, expert_v):
    B, S, D = h.shape
    T = B * S
    t = h.reshape(T, D)
    q = (t @ w_query).reshape(T, PEER_HEADS, 2, PEER_HALF)
    s = jnp.einsum('thpd,pnd->thpn', q, sub_keys).astype(jnp.float32)
    top_s, top_i = lax.top_k(s, PEER_TOPK)
    cand_s = (top_s[:, :, 0, :, None] + top_s[:, :, 1, None, :]).reshape(T, PEER_HEADS, PEER_TOPK * PEER_TOPK)
    cand_i = (top_i[:, :, 0, :, None] * N_KEYS + top_i[:, :, 1, None, :]).reshape(T, PEER_HEADS, PEER_TOPK * PEER_TOPK)
    best_s, best_pos = lax.top_k(cand_s, PEER_TOPK)
    experts = jnp.take_along_axis(cand_i, best_pos, axis=-1).reshape(T, PEER_HEADS * PEER_TOPK)
    gates = jax.nn.softmax(best_s, axis=-1).reshape(T, PEER_HEADS * PEER_TOPK)
    nblk = T // TOKEN_BLOCK

    def expert_block(args):
        tb, eb, gb = args
        act = jax.nn.gelu(jnp.einsum('td,ted->te', tb, expert_u[eb]).astype(jnp.float32), approximate=False)
        return jnp.einsum('te,ted->td', (gb * act).astype(tb.dtype), expert_v[eb])

    out = lax.map(expert_block, (t.reshape(nblk, TOKEN_BLOCK, D),
                                 experts.reshape(nblk, TOKEN_BLOCK, -1),
                                 gates.reshape(nblk, TOKEN_BLOCK, -1)))
    return out.reshape(B, S, D)


def setup_inputs(seed: int = 0) -> dict:
    key = jax.random.key(seed)
    ks = jax.random.split(key, 17)
    nrm = lambda k, shape, scale: jax.random.normal(k, shape, jnp.float32) * scale
    return {
        "x": nrm(ks[0], (BATCH, SEQ, D_MODEL), 1.0),
        "norm1_g": 1.0 + nrm(ks[1], (DEPTH, D_MODEL), 0.02),
        "w_in": nrm(ks[2], (DEPTH, D_MODEL, IN_WIDTH), D_MODEL ** -0.5),
        "b_forget": 2.0 + nrm(ks[3], (DEPTH, N_FOX_HEADS), 0.1),
        "diff_lambda": nrm(ks[4], (DEPTH, 4, HEAD_DIM), 0.1),
        "diff_subln_g": 1.0 + nrm(ks[5], (DEPTH, DIFF_V_DIM), 0.02),
        "w_diff_o": nrm(ks[6], (DEPTH, DIFF_WIDTH, D_MODEL), DIFF_WIDTH ** -0.5),
        "w_fox_o": nrm(ks[7], (DEPTH, FOX_WIDTH, D_MODEL), FOX_WIDTH ** -0.5),
        "w_out": nrm(ks[8], (DEPTH, D_MODEL, D_MODEL), D_MODEL ** -0.5),
        "norm2_g": 1.0 + nrm(ks[9], (DEPTH, D_MODEL), 0.02),
        "w_query": nrm(ks[10], (DEPTH, D_MODEL, PEER_HEADS * PEER_QDIM), D_MODEL ** -0.5),
        "sub_keys": nrm(ks[11], (DEPTH, 2, N_KEYS, PEER_HALF), PEER_HALF ** -0.5),
        "expert_u": nrm(ks[12], (DEPTH, N_EXPERTS, D_MODEL), D_MODEL ** -0.5),
        "expert_v": nrm(ks[13], (DEPTH, N_EXPERTS, D_MODEL), (PEER_HEADS * PEER_TOPK) ** -0.5),
        "rel_bias": nrm(ks[14], (N_BUCKETS, N_DIFF_HEADS), 0.5),
        "final_norm_g": 1.0 + nrm(ks[15], (D_MODEL,), 0.02),
    }


def reference(x, norm1_g, w_in, b_forget, diff_lambda, diff_subln_g, w_diff_o, w_fox_o, w_out,
              norm2_g, w_query, sub_keys, expert_u, expert_v, rel_bias, final_norm_g):
    B, S, D = x.shape
    split_points = np.cumsum(IN_SIZES)[:-1].tolist()
    for layer in range(DEPTH):
        lam_init = 0.8 - 0.6 * math.exp(-0.3 * layer)
        h = rms_norm(x, norm1_g[layer])
        proj = h @ w_in[layer]
        dq, dk, dv, fq, fk, fv, ff, ga, gb = jnp.split(proj, split_points, axis=-1)
        y_diff = diff_attention(dq.reshape(B, S, N_DIFF_HEADS, 2, HEAD_DIM),
                                dk.reshape(B, S, N_DIFF_HEADS, 2, HEAD_DIM),
                                dv.reshape(B, S, N_DIFF_HEADS, DIFF_V_DIM),
                                rel_bias, diff_lambda[layer], lam_init, diff_subln_g[layer])
        y_fox = forgetting_attention(fq.reshape(B, S, N_FOX_HEADS, HEAD_DIM),
                                     fk.reshape(B, S, N_FOX_HEADS, HEAD_DIM),
                                     fv.reshape(B, S, N_FOX_HEADS, HEAD_DIM),
                                     ff + b_forget[layer])
        merged = jax.nn.sigmoid(ga) * (y_diff @ w_diff_o[layer]) + jax.nn.sigmoid(gb) * (y_fox @ w_fox_o[layer])
        x = x + merged @ w_out[layer]
        x = x + peer_ffn(rms_norm(x, norm2_g[layer]), w_query[layer], sub_keys[layer],
                         expert_u[layer], expert_v[layer])
    return rms_norm(x, final_norm_g)
```

```python
import math
import numpy as np
from contextlib import ExitStack
import concourse.bass as bass
import concourse.mybir as mybir
from concourse.bass_utils import run_bass_kernel_spmd

F32 = mybir.dt.float32
BF16 = mybir.dt.bfloat16
AF = mybir.ActivationFunctionType
ALU = mybir.AluOpType
AX = mybir.AxisListType

S = 2048
D = 1024
NT = 16
DEPTH = 2
INW = 5128
EPS = 1e-6
NEG = -30000.0
ENGS = ("pe", "act", "dve", "pool", "sp")


class R:
    __slots__ = ("name", "w", "rd")

    def __init__(self, name=""):
        self.name = name
        self.w = None
        self.rd = {}


class Prog:
    def __init__(self, nc, stack):
        self.nc = nc
        self.stack = stack
        self.ops = {e: [] for e in ENGS}
        self.sems = {}
        self.cnt = {}
        self.seen = {e: {} for e in ENGS}
        for e in ENGS:
            self._mksem(e)
        self.n_dma_sem = 0

    CH = 16000

    def _mksem(self, key):
        self.sems[key] = self.stack.enter_context(self.nc.semaphore(str(key)))
        self.cnt[key] = 0

    def _semval(self, k, v):
        if k in ENGS:
            c = (v - 1) // self.CH
            key = (k, c)
            if key not in self.sems:
                self.sems[key] = self.stack.enter_context(self.nc.semaphore("%s_%d" % (k, c)))
            return self.sems[key], (v - 1) % self.CH + 1
        return self.sems[k], v

    def dma_sem(self):
        key = "dma%d" % self.n_dma_sem
        self.n_dma_sem += 1
        self._mksem(key)
        return key

    def _deps(self, eng, reads, writes):
        deps = {}

        def add(d):
            if d is None:
                return
            k, v = d
            if deps.get(k, 0) < v:
                deps[k] = v
        for r in reads:
            add(r.w)
        for r in writes:
            add(r.w)
            for k, v in r.rd.items():
                add((k, v))
        waits = []
        for k, v in deps.items():
            if k == eng and eng == "pe":
                continue
            if self.seen[eng].get(k, 0) >= v:
                continue
            self.seen[eng][k] = v
            waits.append(self._semval(k, v))
        return waits

    def op(self, eng, name, kw, reads=(), writes=()):
        wl = self._deps(eng, reads, writes)
        self.cnt[eng] += 1
        seq = self.cnt[eng]
        self.ops[eng].append((wl, name, kw, self._semval(eng, seq)[0], 1))
        for r in writes:
            r.w = (eng, seq)
            r.rd = {}
        for r in reads:
            r.rd[eng] = seq

    def dma(self, eng, out, in_, semkey, reads=(), writes=()):
        wl = self._deps(eng, reads, writes)
        self.cnt[semkey] += 16
        val = self.cnt[semkey]
        self.ops[eng].append((wl, "dma_start", dict(out=out, in_=in_), self.sems[semkey], 16))
        for r in writes:
            r.w = (semkey, val)
            r.rd = {}
        for r in reads:
            r.rd[semkey] = val

    def barrier(self):
        for e in ENGS:
            wl = []
            for k in self.cnt:
                if k == e:
                    continue
                v = self.cnt[k]
                if v > self.seen[e].get(k, 0):
                    self.seen[e][k] = v
                    wl.append(self._semval(k, v))
            if wl:
                self.ops[e].append((wl, None, None, None, 0))

    def emit(self):
        nc = self.nc

        def play(e, lst):
            for wl, name, kw, sem, inc in lst:
                for s, v in wl:
                    e.wait_ge(s, v)
                if name is not None:
                    getattr(e, name)(**kw).then_inc(sem, inc)

        with nc.Block() as block:
            @block.tensor
            def _(e):
                play(e, self.ops["pe"])

            @block.scalar
            def _(e):
                play(e, self.ops["act"])

            @block.vector
            def _(e):
                play(e, self.ops["dve"])

            @block.gpsimd
            def _(e):
                play(e, self.ops["pool"])

            @block.sync
            def _(e):
                play(e, self.ops["sp"])


class T:
    def __init__(self, t, nsub=0, name=""):
        self.t = t
        self.r = R(name)
        self.rs = [R("%s%d" % (name, i)) for i in range(nsub)]

    def __getitem__(self, k):
        return self.t[k]


def t5_bucket_np(dist):
    n = np.maximum(dist, 0)
    me = 16
    nf = np.maximum(n, 1).astype(np.float32)
    large = me + (np.log(nf / me) / math.log(128 / me) * (32 - me)).astype(np.int32)
    large = np.minimum(large, 31)
    return np.where(n < me, n, large)


def build_program(debug=None, depth=DEPTH):
    nc = bass.Bass("TRN2", target_bir_lowering=False)
    dr = {}

    def din(name, shape):
        dr[name] = nc.dram_tensor(name, list(shape), F32, kind="ExternalInput").ap()
        return dr[name]

    x_d = din("x", [S, D])
    n1_d = din("norm1_g", [DEPTH, D])
    win_d = din("w_in", [DEPTH, D, INW])
    bf_d = din("b_forget", [DEPTH, 8])
    lam_d = din("diff_lambda", [DEPTH, 4, 64])
    sg_d = din("diff_subln_g", [DEPTH, 128])
    wdo_d = din("w_diff_o", [DEPTH, 512, D])
    wfo_d = din("w_fox_o", [DEPTH, 512, D])
    wout_d = din("w_out", [DEPTH, D, D])
    n2_d = din("norm2_g", [DEPTH, D])
    wq_d = din("w_query", [DEPTH, D, 2048])
    sk_d = din("sub_keys", [DEPTH, 2, 128, 128])
    eu_d = din("expert_u", [DEPTH, 16384, D])
    ev_d = din("expert_v", [DEPTH, 16384, D])
    rb_d = din("rel_bias", [32, 4])
    nf_d = din("final_norm_g", [1, D])
    bt_d = din("bias_tiles", [128, 2, 4, 128])
    cm_d = din("cmask", [128, 128])
    id_d = din("ident", [128, 128])
    s127_d = din("sel127", [128, 128])
    out_d = nc.dram_tensor("out", [S, D], F32, kind="ExternalOutput").ap()
    gscr = nc.dram_tensor("gscr", [128, 128, S], BF16).ap()
    dbg_d = None
    if debug is not None:
        dbg_d = nc.dram_tensor("dbg", [S, D], F32, kind="ExternalOutput").ap()

    with ExitStack() as st:
        P = Prog(nc, st)

        uid = [0]

        def sb(stk, name, shape, dt, nsub=0):
            uid[0] += 1
            name = "%s_%d" % (name, uid[0])
            return T(stk.enter_context(nc.sbuf_tensor(name, list(shape), dt)), nsub, name)

        xres = sb(st, "xres", [128, NT, D], F32, nsub=NT)
        pf = [T(st.enter_context(nc.psum_tensor("pf%d" % i, [128, 512], F32)), 0, "pf%d" % i) for i in range(7)]
        pb16 = T(st.enter_context(nc.psum_tensor("pb16", [128, 1024], BF16)), 0, "pb16")
        identf = sb(st, "identf", [128, 128], F32)
        identb = sb(st, "identb", [128, 128], BF16)
        cmask = sb(st, "cmask_sb", [128, 128], F32)
        tri = sb(st, "tri", [128, 128], F32)
        sel127 = sb(st, "sel127_sb", [128, 128], F32)
        btile = sb(st, "btile", [128, 2, 4, 128], F32)
        rb31 = sb(st, "rb31", [128, 4], F32)
        gbc = sb(st, "gbc", [128, D], F32)
        junk = sb(st, "junk", [128, D], F32)
        nsc = [sb(st, "nsc%d" % i, [128, 4], F32) for i in range(2)]
        sem_c = P.dma_sem()
        sem_x = P.dma_sem()
        sem_g = P.dma_sem()

        P.dma("sp", identf[:, :], id_d, sem_c, writes=[identf.r])
        P.dma("sp", cmask[:, :], cm_d, sem_c, writes=[cmask.r])
        P.dma("sp", sel127[:, :], s127_d, sem_c, writes=[sel127.r])
        P.dma("sp", btile[:, :, :, :], bt_d, sem_c, writes=[btile.r])
        P.dma("sp", rb31[:, :], rb_d[31:32, :].partition_broadcast(128).rearrange("p a b -> p (a b)"), sem_c, writes=[rb31.r])
        for r_ in (identf.r, cmask.r, sel127.r, btile.r, rb31.r):
            r_.w = (sem_c, P.cnt[sem_c])
        for tt in range(NT):
            P.dma("sp", xres[:, tt, :], x_d[tt * 128:(tt + 1) * 128, :], sem_x, writes=[xres.rs[tt]])
        for tt in range(NT):
            xres.rs[tt].w = (sem_x, P.cnt[sem_x])
        P.op("dve", "tensor_copy", dict(out=identb[:, :], in_=identf[:, :]), [identf.r], [identb.r])
        P.op("dve", "tensor_scalar", dict(out=tri[:, :], in0=cmask[:, :], scalar1=0.0, scalar2=None, op0=ALU.is_equal),
             [cmask.r], [tri.r])
        for h in range(4):
            P.op("dve", "tensor_tensor", dict(out=btile[:, 0, h, :], in0=btile[:, 0, h, :], in1=cmask[:, :], op=ALU.add),
                 [btile.r, cmask.r], [btile.r])

        evac_flip = [0]

        def evac(out_ap, in_ap, reads, writes, eng=None):
            if eng is None:
                eng = "act" if evac_flip[0] % 2 == 0 else "dve"
                evac_flip[0] += 1
            if eng == "act":
                P.op("act", "copy", dict(out=out_ap, in_=in_ap), reads, writes)
            else:
                P.op("dve", "tensor_copy", dict(out=out_ap, in_=in_ap), reads, writes)

        def load_gbc(src_row_ap):
            P.dma("sp", gbc[:, :], src_row_ap.partition_broadcast(128).rearrange("p a b -> p (a b)"), sem_g, writes=[gbc.r])

        def rms_tile(tt, dst_ap, dst_r):
            sc = nsc[tt % 2]
            P.op("act", "activation", dict(out=junk[:, :], in_=xres[:, tt, :], func=AF.Square), [xres.rs[tt]], [junk.r])
            P.op("dve", "reduce_sum", dict(out=sc[:, 0:1], in_=junk[:, :], axis=AX.X), [junk.r], [sc.r])
            P.op("act", "activation", dict(out=sc[:, 1:2], in_=sc[:, 0:1], func=AF.Sqrt, bias=EPS, scale=1.0 / D), [sc.r], [sc.r])
            P.op("dve", "reciprocal", dict(out=sc[:, 2:3], in_=sc[:, 1:2]), [sc.r], [sc.r])
            P.op("dve", "scalar_tensor_tensor", dict(out=dst_ap, in0=xres[:, tt, :], scalar=sc[:, 2:3], in1=gbc[:, :],
                                                     op0=ALU.mult, op1=ALU.mult), [xres.rs[tt], sc.r, gbc.r], [dst_r])

        def norm_T(hT, hn):
            for tt in range(NT):
                hb = hn[tt % 2]
                rms_tile(tt, hb[:, :], hb.r)
                for dc in range(8):
                    P.op("pe", "transpose", dict(out=pb16[:, dc * 128:(dc + 1) * 128], in_=hb[:, dc * 128:(dc + 1) * 128],
                                                 identity=identb[:, :]), [hb.r, identb.r], [pb16.r])
                evac(hT[:, :, tt * 128:(tt + 1) * 128], pb16[:, :].rearrange("p (a b) -> p a b", b=128),
                     [pb16.r], [hT.rs[tt // 4]])

        def load_w(dst, src_ap, semkey):
            P.dma("pool", dst_ap_full(dst), src_ap, semkey, writes=[dst.r])

        def dst_ap_full(t):
            nd = len(t.t.shape)
            return t[tuple([slice(None)] * nd)]

        def proj_fm(w, ncol_chunks, hT, dst, dst_chunk0, kch=8, rows=128, act_func=None):
            for oc in range(ncol_chunks):
                for tb in range(4):
                    bank = pf[4 + (tb % 2)]
                    for kc in range(kch):
                        P.op("pe", "matmul", dict(out=bank[:, :], lhsT=w[:, kc, oc * 128:(oc + 1) * 128],
                                                  rhs=hT[:, kc, tb * 512:(tb + 1) * 512], start=(kc == 0), stop=(kc == kch - 1)),
                             [w.r, hT.rs[tb]], [bank.r])
                    evac(dst[:, dst_chunk0 + oc, tb * 512:(tb + 1) * 512], bank[:, :], [bank.r], [dst.rs[tb]])

        def wcast(dst_ap, src_ap, semkey, dst_r):
            P.dma("pool", dst_ap, src_ap, semkey, writes=[dst_r])

        def proj_fm(w, nchunk, hT, dst, c0):
            for oc in range(nchunk):
                for tb in range(4):
                    bank = pf[4 + (tb % 2)]
                    for kc in range(8):
                        P.op("pe", "matmul", dict(out=bank[:, :], lhsT=w[:, kc, oc * 128:(oc + 1) * 128],
                                                  rhs=hT[:, kc, tb * 512:(tb + 1) * 512], start=(kc == 0), stop=(kc == 7)),
                             [w.r, hT.rs[tb]], [bank.r])
                    evac(dst[:, c0 + oc, tb * 512:(tb + 1) * 512], bank[:, :], [bank.r], [dst.rs[tb]])

        def proj_tm(w, hT, va, nh, dv):
            for tt in range(NT):
                bank = pf[4 + (tt % 2)]
                for kc in range(8):
                    P.op("pe", "matmul", dict(out=bank[:, 0:256], lhsT=hT[:, kc, tt * 128:(tt + 1) * 128], rhs=w[:, kc, :],
                                              start=(kc == 0), stop=(kc == 7)), [hT.rs[tt // 4], w.r], [bank.r])
                evac(va[:, tt, :, 0:dv], bank[:, 0:256].rearrange("p (a b) -> p a b", b=dv), [bank.r], [va.r])

        def dump_x(dst):
            semd = P.dma_sem()
            for tt in range(NT):
                P.dma("sp", dst[tt * 128:(tt + 1) * 128, :], xres[:, tt, :], semd, reads=[xres.rs[tt]])

        def phase_attention(layer):
            lam_init = 0.8 - 0.6 * math.exp(-0.3 * layer)
            with ExitStack() as sa:
                hT = sb(sa, "hT", [128, 8, S], BF16, nsub=4)
                ydT = sb(sa, "ydT", [128, 4, S], BF16, nsub=4)
                yfT = sb(sa, "yfT", [128, 4, S], BF16, nsub=4)
                lamb = sb(sa, "lamb", [128, 4, 64], F32)
                lsc = sb(sa, "lsc", [128, 8], F32)
                sgb = sb(sa, "sgb", [128, 128], F32)
                bfb = sb(sa, "bfb", [128, 8], F32)
                sx = ExitStack()
                wbuf = [sb(sx, "wbuf%d" % i, [128, 8, 256], BF16) for i in range(2)]
                semw = [P.dma_sem() for _ in range(2)]
                PT = [sb(sx, "PT%d" % i, [128, 512], BF16, nsub=4) for i in range(2)]
                dtmp = [sb(sx, "dtmp%d" % i, [128, 128], F32) for i in range(2)]
                ytk = sb(sx, "ytk", [128, 4, 128], BF16)
                y0 = [sb(sx, "y0_%d" % i, [128, 128], F32) for i in range(4)]
                ysc = [sb(sx, "ysc%d" % i, [128, 8], F32) for i in range(4)]
                sem_s = P.dma_sem()
                wslot = [0]

                def wcols(c0, n):
                    return win_d[layer, :, c0:c0 + n].rearrange("(kc p) c -> p kc c", p=128)

                def wload(c0):
                    i = wslot[0] % 2
                    wslot[0] += 1
                    wcast(wbuf[i][:, :, :], wcols(c0, 256), semw[i], wbuf[i].r)
                    return wbuf[i]

                load_gbc(n1_d[layer:layer + 1, :])
                P.dma("sp", lamb[:, :, :], lam_d[layer].partition_broadcast(128), sem_s, writes=[lamb.r])
                P.dma("sp", sgb[:, :], sg_d[layer:layer + 1, :].partition_broadcast(128).rearrange("p a b -> p (a b)"), sem_s, writes=[sgb.r])
                P.dma("sp", bfb[:, :], bf_d[layer:layer + 1, :].partition_broadcast(128).rearrange("p a b -> p (a b)"), sem_s, writes=[bfb.r])
                for r_ in (lamb.r, sgb.r, bfb.r):
                    r_.w = (sem_s, P.cnt[sem_s])
                with ExitStack() as sh:
                    hn = [sb(sh, "hn%d" % i, [128, D], BF16) for i in range(2)]
                    norm_T(hT, hn)
                P.barrier()

                P.op("dve", "tensor_tensor", dict(out=lamb[:, 0, :], in0=lamb[:, 0, :], in1=lamb[:, 1, :], op=ALU.mult), [lamb.r], [lamb.r])
                P.op("dve", "tensor_tensor", dict(out=lamb[:, 2, :], in0=lamb[:, 2, :], in1=lamb[:, 3, :], op=ALU.mult), [lamb.r], [lamb.r])
                P.op("dve", "reduce_sum", dict(out=lsc[:, 0:1], in_=lamb[:, 0, :], axis=AX.X), [lamb.r], [lsc.r])
                P.op("dve", "reduce_sum", dict(out=lsc[:, 1:2], in_=lamb[:, 2, :], axis=AX.X), [lamb.r], [lsc.r])
                P.op("act", "activation", dict(out=lsc[:, 2:4], in_=lsc[:, 0:2], func=AF.Exp), [lsc.r], [lsc.r])
                P.op("dve", "tensor_tensor", dict(out=lsc[:, 4:5], in0=lsc[:, 3:4], in1=lsc[:, 2:3], op=ALU.subtract), [lsc.r], [lsc.r])
                P.op("dve", "tensor_scalar", dict(out=lsc[:, 4:5], in0=lsc[:, 4:5], scalar1=-lam_init, scalar2=None, op0=ALU.add), [lsc.r], [lsc.r])
                P.op("dve", "tensor_scalar", dict(out=sgb[:, :], in0=sgb[:, :], scalar1=(1.0 - lam_init), scalar2=None, op0=ALU.mult), [sgb.r], [sgb.r])

                sflip = [0]

                def attn_core(kT, qT, ch, rows, va, hloc, dv, qc, kind, hglob, fb):
                    for kt in range(4 * qc + 4):
                        q_lo = max(qc * 512, kt * 128)
                        n = (qc + 1) * 512 - q_lo
                        sbk = pf[4 + (sflip[0] % 2)]
                        pt = PT[sflip[0] % 2]
                        sflip[0] += 1
                        P.op("pe", "matmul", dict(out=sbk[:, 0:n], lhsT=kT[rows, ch, kt * 128:(kt + 1) * 128],
                                                  rhs=qT[rows, ch, q_lo:q_lo + n], start=True, stop=True),
                             [kT.rs[kt // 4], qT.rs[qc]], [sbk.r])
                        qt0 = q_lo // 128
                        nq = n // 128
                        if kind == "d":
                            col = 0
                            for j in range(nq):
                                dl = qt0 + j - kt
                                if dl <= 1:
                                    tmp = dtmp[(qt0 + j) % 2]
                                    P.op("dve", "scalar_tensor_tensor", dict(out=tmp[:, :], in0=sbk[:, j * 128:(j + 1) * 128], scalar=0.125,
                                                                             in1=btile[:, dl, hglob, :], op0=ALU.mult, op1=ALU.add),
                                         [sbk.r, btile.r], [tmp.r])
                                    P.op("act", "activation", dict(out=pt[:, j * 128:(j + 1) * 128], in_=tmp[:, :], func=AF.Exp),
                                         [tmp.r], [pt.rs[j]])
                                    col = (j + 1) * 128
                            if col < n:
                                P.op("act", "activation", dict(out=pt[:, col:n], in_=sbk[:, col:n], func=AF.Exp,
                                                               bias=rb31[:, hglob:hglob + 1], scale=0.125), [sbk.r, rb31.r],
                                     [pt.rs[j] for j in range(col // 128, nq)])
                        else:
                            for j in range(nq):
                                qt = qt0 + j
                                if qt == kt:
                                    tmp = dtmp[qt % 2]
                                    P.op("dve", "scalar_tensor_tensor", dict(out=tmp[:, :], in0=sbk[:, j * 128:(j + 1) * 128], scalar=0.125,
                                                                             in1=cmask[:, :], op0=ALU.mult, op1=ALU.add),
                                         [sbk.r, cmask.r], [tmp.r])
                                    P.op("act", "activation", dict(out=pt[:, j * 128:(j + 1) * 128], in_=tmp[:, :], func=AF.Exp,
                                                                   bias=fb[:, kt, qt, hglob:hglob + 1], scale=1.0), [tmp.r, fb.r], [pt.rs[j]])
                                else:
                                    P.op("act", "activation", dict(out=pt[:, j * 128:(j + 1) * 128], in_=sbk[:, j * 128:(j + 1) * 128], func=AF.Exp,
                                                                   bias=fb[:, kt, qt, hglob:hglob + 1], scale=0.125), [sbk.r, fb.r], [pt.rs[j]])
                        for j in range(nq):
                            qt = qt0 + j
                            acc = pf[qt - 4 * qc]
                            P.op("pe", "matmul", dict(out=acc[:, 0:dv + 1], lhsT=pt[:, j * 128:(j + 1) * 128], rhs=va[:, kt, hloc, :],
                                                      start=(kt == 0), stop=(kt == qt)), [pt.rs[j], va.r], [acc.r])

                for half in range(2):
                    with ExitStack() as s2:
                        dqT = sb(s2, "dqT", [128, 2, S], BF16, nsub=4)
                        dkT = sb(s2, "dkT", [128, 2, S], BF16, nsub=4)
                        dva = sb(s2, "dva", [128, NT, 2, 129], BF16)
                        P.op("pool", "memset", dict(ap=dva[:, :, :, 128:129], constant=1.0), [], [dva.r])
                        w1 = wload(0 + half * 256)
                        w2 = wload(512 + half * 256)
                        proj_fm(w1, 2, hT, dqT, 0)
                        w3 = wload(1024 + half * 256)
                        proj_fm(w2, 2, hT, dkT, 0)
                        proj_tm(w3, hT, dva, 2, 128)
                        for hl in range(2):
                            h = half * 2 + hl
                            for qc in range(4):
                                for m in range(2):
                                    attn_core(dkT, dqT, hl, slice(m * 64, (m + 1) * 64), dva, hl, 128, qc, "d", h, None)
                                    for j in range(4):
                                        acc = pf[j]
                                        sc = ysc[j]
                                        if m == 0:
                                            P.op("dve", "reciprocal", dict(out=sc[:, 0:1], in_=acc[:, 128:129]), [acc.r], [sc.r])
                                            P.op("dve", "tensor_scalar", dict(out=y0[j][:, :], in0=acc[:, 0:128], scalar1=sc[:, 0:1], scalar2=None,
                                                                              op0=ALU.mult), [acc.r, sc.r], [y0[j].r])
                                        else:
                                            P.op("dve", "reciprocal", dict(out=sc[:, 1:2], in_=acc[:, 128:129]), [acc.r], [sc.r])
                                            P.op("dve", "tensor_tensor", dict(out=sc[:, 2:3], in0=sc[:, 1:2], in1=lsc[:, 4:5], op=ALU.mult),
                                                 [sc.r, lsc.r], [sc.r])
                                            P.op("dve", "scalar_tensor_tensor", dict(out=y0[j][:, :], in0=acc[:, 0:128], scalar=sc[:, 2:3],
                                                                                     in1=y0[j][:, :], op0=ALU.mult, op1=ALU.add),
                                                 [acc.r, sc.r, y0[j].r], [y0[j].r])
                                for j in range(4):
                                    sc = ysc[j]
                                    tmp = dtmp[j % 2]
                                    P.op("act", "activation", dict(out=tmp[:, :], in_=y0[j][:, :], func=AF.Square), [y0[j].r], [tmp.r])
                                    P.op("dve", "reduce_sum", dict(out=sc[:, 3:4], in_=tmp[:, :], axis=AX.X), [tmp.r], [sc.r])
                                    P.op("act", "activation", dict(out=sc[:, 4:5], in_=sc[:, 3:4], func=AF.Sqrt, bias=EPS, scale=1.0 / 128),
                                         [sc.r], [sc.r])
                                    P.op("dve", "reciprocal", dict(out=sc[:, 5:6], in_=sc[:, 4:5]), [sc.r], [sc.r])
                                    P.op("dve", "scalar_tensor_tensor", dict(out=ytk[:, j, :], in0=y0[j][:, :], scalar=sc[:, 5:6], in1=sgb[:, :],
                                                                             op0=ALU.mult, op1=ALU.mult), [y0[j].r, sc.r, sgb.r], [ytk.r])
                                    P.op("pe", "transpose", dict(out=pb16[:, j * 128:(j + 1) * 128], in_=ytk[:, j, :], identity=identb[:, :]),
                                         [ytk.r, identb.r], [pb16.r])
                                evac(ydT[:, h, qc * 512:(qc + 1) * 512], pb16[:, 0:512], [pb16.r], [ydT.rs[qc]])
                    P.barrier()

                with ExitStack() as sf:
                    wff = sb(sf, "wff", [128, 8, 8], BF16)
                    zz = sb(sf, "zz", [128, NT, 8], F32)
                    csb = sb(sf, "csb", [128, NT, 8], F32)
                    bcs = sb(sf, "bcs", [128, NT, 8], F32)
                    off = sb(sf, "off", [128, NT + 1, 8], F32)
                    fb = sb(sf, "fb", [128, NT, NT, 8], F32)
                    semf = P.dma_sem()
                    wcast(wff[:, :, :], wcols(3072, 8), semf, wff.r)
                    for tt in range(NT):
                        for kc in range(8):
                            P.op("pe", "matmul", dict(out=pf[6][:, tt * 8:(tt + 1) * 8], lhsT=hT[:, kc, tt * 128:(tt + 1) * 128], rhs=wff[:, kc, :],
                                                      start=(kc == 0), stop=(kc == 7)), [hT.rs[tt // 4], wff.r], [pf[6].r])
                    P.op("dve", "tensor_tensor", dict(out=zz[:, :, :], in0=pf[6][:, 0:128].rearrange("p (a b) -> p a b", b=8),
                                                      in1=bfb[:, :].unsqueeze(1).to_broadcast([128, NT, 8]), op=ALU.add),
                         [pf[6].r, bfb.r], [zz.r])
                    P.op("act", "activation", dict(out=zz[:, :, :], in_=zz[:, :, :], func=AF.Exp, scale=-1.0), [zz.r], [zz.r])
                    P.op("act", "activation", dict(out=zz[:, :, :], in_=zz[:, :, :], func=AF.Ln, bias=1.0, scale=1.0), [zz.r], [zz.r])
                    P.op("pe", "matmul", dict(out=pf[6][:, 0:128], lhsT=tri[:, :], rhs=zz[:, :, :].rearrange("p a b -> p (a b)"),
                                              start=True, stop=True), [tri.r, zz.r], [pf[6].r])
                    P.op("dve", "tensor_copy", dict(out=csb[:, :, :], in_=pf[6][:, 0:128].rearrange("p (a b) -> p a b", b=8)), [pf[6].r], [csb.r])
                    P.op("pe", "matmul", dict(out=pf[6][:, 128:256], lhsT=sel127[:, :], rhs=csb[:, :, :].rearrange("p a b -> p (a b)"),
                                              start=True, stop=True), [sel127.r, csb.r], [pf[6].r])
                    P.op("dve", "tensor_copy", dict(out=bcs[:, :, :], in_=pf[6][:, 128:256].rearrange("p (a b) -> p a b", b=8)), [pf[6].r], [bcs.r])
                    P.op("dve", "memset", dict(ap=off[:, 0, :], constant=0.0), [], [off.r])
                    for tt in range(NT):
                        P.op("dve", "tensor_tensor", dict(out=off[:, tt + 1, :], in0=off[:, tt, :], in1=bcs[:, tt, :], op=ALU.add),
                             [off.r, bcs.r], [off.r])
                    P.op("dve", "tensor_tensor", dict(out=csb[:, :, :], in0=csb[:, :, :], in1=off[:, 0:NT, :], op=ALU.add), [csb.r, off.r], [csb.r])
                    P.op("dve", "tensor_tensor", dict(out=fb[:, :, :, :], in0=csb[:, :, :].unsqueeze(2).to_broadcast([128, NT, NT, 8]),
                                                      in1=off[:, 1:NT + 1, :].unsqueeze(1).to_broadcast([128, NT, NT, 8]), op=ALU.subtract),
                         [csb.r, off.r], [fb.r])
                    for half in range(2):
                        with ExitStack() as s3:
                            fqT = sb(s3, "fqT", [128, 2, S], BF16, nsub=4)
                            fkT = sb(s3, "fkT", [128, 2, S], BF16, nsub=4)
                            fva = sb(s3, "fva", [128, NT, 4, 65], BF16)
                            P.op("pool", "memset", dict(ap=fva[:, :, :, 64:65], constant=1.0), [], [fva.r])
                            w1 = wload(1536 + half * 256)
                            w2 = wload(2048 + half * 256)
                            proj_fm(w1, 2, hT, fqT, 0)
                            w3 = wload(2560 + half * 256)
                            proj_fm(w2, 2, hT, fkT, 0)
                            proj_tm(w3, hT, fva, 4, 64)
                            for pair in range(2):
                                for qc in range(4):
                                    for hp in range(2):
                                        hl = pair * 2 + hp
                                        h = half * 4 + hl
                                        attn_core(fkT, fqT, pair, slice(hp * 64, hp * 64 + 64), fva, hl, 64, qc, "f", h, fb)
                                        for j in range(4):
                                            acc = pf[j]
                                            sc = ysc[j]
                                            P.op("dve", "reciprocal", dict(out=sc[:, 0:1], in_=acc[:, 64:65]), [acc.r], [sc.r])
                                            P.op("dve", "tensor_scalar", dict(out=ytk[:, j, hp * 64:hp * 64 + 64], in0=acc[:, 0:64], scalar1=sc[:, 0:1],
                                                                              scalar2=None, op0=ALU.mult), [acc.r, sc.r], [ytk.r])
                                    for j in range(4):
                                        P.op("pe", "transpose", dict(out=pb16[:, j * 128:(j + 1) * 128], in_=ytk[:, j, :], identity=identb[:, :]),
                                             [ytk.r, identb.r], [pb16.r])
                                    evac(yfT[:, half * 2 + pair, qc * 512:(qc + 1) * 512], pb16[:, 0:512], [pb16.r], [yfT.rs[qc]])
                        P.barrier()

                P.barrier()
                sx.close()
                mT = sb(sa, "mT", [128, 8, S], BF16, nsub=4)
                with ExitStack() as s4:
                    wga = [sb(s4, "wga%d" % i, [128, 8, 128], BF16) for i in range(2)]
                    wgb = [sb(s4, "wgb%d" % i, [128, 8, 128], BF16) for i in range(2)]
                    wdo = [sb(s4, "wdo%d" % i, [128, 4, 128], BF16) for i in range(2)]
                    wfo = [sb(s4, "wfo%d" % i, [128, 4, 128], BF16) for i in range(2)]
                    semg4 = [P.dma_sem() for _ in range(2)]
                    sga = [sb(s4, "sga%d" % i, [128, 512], F32) for i in range(2)]
                    sgbb = [sb(s4, "sgbb%d" % i, [128, 512], F32) for i in range(2)]
                    m1 = [sb(s4, "m1_%d" % i, [128, 512], F32) for i in range(2)]

                    def load_c(c):
                        i = c % 2
                        wcast(wga[i][:, :, :], wcols(3080 + c * 128, 128), semg4[i], wga[i].r)
                        wcast(wgb[i][:, :, :], wcols(4104 + c * 128, 128), semg4[i], wgb[i].r)
                        wcast(wdo[i][:, :, :], wdo_d[layer, :, c * 128:(c + 1) * 128].rearrange("(kc p) n -> p kc n", p=128), semg4[i], wdo[i].r)
                        wcast(wfo[i][:, :, :], wfo_d[layer, :, c * 128:(c + 1) * 128].rearrange("(kc p) n -> p kc n", p=128), semg4[i], wfo[i].r)
                        for t_ in (wga[i], wgb[i], wdo[i], wfo[i]):
                            t_.r.w = (semg4[i], P.cnt[semg4[i]])
                    load_c(0)
                    it = 0
                    for c in range(8):
                        if c + 1 < 8:
                            load_c(c + 1)
                        i = c % 2
                        for tb in range(4):
                            k = it % 2
                            it += 1
                            tsl = slice(tb * 512, (tb + 1) * 512)
                            for kc in range(8):
                                P.op("pe", "matmul", dict(out=pf[4][:, :], lhsT=wga[i][:, kc, :], rhs=hT[:, kc, tsl], start=(kc == 0), stop=(kc == 7)),
                                     [wga[i].r, hT.rs[tb]], [pf[4].r])
                            P.op("act", "activation", dict(out=sga[k][:, :], in_=pf[4][:, :], func=AF.Sigmoid), [pf[4].r], [sga[k].r])
                            for kc in range(8):
                                P.op("pe", "matmul", dict(out=pf[5][:, :], lhsT=wgb[i][:, kc, :], rhs=hT[:, kc, tsl], start=(kc == 0), stop=(kc == 7)),
                                     [wgb[i].r, hT.rs[tb]], [pf[5].r])
                            P.op("act", "activation", dict(out=sgbb[k][:, :], in_=pf[5][:, :], func=AF.Sigmoid), [pf[5].r], [sgbb[k].r])
                            bd = pf[0 + 2 * k]
                            bf_ = pf[1 + 2 * k]
                            for hh in range(4):
                                P.op("pe", "matmul", dict(out=bd[:, :], lhsT=wdo[i][:, hh, :], rhs=ydT[:, hh, tsl], start=(hh == 0), stop=(hh == 3)),
                                     [wdo[i].r, ydT.rs[tb]], [bd.r])
                            for hh in range(4):
                                P.op("pe", "matmul", dict(out=bf_[:, :], lhsT=wfo[i][:, hh, :], rhs=yfT[:, hh, tsl], start=(hh == 0), stop=(hh == 3)),
                                     [wfo[i].r, yfT.rs[tb]], [bf_.r])
                            P.op("dve", "tensor_tensor", dict(out=m1[k][:, :], in0=bd[:, :], in1=sga[k][:, :], op=ALU.mult), [bd.r, sga[k].r], [m1[k].r])
                            P.op("dve", "tensor_tensor", dict(out=sgbb[k][:, :], in0=bf_[:, :], in1=sgbb[k][:, :], op=ALU.mult), [bf_.r, sgbb[k].r], [sgbb[k].r])
                            P.op("dve", "tensor_tensor", dict(out=mT[:, c, tsl], in0=m1[k][:, :], in1=sgbb[k][:, :], op=ALU.add),
                                 [m1[k].r, sgbb[k].r], [mT.rs[tb]])
                P.barrier()
                with ExitStack() as s5:
                    wo = sb(s5, "wo", [128, 8, D], BF16)
                    semo5 = P.dma_sem()
                    wsrc = wout_d[layer].rearrange("(kc p) n -> p kc n", p=128)
                    for q4 in range(4):
                        wcast(wo[:, 2 * q4:2 * q4 + 2, :], wsrc[:, 2 * q4:2 * q4 + 2, :], semo5, wo.r)
                    wo.r.w = (semo5, P.cnt[semo5])
                    for tt in range(NT):
                        for hf in range(2):
                            bank = pf[(2 * tt + hf) % 4]
                            for kc in range(8):
                                P.op("pe", "matmul", dict(out=bank[:, :], lhsT=mT[:, kc, tt * 128:(tt + 1) * 128], rhs=wo[:, kc, hf * 512:(hf + 1) * 512],
                                                          start=(kc == 0), stop=(kc == 7)), [mT.rs[tt // 4], wo.r], [bank.r])
                            P.op("dve", "tensor_tensor", dict(out=xres[:, tt, hf * 512:(hf + 1) * 512], in0=xres[:, tt, hf * 512:(hf + 1) * 512],
                                                              in1=bank[:, :], op=ALU.add), [xres.rs[tt], bank.r], [xres.rs[tt]])
            P.barrier()

        def phase_peer(layer):
            with ExitStack() as sp_:
                h2T = sb(sp_, "h2T", [128, 8, S], BF16, nsub=4)
                load_gbc(n2_d[layer:layer + 1, :])
                with ExitStack() as sh:
                    hn = [sb(sh, "hn%d" % i, [128, D], BF16) for i in range(2)]
                    norm_T(h2T, hn)
                P.barrier()
                with ExitStack() as s1:
                    KTf = sb(s1, "KTf", [128, 2, 128], F32)
                    KTc = sb(s1, "KTc", [128, 2, 128], BF16)
                    KT = sb(s1, "KT", [128, 2, 128], BF16)
                    qTb = sb(s1, "qTb", [128, 16, 512], BF16)
                    wqb = sb(s1, "wqb", [128, 8, 512], BF16)
                    s_sb = sb(s1, "s_sb", [128, 2048], F32)
                    wk = sb(s1, "wk", [128, 2048], F32)
                    vt = sb(s1, "vt", [128, 256], F32)
                    best = sb(s1, "best", [128, 128], F32)
                    eb = sb(s1, "eb", [128, 128], F32)
                    ev0 = sb(s1, "ev0", [128, 8, 16], F32)
                    zs = sb(s1, "zs", [128, 16], F32)
                    tm = sb(s1, "tm", [128, 3, 128], F32)
                    tT = sb(s1, "tT", [128, 3, 512], F32)
                    e1 = [sb(s1, "e1_%d" % i, [128, 4, 128], F32) for i in range(2)]
                    Fb = [sb(s1, "Fb%d" % i, [128, 4, 128], BF16) for i in range(2)]
                    E0b = [sb(s1, "E0b%d" % i, [128, 4, 128], BF16) for i in range(2)]
                    stg = [sb(s1, "stg%d" % i, [128, 128, 64], BF16) for i in range(2)]
                    qrep = [sb(s1, "qrep%d" % i, [128, 2, 4, 128], BF16) for i in range(2)]
                    semk = P.dma_sem()
                    semq = P.dma_sem()
                    semst = [P.dma_sem() for _ in range(2)]
                    P.dma("sp", KTf[:, :, :], sk_d[layer].rearrange("p n d -> n p d"), semk, writes=[KTf.r])
                    P.op("dve", "tensor_copy", dict(out=KTc[:, :, :], in_=KTf[:, :, :]), [KTf.r], [KTc.r])
                    for p in range(2):
                        P.op("pe", "transpose", dict(out=pb16[:, p * 128:(p + 1) * 128], in_=KTc[:, p, :], identity=identb[:, :]),
                             [KTc.r, identb.r], [pb16.r])
                    evac(KT[:, :, :], pb16[:, 0:256].rearrange("p (a b) -> p a b", b=128), [pb16.r], [KT.r])
                    sv = s_sb[:, :].rearrange("p (o n) -> p o n", n=128)
                    wv = wk[:, :].rearrange("p (o n) -> p o n", n=128)
                    cv4 = s_sb[:, :].rearrange("p (h a b) -> p h a b", a=16, b=16)
                    cv = s_sb[:, :].rearrange("p (h c) -> p h c", c=256)
                    wv2 = wk[:, :].rearrange("p (h c) -> p h c", c=256)
                    vv = vt[:, :].rearrange("p (h q a) -> p h q a", q=2, a=16)
                    bv = best[:, :].rearrange("p (h a) -> p h a", a=16)
                    S0 = [pf[0], pf[1]]
                    S1 = [pf[2], pf[3]]
                    GT = [pf[4], pf[5]]
                    for tb in range(4):
                        tsl = slice(tb * 512, (tb + 1) * 512)
                        for g in range(4):
                            wcast(wqb[:, :, :], wq_d[layer, :, g * 512:(g + 1) * 512].rearrange("(kc p) c -> p kc c", p=128), semq, wqb.r)
                            for oc in range(4):
                                bank = pf[4 + (oc % 2)]
                                for kc in range(8):
                                    P.op("pe", "matmul", dict(out=bank[:, :], lhsT=wqb[:, kc, oc * 128:(oc + 1) * 128], rhs=h2T[:, kc, tsl],
                                                              start=(kc == 0), stop=(kc == 7)), [wqb.r, h2T.rs[tb]], [bank.r])
                                evac(qTb[:, g * 4 + oc, :], bank[:, :], [bank.r], [qTb.r])
                        for ti in range(4):
                            tcol = slice(ti * 128, (ti + 1) * 128)
                            for oc in range(16):
                                P.op("pe", "matmul", dict(out=pf[oc // 4][:, (oc % 4) * 128:(oc % 4 + 1) * 128], lhsT=qTb[:, oc, tcol], rhs=KT[:, oc % 2, :],
                                                          start=True, stop=True), [qTb.r, KT.r], [pf[oc // 4].r])
                            for b4 in range(4):
                                P.op("act", "copy", dict(out=s_sb[:, b4 * 512:(b4 + 1) * 512], in_=pf[b4][:, :]), [pf[b4].r], [s_sb.r])
                            for oc in range(16):
                                P.op("dve", "max", dict(out=vt[:, oc * 16:oc * 16 + 8], in_=sv[:, oc, :]), [s_sb.r], [vt.r])
                                P.op("dve", "match_replace", dict(out=wv[:, oc, :], in_to_replace=vt[:, oc * 16:oc * 16 + 8], in_values=sv[:, oc, :],
                                                                  imm_value=-1e30), [vt.r, s_sb.r], [wk.r])
                                P.op("dve", "max", dict(out=vt[:, oc * 16 + 8:oc * 16 + 16], in_=wv[:, oc, :]), [wk.r], [vt.r])
                            P.op("dve", "tensor_tensor", dict(out=cv4, in0=vv[:, :, 0, :].unsqueeze(3).to_broadcast([128, 8, 16, 16]),
                                                              in1=vv[:, :, 1, :].unsqueeze(2).to_broadcast([128, 8, 16, 16]), op=ALU.add),
                                 [vt.r], [s_sb.r])
                            for h in range(8):
                                P.op("dve", "max", dict(out=best[:, h * 16:h * 16 + 8], in_=cv[:, h, :]), [s_sb.r], [best.r])
                                P.op("dve", "match_replace", dict(out=wv2[:, h, :], in_to_replace=best[:, h * 16:h * 16 + 8], in_values=cv[:, h, :],
                                                                  imm_value=-1e30), [best.r, s_sb.r], [wk.r])
                                P.op("dve", "max", dict(out=best[:, h * 16 + 8:h * 16 + 16], in_=wv2[:, h, :]), [wk.r], [best.r])
                            P.op("act", "activation", dict(out=eb[:, :], in_=best[:, :], func=AF.Exp), [best.r], [eb.r])
                            P.op("dve", "reduce_sum", dict(out=zs[:, 0:8], in_=eb[:, :].rearrange("p (h a) -> p h a", a=16), axis=AX.X), [eb.r], [zs.r])
                            P.op("dve", "reciprocal", dict(out=zs[:, 8:16], in_=zs[:, 0:8]), [zs.r], [zs.r])
                            P.op("act", "activation", dict(out=ev0[:, :, :], in_=vv[:, :, 0, :], func=AF.Exp), [vt.r], [ev0.r])
                            P.op("dve", "tensor_tensor", dict(out=tm[:, 0, :].rearrange("p (h a) -> p h a", a=16), in0=bv[:, :, 15:16].to_broadcast([128, 8, 16]),
                                                              in1=vv[:, :, 0, :], op=ALU.subtract), [best.r, vt.r], [tm.r])
                            P.op("dve", "tensor_copy", dict(out=tm[:, 1, :].rearrange("p (h a) -> p h a", a=16), in_=vv[:, :, 0, :]), [vt.r], [tm.r])
                            P.op("dve", "tensor_tensor", dict(out=tm[:, 2, :].rearrange("p (h a) -> p h a", a=16), in0=ev0[:, :, :],
                                                              in1=zs[:, 8:16].unsqueeze(2).to_broadcast([128, 8, 16]), op=ALU.mult), [ev0.r, zs.r], [tm.r])
                            for k3 in range(3):
                                P.op("pe", "transpose", dict(out=pf[6][:, k3 * 128:(k3 + 1) * 128], in_=tm[:, k3, :], identity=identf[:, :]),
                                     [tm.r, identf.r], [pf[6].r])
                            evac(tT[:, :, tcol], pf[6][:, 0:384].rearrange("p (a b) -> p a b", b=128), [pf[6].r], [tT.r])

                        def s_stage(nb):
                            k = nb % 2
                            for p in range(2):
                                P.op("pool", "tensor_copy", dict(out=qrep[k][:, p, :, :].rearrange("d t (h a) -> d t h a", a=16),
                                                                 in_=qTb[:, p:16:2, nb * 4:nb * 4 + 4].rearrange("d h t -> d t h").unsqueeze(3).to_broadcast([128, 4, 8, 16])),
                                     [qTb.r], [qrep[k].r])
                            for u in range(4):
                                tl = nb * 4 + u
                                us = slice(u * 128, (u + 1) * 128)
                                P.op("pe", "matmul", dict(out=S0[k][:, us], lhsT=qrep[k][:, 0, u, :], rhs=KT[:, 0, :],
                                                          start=True, stop=True), [qrep[k].r, KT.r], [S0[k].r])
                                P.op("pe", "matmul", dict(out=S1[k][:, us], lhsT=qrep[k][:, 1, u, :], rhs=KT[:, 1, :],
                                                          start=True, stop=True), [qrep[k].r, KT.r], [S1[k].r])
                            P.op("act", "activation", dict(out=e1[k][:, :, :], in_=S1[k][:, :].rearrange("p (a b) -> p a b", b=128), func=AF.Exp),
                                 [S1[k].r], [e1[k].r])
                            for u in range(4):
                                tl = nb * 4 + u
                                us = slice(u * 128, (u + 1) * 128)
                                P.op("dve", "scalar_tensor_tensor", dict(out=Fb[k][:, u, :], in0=S1[k][:, us], scalar=tT[:, 0, tl:tl + 1], in1=e1[k][:, u, :],
                                                                         op0=ALU.is_ge, op1=ALU.mult), [S1[k].r, tT.r, e1[k].r], [Fb[k].r])
                                P.op("dve", "tensor_scalar", dict(out=E0b[k][:, u, :], in0=S0[k][:, us], scalar1=tT[:, 1, tl:tl + 1], scalar2=tT[:, 2, tl:tl + 1],
                                                                  op0=ALU.is_equal, op1=ALU.mult), [S0[k].r, tT.r], [E0b[k].r])

                        def g_stage(nb):
                            k = nb % 2
                            for u in range(4):
                                us = slice(u * 128, (u + 1) * 128)
                                P.op("pe", "matmul", dict(out=GT[k][:, us], lhsT=Fb[k][:, u, :], rhs=E0b[k][:, u, :], start=True, stop=True),
                                     [Fb[k].r, E0b[k].r], [GT[k].r])
                            tg = tb * 512 + nb * 4
                            ss = (tg // 64) % 2
                            t64 = tg % 64
                            P.op("act", "copy", dict(out=stg[ss][:, :, t64:t64 + 4].rearrange("j i t -> j t i"),
                                                     in_=GT[k][:, :].rearrange("p (a b) -> p a b", b=128)), [GT[k].r], [stg[ss].r])
                            if t64 + 4 == 64:
                                t0 = tg + 4 - 64
                                for i4 in range(4):
                                    P.dma("sp", gscr[i4 * 32:(i4 + 1) * 32, :, t0:t0 + 64].rearrange("i j t -> j i t"), stg[ss][:, i4 * 32:(i4 + 1) * 32, :],
                                          semst[ss], reads=[stg[ss].r])

                        for nb in range(128):
                            s_stage(nb)
                            if nb >= 1:
                                g_stage(nb - 1)
                        g_stage(127)
                P.barrier()
                with ExitStack() as s2:
                    Ub = [sb(s2, "Ub%d" % i, [128, 4, D], BF16) for i in range(2)]
                    Vb = [sb(s2, "Vb%d" % i, [128, 4, D], BF16) for i in range(2)]
                    UT = [sb(s2, "UT%d" % i, [128, 4, 8, 128], BF16) for i in range(2)]
                    Gb = [sb(s2, "Gb%d" % i, [128, 4, 256], BF16) for i in range(2)]
                    gl = [sb(s2, "gl%d" % i, [128, 256], BF16) for i in range(2)]
                    PTb = [sb(s2, "PTb%d" % i, [128, 256], BF16) for i in range(2)]
                    semU = [P.dma_sem() for _ in range(2)]
                    semV = [P.dma_sem() for _ in range(2)]
                    semG = [P.dma_sem() for _ in range(2)]
                    NEG_ = 32

                    def load_eg(eg):
                        i = eg % 2
                        wcast(Ub[i][:, :, :], eu_d[layer, eg * 512:(eg + 1) * 512, :].rearrange("(c p) d -> p c d", p=128), semU[i], Ub[i].r)
                        wcast(Vb[i][:, :, :], ev_d[layer, eg * 512:(eg + 1) * 512, :].rearrange("(c p) d -> p c d", p=128), semV[i], Vb[i].r)

                    def prep_eg(eg):
                        i = eg % 2
                        for ci in range(4):
                            for dc in range(8):
                                P.op("pe", "transpose", dict(out=pb16[:, dc * 128:(dc + 1) * 128], in_=Ub[i][:, ci, dc * 128:(dc + 1) * 128],
                                                             identity=identb[:, :]), [Ub[i].r, identb.r], [pb16.r])
                            evac(UT[i][:, ci, :, :], pb16[:, :].rearrange("p (a b) -> p a b", b=128), [pb16.r], [UT[i].r])

                    gcount = [0]

                    def load_g(eg, tb):
                        i = gcount[0] % 2
                        gcount[0] += 1
                        P.dma("sp", Gb[i][:, :, :], gscr[eg * 4:(eg + 1) * 4, :, tb * 256:(tb + 1) * 256].rearrange("i j t -> j i t"), semG[i], writes=[Gb[i].r])
                        return Gb[i]

                    def pv_stage(item):
                        eg, tb, ci, n, gbuf = item
                        i = eg % 2
                        for t2 in range(2):
                            for hf in range(2):
                                acc = pf[t2 * 2 + hf]
                                P.op("pe", "matmul", dict(out=acc[:, :], lhsT=PTb[n % 2][:, t2 * 128:(t2 + 1) * 128], rhs=Vb[i][:, ci, hf * 512:(hf + 1) * 512],
                                                          start=(ci == 0), stop=(ci == 3)), [PTb[n % 2].r, Vb[i].r], [acc.r])
                        if ci == 3:
                            for t2 in range(2):
                                tt = tb * 2 + t2
                                for hf in range(2):
                                    acc = pf[t2 * 2 + hf]
                                    P.op("dve", "tensor_tensor", dict(out=xres[:, tt, hf * 512:(hf + 1) * 512], in0=xres[:, tt, hf * 512:(hf + 1) * 512],
                                                                      in1=acc[:, :], op=ALU.add), [xres.rs[tt], acc.r], [xres.rs[tt]])

                    load_eg(0)
                    prev = None
                    n = 0
                    gnext = load_g(0, 0)
                    for eg in range(NEG_):
                        if eg + 1 < NEG_:
                            load_eg(eg + 1)
                        prep_eg(eg)
                        i = eg % 2
                        for tb in range(8):
                            gbuf = gnext
                            if tb + 1 < 8:
                                gnext = load_g(eg, tb + 1)
                            elif eg + 1 < NEG_:
                                gnext = load_g(eg + 1, 0)
                            for ci in range(4):
                                bank = pf[4 + (n % 2)]
                                for dc in range(8):
                                    P.op("pe", "matmul", dict(out=bank[:, 0:256], lhsT=UT[i][:, ci, dc, :], rhs=h2T[:, dc, tb * 256:(tb + 1) * 256],
                                                              start=(dc == 0), stop=(dc == 7)), [UT[i].r, h2T.rs[tb // 2]], [bank.r])
                                P.op("act", "activation", dict(out=gl[n % 2][:, :], in_=bank[:, 0:256], func=AF.Gelu), [bank.r], [gl[n % 2].r])
                                P.op("dve", "tensor_tensor", dict(out=PTb[n % 2][:, :], in0=gl[n % 2][:, :], in1=gbuf[:, ci, :], op=ALU.mult),
                                     [gl[n % 2].r, gbuf.r], [PTb[n % 2].r])
                                if prev is not None:
                                    pv_stage(prev)
                                prev = (eg, tb, ci, n, gbuf)
                                n += 1
                        pv_stage(prev)
                        prev = None
            P.barrier()

        for layer in range(depth):
            phase_attention(layer)
            if debug == "A" and layer == depth - 1:
                dump_x(dbg_d)
                break
            phase_peer(layer)
            if debug == "P" and layer == depth - 1:
                dump_x(dbg_d)
        with ExitStack() as sfin:
            ostg = [sb(sfin, "ostg%d" % i, [128, D], F32) for i in range(2)]
            semo = [P.dma_sem() for _ in range(2)]
            load_gbc(nf_d[0:1, :])
            for tt in range(NT):
                o = ostg[tt % 2]
                rms_tile(tt, o[:, :], o.r)
                P.dma("sp", out_d[tt * 128:(tt + 1) * 128, :], o[:, :], semo[tt % 2], reads=[o.r])
            P.barrier()
        P.emit()
    return nc


_CONSTS = None


def _consts():
    global _CONSTS
    if _CONSTS is None:
        p = np.arange(128)
        cm = np.where(p[None, :] >= p[:, None], 0.0, NEG).astype(np.float32)
        ident = np.eye(128, dtype=np.float32)
        s127 = np.zeros((128, 128), np.float32)
        s127[127, :] = 1.0
        bidx = np.stack([t5_bucket_np(dl * 128 + p[None, :] - p[:, None]) for dl in range(2)], 0)
        _CONSTS = (cm, ident, s127, bidx)
    return _CONSTS


def kernel(**inputs):
    cm, ident, s127, bidx = _consts()
    f = lambda a: np.ascontiguousarray(np.asarray(a, dtype=np.float32))
    rel_bias = f(inputs["rel_bias"])
    bt = np.ascontiguousarray(np.transpose(rel_bias[bidx], (1, 0, 3, 2)))
    shared = {k: f(inputs[k]) for k in ("norm1_g", "w_in", "b_forget", "diff_lambda", "diff_subln_g", "w_diff_o", "w_fox_o",
                                        "w_out", "norm2_g", "w_query", "sub_keys", "expert_u", "expert_v")}
    shared["rel_bias"] = rel_bias
    shared["final_norm_g"] = f(inputs["final_norm_g"]).reshape(1, D)
    shared["bias_tiles"] = bt
    shared["cmask"] = cm
    shared["ident"] = ident
    shared["sel127"] = s127
    x = f(inputs["x"])
    nb = x.shape[0]
    nc = build_program()
    in_maps = []
    for b in range(nb):
        m = dict(shared)
        m["x"] = np.ascontiguousarray(x[b])
        in_maps.append(m)
    res = run_bass_kernel_spmd(nc, in_maps, core_ids=list(range(nb)))
    return np.stack([np.asarray(r["out"], dtype=np.float32) for r in res.results], axis=0)
```

```python
import math
import numpy as np
from contextlib import ExitStack
import concourse.bass as bass
import concourse.mybir as mybir
from concourse.bass_utils import run_bass_kernel_spmd

F32 = mybir.dt.float32
BF16 = mybir.dt.bfloat16
AF = mybir.ActivationFunctionType
ALU = mybir.AluOpType
AX = mybir.AxisListType

S = 2048
D = 1024
NT = 16
DEPTH = 2
INW = 5128
EPS = 1e-6
NEG = -30000.0
ENGS = ("pe", "act", "dve", "pool", "sp")


class R:
    __slots__ = ("name", "w", "rd")

    def __init__(self, name=""):
        self.name = name
        self.w = None
        self.rd = {}


class Prog:
    def __init__(self, nc, stack):
        self.nc = nc
        self.stack = stack
        self.ops = {e: [] for e in ENGS}
        self.sems = {}
        self.cnt = {}
        self.seen = {e: {} for e in ENGS}
        for e in ENGS:
            self._mksem(e)
        self.n_dma_sem = 0

    CH = 16000

    def _mksem(self, key):
        self.sems[key] = self.stack.enter_context(self.nc.semaphore(str(key)))
        self.cnt[key] = 0

    def _semval(self, k, v):
        if k in ENGS:
            c = (v - 1) // self.CH
            key = (k, c)
            if key not in self.sems:
                self.sems[key] = self.stack.enter_context(self.nc.semaphore("%s_%d" % (k, c)))
            return self.sems[key], (v - 1) % self.CH + 1
        return self.sems[k], v

    def dma_sem(self):
        key = "dma%d" % self.n_dma_sem
        self.n_dma_sem += 1
        self._mksem(key)
        return key

    def _deps(self, eng, reads, writes):
        deps = {}

        def add(d):
            if d is None:
                return
            k, v = d
            if deps.get(k, 0) < v:
                deps[k] = v
        for r in reads:
            add(r.w)
        for r in writes:
            add(r.w)
            for k, v in r.rd.items():
                add((k, v))
        waits = []
        for k, v in deps.items():
            if k == eng and eng == "pe":
                continue
            if self.seen[eng].get(k, 0) >= v:
                continue
            self.seen[eng][k] = v
            waits.append(self._semval(k, v))
        return waits

    def op(self, eng, name, kw, reads=(), writes=()):
        wl = self._deps(eng, reads, writes)
        self.cnt[eng] += 1
        seq = self.cnt[eng]
        self.ops[eng].append((wl, name, kw, self._semval(eng, seq)[0], 1))
        for r in writes:
            r.w = (eng, seq)
            r.rd = {}
        for r in reads:
            r.rd[eng] = seq

    def dma(self, eng, out, in_, semkey, reads=(), writes=()):
        wl = self._deps(eng, reads, writes)
        self.cnt[semkey] += 16
        val = self.cnt[semkey]
        self.ops[eng].append((wl, "dma_start", dict(out=out, in_=in_), self.sems[semkey], 16))
        for r in writes:
            r.w = (semkey, val)
            r.rd = {}
        for r in reads:
            r.rd[semkey] = val

    def barrier(self):
        for e in ENGS:
            wl = []
            for k in self.cnt:
                if k == e:
                    continue
                v = self.cnt[k]
                if v > self.seen[e].get(k, 0):
                    self.seen[e][k] = v
                    wl.append(self._semval(k, v))
            if wl:
                self.ops[e].append((wl, None, None, None, 0))

    def emit(self):
        nc = self.nc

        def play(e, lst):
            for wl, name, kw, sem, inc in lst:
                for s, v in wl:
                    e.wait_ge(s, v)
                if name is not None:
                    getattr(e, name)(**kw).then_inc(sem, inc)

        with nc.Block() as block:
            @block.tensor
            def _(e):
                play(e, self.ops["pe"])

            @block.scalar
            def _(e):
                play(e, self.ops["act"])

            @block.vector
            def _(e):
                play(e, self.ops["dve"])

            @block.gpsimd
            def _(e):
                play(e, self.ops["pool"])

            @block.sync
            def _(e):
                play(e, self.ops["sp"])


class T:
    def __init__(self, t, nsub=0, name=""):
        self.t = t
        self.r = R(name)
        self.rs = [R("%s%d" % (name, i)) for i in range(nsub)]

    def __getitem__(self, k):
        return self.t[k]


def t5_bucket_np(dist):
    n = np.maximum(dist, 0)
    me = 16
    nf = np.maximum(n, 1).astype(np.float32)
    large = me + (np.log(nf / me) / math.log(128 / me) * (32 - me)).astype(np.int32)
    large = np.minimum(large, 31)
    return np.where(n < me, n, large)


def build_program(debug=None, depth=DEPTH):
    nc = bass.Bass("TRN2", target_bir_lowering=False)
    dr = {}

    def din(name, shape):
        dr[name] = nc.dram_tensor(name, list(shape), F32, kind="ExternalInput").ap()
        return dr[name]

    x_d = din("x", [S, D])
    n1_d = din("norm1_g", [DEPTH, D])
    win_d = din("w_in", [DEPTH, D, INW])
    bf_d = din("b_forget", [DEPTH, 8])
    lam_d = din("diff_lambda", [DEPTH, 4, 64])
    sg_d = din("diff_subln_g", [DEPTH, 128])
    wdo_d = din("w_diff_o", [DEPTH, 512, D])
    wfo_d = din("w_fox_o", [DEPTH, 512, D])
    wout_d = din("w_out", [DEPTH, D, D])
    n2_d = din("norm2_g", [DEPTH, D])
    wq_d = din("w_query", [DEPTH, D, 2048])
    sk_d = din("sub_keys", [DEPTH, 2, 128, 128])
    eu_d = din("expert_u", [DEPTH, 16384, D])
    ev_d = din("expert_v", [DEPTH, 16384, D])
    rb_d = din("rel_bias", [32, 4])
    nf_d = din("final_norm_g", [1, D])
    bt_d = din("bias_tiles", [128, 2, 4, 128])
    cm_d = din("cmask", [128, 128])
    id_d = din("ident", [128, 128])
    s127_d = din("sel127", [128, 128])
    out_d = nc.dram_tensor("out", [S, D], F32, kind="ExternalOutput").ap()
    gscr = nc.dram_tensor("gscr", [128, 128, S], BF16).ap()
    dbg_d = None
    if debug is not None:
        dbg_d = nc.dram_tensor("dbg", [S, D], F32, kind="ExternalOutput").ap()

    with ExitStack() as st:
        P = Prog(nc, st)

        uid = [0]

        def sb(stk, name, shape, dt, nsub=0):
            uid[0] += 1
            name = "%s_%d" % (name, uid[0])
            return T(stk.enter_context(nc.sbuf_tensor(name, list(shape), dt)), nsub, name)

        xres = sb(st, "xres", [128, NT, D], F32, nsub=NT)
        pf = [T(st.enter_context(nc.psum_tensor("pf%d" % i, [128, 512], F32)), 0, "pf%d" % i) for i in range(7)]
        pb16 = T(st.enter_context(nc.psum_tensor("pb16", [128, 1024], BF16)), 0, "pb16")
        identf = sb(st, "identf", [128, 128], F32)
        identb = sb(st, "identb", [128, 128], BF16)
        cmask = sb(st, "cmask_sb", [128, 128], F32)
        tri = sb(st, "tri", [128, 128], F32)
        sel127 = sb(st, "sel127_sb", [128, 128], F32)
        btile = sb(st, "btile", [128, 2, 4, 128], F32)
        rb31 = sb(st, "rb31", [128, 4], F32)
        gbc = sb(st, "gbc", [128, D], F32)
        junk = sb(st, "junk", [128, D], F32)
        nsc = [sb(st, "nsc%d" % i, [128, 4], F32) for i in range(2)]
        sem_c = P.dma_sem()
        sem_x = P.dma_sem()
        sem_g = P.dma_sem()

        P.dma("sp", identf[:, :], id_d, sem_c, writes=[identf.r])
        P.dma("sp", cmask[:, :], cm_d, sem_c, writes=[cmask.r])
        P.dma("sp", sel127[:, :], s127_d, sem_c, writes=[sel127.r])
        P.dma("sp", btile[:, :, :, :], bt_d, sem_c, writes=[btile.r])
        P.dma("sp", rb31[:, :], rb_d[31:32, :].partition_broadcast(128).rearrange("p a b -> p (a b)"), sem_c, writes=[rb31.r])
        for r_ in (identf.r, cmask.r, sel127.r, btile.r, rb31.r):
            r_.w = (sem_c, P.cnt[sem_c])
        for tt in range(NT):
            P.dma("sp", xres[:, tt, :], x_d[tt * 128:(tt + 1) * 128, :], sem_x, writes=[xres.rs[tt]])
        for tt in range(NT):
            xres.rs[tt].w = (sem_x, P.cnt[sem_x])
        P.op("dve", "tensor_copy", dict(out=identb[:, :], in_=identf[:, :]), [identf.r], [identb.r])
        P.op("dve", "tensor_scalar", dict(out=tri[:, :], in0=cmask[:, :], scalar1=0.0, scalar2=None, op0=ALU.is_equal),
             [cmask.r], [tri.r])
        for h in range(4):
            P.op("dve", "tensor_tensor", dict(out=btile[:, 0, h, :], in0=btile[:, 0, h, :], in1=cmask[:, :], op=ALU.add),
                 [btile.r, cmask.r], [btile.r])

        evac_flip = [0]

        def evac(out_ap, in_ap, reads, writes, eng=None):
            if eng is None:
                eng = "act" if evac_flip[0] % 2 == 0 else "dve"
                evac_flip[0] += 1
            if eng == "act":
                P.op("act", "copy", dict(out=out_ap, in_=in_ap), reads, writes)
            else:
                P.op("dve", "tensor_copy", dict(out=out_ap, in_=in_ap), reads, writes)

        def load_gbc(src_row_ap):
            P.dma("sp", gbc[:, :], src_row_ap.partition_broadcast(128).rearrange("p a b -> p (a b)"), sem_g, writes=[gbc.r])

        def rms_tile(tt, dst_ap, dst_r):
            sc = nsc[tt % 2]
            P.op("act", "activation", dict(out=junk[:, :], in_=xres[:, tt, :], func=AF.Square), [xres.rs[tt]], [junk.r])
            P.op("dve", "reduce_sum", dict(out=sc[:, 0:1], in_=junk[:, :], axis=AX.X), [junk.r], [sc.r])
            P.op("act", "activation", dict(out=sc[:, 1:2], in_=sc[:, 0:1], func=AF.Sqrt, bias=EPS, scale=1.0 / D), [sc.r], [sc.r])
            P.op("dve", "reciprocal", dict(out=sc[:, 2:3], in_=sc[:, 1:2]), [sc.r], [sc.r])
            P.op("dve", "scalar_tensor_tensor", dict(out=dst_ap, in0=xres[:, tt, :], scalar=sc[:, 2:3], in1=gbc[:, :],
                                                     op0=ALU.mult, op1=ALU.mult), [xres.rs[tt], sc.r, gbc.r], [dst_r])

        def norm_T(hT, hn):
            for tt in range(NT):
                hb = hn[tt % 2]
                rms_tile(tt, hb[:, :], hb.r)
                for dc in range(8):
                    P.op("pe", "transpose", dict(out=pb16[:, dc * 128:(dc + 1) * 128], in_=hb[:, dc * 128:(dc + 1) * 128],
                                                 identity=identb[:, :]), [hb.r, identb.r], [pb16.r])
                evac(hT[:, :, tt * 128:(tt + 1) * 128], pb16[:, :].rearrange("p (a b) -> p a b", b=128),
                     [pb16.r], [hT.rs[tt // 4]])

        def load_w(dst, src_ap, semkey):
            P.dma("pool", dst_ap_full(dst), src_ap, semkey, writes=[dst.r])

        def dst_ap_full(t):
            nd = len(t.t.shape)
            return t[tuple([slice(None)] * nd)]

        def proj_fm(w, ncol_chunks, hT, dst, dst_chunk0, kch=8, rows=128, act_func=None):
            for oc in range(ncol_chunks):
                for tb in range(4):
                    bank = pf[4 + (tb % 2)]
                    for kc in range(kch):
                        P.op("pe", "matmul", dict(out=bank[:, :], lhsT=w[:, kc, oc * 128:(oc + 1) * 128],
                                                  rhs=hT[:, kc, tb * 512:(tb + 1) * 512], start=(kc == 0), stop=(kc == kch - 1)),
                             [w.r, hT.rs[tb]], [bank.r])
                    evac(dst[:, dst_chunk0 + oc, tb * 512:(tb + 1) * 512], bank[:, :], [bank.r], [dst.rs[tb]])

        def wcast(dst_ap, src_ap, semkey, dst_r):
            P.dma("pool", dst_ap, src_ap, semkey, writes=[dst_r])

        def proj_fm(w, nchunk, hT, dst, c0):
            for oc in range(nchunk):
                for tb in range(4):
                    bank = pf[4 + (tb % 2)]
                    for kc in range(8):
                        P.op("pe", "matmul", dict(out=bank[:, :], lhsT=w[:, kc, oc * 128:(oc + 1) * 128],
                                                  rhs=hT[:, kc, tb * 512:(tb + 1) * 512], start=(kc == 0), stop=(kc == 7)),
                             [w.r, hT.rs[tb]], [bank.r])
                    evac(dst[:, c0 + oc, tb * 512:(tb + 1) * 512], bank[:, :], [bank.r], [dst.rs[tb]])

        def proj_tm(w, hT, va, nh, dv):
            for tt in range(NT):
                bank = pf[4 + (tt % 2)]
                for kc in range(8):
                    P.op("pe", "matmul", dict(out=bank[:, 0:256], lhsT=hT[:, kc, tt * 128:(tt + 1) * 128], rhs=w[:, kc, :],
                                              start=(kc == 0), stop=(kc == 7)), [hT.rs[tt // 4], w.r], [bank.r])
                evac(va[:, tt, :, 0:dv], bank[:, 0:256].rearrange("p (a b) -> p a b", b=dv), [bank.r], [va.r])

        def dump_x(dst):
            semd = P.dma_sem()
            for tt in range(NT):
                P.dma("sp", dst[tt * 128:(tt + 1) * 128, :], xres[:, tt, :], semd, reads=[xres.rs[tt]])

        def phase_attention(layer):
            lam_init = 0.8 - 0.6 * math.exp(-0.3 * layer)
            with ExitStack() as sa:
                hT = sb(sa, "hT", [128, 8, S], BF16, nsub=4)
                ydT = sb(sa, "ydT", [128, 4, S], BF16, nsub=4)
                yfT = sb(sa, "yfT", [128, 4, S], BF16, nsub=4)
                lamb = sb(sa, "lamb", [128, 4, 64], F32)
                lsc = sb(sa, "lsc", [128, 8], F32)
                sgb = sb(sa, "sgb", [128, 128], F32)
                bfb = sb(sa, "bfb", [128, 8], F32)
                sx = ExitStack()
                wbuf = [sb(sx, "wbuf%d" % i, [128, 8, 256], BF16) for i in range(2)]
                semw = [P.dma_sem() for _ in range(2)]
                PT = [sb(sx, "PT%d" % i, [128, 512], BF16, nsub=4) for i in range(3)]
                dtmp = [sb(sx, "dtmp%d" % i, [128, 128], F32) for i in range(2)]
                ytk = sb(sx, "ytk", [128, 4, 128], BF16)
                y0 = [sb(sx, "y0_%d" % i, [128, 128], F32) for i in range(4)]
                ysc = [sb(sx, "ysc%d" % i, [128, 8], F32) for i in range(4)]
                sem_s = P.dma_sem()
                wslot = [0]

                def wcols(c0, n):
                    return win_d[layer, :, c0:c0 + n].rearrange("(kc p) c -> p kc c", p=128)

                def wload(c0):
                    i = wslot[0] % 2
                    wslot[0] += 1
                    wcast(wbuf[i][:, :, :], wcols(c0, 256), semw[i], wbuf[i].r)
                    return wbuf[i]

                load_gbc(n1_d[layer:layer + 1, :])
                P.dma("sp", lamb[:, :, :], lam_d[layer].partition_broadcast(128), sem_s, writes=[lamb.r])
                P.dma("sp", sgb[:, :], sg_d[layer:layer + 1, :].partition_broadcast(128).rearrange("p a b -> p (a b)"), sem_s, writes=[sgb.r])
                P.dma("sp", bfb[:, :], bf_d[layer:layer + 1, :].partition_broadcast(128).rearrange("p a b -> p (a b)"), sem_s, writes=[bfb.r])
                for r_ in (lamb.r, sgb.r, bfb.r):
                    r_.w = (sem_s, P.cnt[sem_s])
                with ExitStack() as sh:
                    hn = [sb(sh, "hn%d" % i, [128, D], BF16) for i in range(2)]
                    norm_T(hT, hn)
                P.barrier()

                P.op("dve", "tensor_tensor", dict(out=lamb[:, 0, :], in0=lamb[:, 0, :], in1=lamb[:, 1, :], op=ALU.mult), [lamb.r], [lamb.r])
                P.op("dve", "tensor_tensor", dict(out=lamb[:, 2, :], in0=lamb[:, 2, :], in1=lamb[:, 3, :], op=ALU.mult), [lamb.r], [lamb.r])
                P.op("dve", "reduce_sum", dict(out=lsc[:, 0:1], in_=lamb[:, 0, :], axis=AX.X), [lamb.r], [lsc.r])
                P.op("dve", "reduce_sum", dict(out=lsc[:, 1:2], in_=lamb[:, 2, :], axis=AX.X), [lamb.r], [lsc.r])
                P.op("act", "activation", dict(out=lsc[:, 2:4], in_=lsc[:, 0:2], func=AF.Exp), [lsc.r], [lsc.r])
                P.op("dve", "tensor_tensor", dict(out=lsc[:, 4:5], in0=lsc[:, 3:4], in1=lsc[:, 2:3], op=ALU.subtract), [lsc.r], [lsc.r])
                P.op("dve", "tensor_scalar", dict(out=lsc[:, 4:5], in0=lsc[:, 4:5], scalar1=-lam_init, scalar2=None, op0=ALU.add), [lsc.r], [lsc.r])
                P.op("dve", "tensor_scalar", dict(out=sgb[:, :], in0=sgb[:, :], scalar1=(1.0 - lam_init), scalar2=None, op0=ALU.mult), [sgb.r], [sgb.r])

                sflip = [0]

                def attn_core(kT, qT, ch, rows, va, hloc, dv, qc, kind, hglob, fb):
                    pend = None

                    def pv(kt, pt, qt0, nq):
                        for j in range(nq):
                            qt = qt0 + j
                            acc = pf[qt - 4 * qc]
                            P.op("pe", "matmul", dict(out=acc[:, 0:dv + 1], lhsT=pt[:, j * 128:(j + 1) * 128], rhs=va[:, kt, hloc, :],
                                                      start=(kt == 0), stop=(kt == qt)), [pt.rs[j], va.r], [acc.r])

                    for kt in range(4 * qc + 4):
                        q_lo = max(qc * 512, kt * 128)
                        n = (qc + 1) * 512 - q_lo
                        sbk = pf[4 + (sflip[0] % 3)]
                        pt = PT[sflip[0] % 3]
                        sflip[0] += 1
                        P.op("pe", "matmul", dict(out=sbk[:, 0:n], lhsT=kT[rows, ch, kt * 128:(kt + 1) * 128],
                                                  rhs=qT[rows, ch, q_lo:q_lo + n], start=True, stop=True),
                             [kT.rs[kt // 4], qT.rs[qc]], [sbk.r])
                        qt0 = q_lo // 128
                        nq = n // 128
                        if kind == "d":
                            col = 0
                            for j in range(nq):
                                dl = qt0 + j - kt
                                if dl <= 1:
                                    tmp = dtmp[(qt0 + j) % 2]
                                    P.op("dve", "scalar_tensor_tensor", dict(out=tmp[:, :], in0=sbk[:, j * 128:(j + 1) * 128], scalar=0.125,
                                                                             in1=btile[:, dl, hglob, :], op0=ALU.mult, op1=ALU.add),
                                         [sbk.r, btile.r], [tmp.r])
                                    P.op("act", "activation", dict(out=pt[:, j * 128:(j + 1) * 128], in_=tmp[:, :], func=AF.Exp),
                                         [tmp.r], [pt.rs[j]])
                                    col = (j + 1) * 128
                            if col < n:
                                P.op("act", "activation", dict(out=pt[:, col:n], in_=sbk[:, col:n], func=AF.Exp,
                                                               bias=rb31[:, hglob:hglob + 1], scale=0.125), [sbk.r, rb31.r],
                                     [pt.rs[j] for j in range(col // 128, nq)])
                        else:
                            for j in range(nq):
                                qt = qt0 + j
                                if qt == kt:
                                    tmp = dtmp[qt % 2]
                                    P.op("dve", "scalar_tensor_tensor", dict(out=tmp[:, :], in0=sbk[:, j * 128:(j + 1) * 128], scalar=0.125,
                                                                             in1=cmask[:, :], op0=ALU.mult, op1=ALU.add),
                                         [sbk.r, cmask.r], [tmp.r])
                                    P.op("act", "activation", dict(out=pt[:, j * 128:(j + 1) * 128], in_=tmp[:, :], func=AF.Exp,
                                                                   bias=fb[:, kt, qt, hglob:hglob + 1], scale=1.0), [tmp.r, fb.r], [pt.rs[j]])
                                else:
                                    P.op("act", "activation", dict(out=pt[:, j * 128:(j + 1) * 128], in_=sbk[:, j * 128:(j + 1) * 128], func=AF.Exp,
                                                                   bias=fb[:, kt, qt, hglob:hglob + 1], scale=0.125), [sbk.r, fb.r], [pt.rs[j]])
                        if pend is not None:
                            pv(*pend)
                        pend = (kt, pt, qt0, nq)
                    pv(*pend)

                for half in range(2):
                    with ExitStack() as s2:
                        dqT = sb(s2, "dqT", [128, 2, S], BF16, nsub=4)
                        dkT = sb(s2, "dkT", [128, 2, S], BF16, nsub=4)
                        dva = sb(s2, "dva", [128, NT, 2, 129], BF16)
                        P.op("pool", "memset", dict(ap=dva[:, :, :, 128:129], constant=1.0), [], [dva.r])
                        w1 = wload(0 + half * 256)
                        w2 = wload(512 + half * 256)
                        proj_fm(w1, 2, hT, dqT, 0)
                        w3 = wload(1024 + half * 256)
                        proj_fm(w2, 2, hT, dkT, 0)
                        proj_tm(w3, hT, dva, 2, 128)
                        for hl in range(2):
                            h = half * 2 + hl
                            for qc in range(4):
                                for m in range(2):
                                    attn_core(dkT, dqT, hl, slice(m * 64, (m + 1) * 64), dva, hl, 128, qc, "d", h, None)
                                    for j in range(4):
                                        acc = pf[j]
                                        sc = ysc[j]
                                        if m == 0:
                                            P.op("dve", "reciprocal", dict(out=sc[:, 0:1], in_=acc[:, 128:129]), [acc.r], [sc.r])
                                            P.op("dve", "tensor_scalar", dict(out=y0[j][:, :], in0=acc[:, 0:128], scalar1=sc[:, 0:1], scalar2=None,
                                                                              op0=ALU.mult), [acc.r, sc.r], [y0[j].r])
                                        else:
                                            P.op("dve", "reciprocal", dict(out=sc[:, 1:2], in_=acc[:, 128:129]), [acc.r], [sc.r])
                                            P.op("dve", "tensor_tensor", dict(out=sc[:, 2:3], in0=sc[:, 1:2], in1=lsc[:, 4:5], op=ALU.mult),
                                                 [sc.r, lsc.r], [sc.r])
                                            P.op("dve", "scalar_tensor_tensor", dict(out=y0[j][:, :], in0=acc[:, 0:128], scalar=sc[:, 2:3],
                                                                                     in1=y0[j][:, :], op0=ALU.mult, op1=ALU.add),
                                                 [acc.r, sc.r, y0[j].r], [y0[j].r])
                                for j in range(4):
                                    sc = ysc[j]
                                    tmp = dtmp[j % 2]
                                    P.op("act", "activation", dict(out=tmp[:, :], in_=y0[j][:, :], func=AF.Square), [y0[j].r], [tmp.r])
                                    P.op("dve", "reduce_sum", dict(out=sc[:, 3:4], in_=tmp[:, :], axis=AX.X), [tmp.r], [sc.r])
                                    P.op("act", "activation", dict(out=sc[:, 4:5], in_=sc[:, 3:4], func=AF.Sqrt, bias=EPS, scale=1.0 / 128),
                                         [sc.r], [sc.r])
                                    P.op("dve", "reciprocal", dict(out=sc[:, 5:6], in_=sc[:, 4:5]), [sc.r], [sc.r])
                                    P.op("dve", "scalar_tensor_tensor", dict(out=ytk[:, j, :], in0=y0[j][:, :], scalar=sc[:, 5:6], in1=sgb[:, :],
                                                                             op0=ALU.mult, op1=ALU.mult), [y0[j].r, sc.r, sgb.r], [ytk.r])
                                    P.op("pe", "transpose", dict(out=pb16[:, j * 128:(j + 1) * 128], in_=ytk[:, j, :], identity=identb[:, :]),
                                         [ytk.r, identb.r], [pb16.r])
                                evac(ydT[:, h, qc * 512:(qc + 1) * 512], pb16[:, 0:512], [pb16.r], [ydT.rs[qc]])
                    P.barrier()

                with ExitStack() as sf:
                    wff = sb(sf, "wff", [128, 8, 8], BF16)
                    zz = sb(sf, "zz", [128, NT, 8], F32)
                    csb = sb(sf, "csb", [128, NT, 8], F32)
                    bcs = sb(sf, "bcs", [128, NT, 8], F32)
                    off = sb(sf, "off", [128, NT + 1, 8], F32)
                    fb = sb(sf, "fb", [128, NT, NT, 8], F32)
                    semf = P.dma_sem()
                    wcast(wff[:, :, :], wcols(3072, 8), semf, wff.r)
                    for tt in range(NT):
                        for kc in range(8):
                            P.op("pe", "matmul", dict(out=pf[6][:, tt * 8:(tt + 1) * 8], lhsT=hT[:, kc, tt * 128:(tt + 1) * 128], rhs=wff[:, kc, :],
                                                      start=(kc == 0), stop=(kc == 7)), [hT.rs[tt // 4], wff.r], [pf[6].r])
                    P.op("dve", "tensor_tensor", dict(out=zz[:, :, :], in0=pf[6][:, 0:128].rearrange("p (a b) -> p a b", b=8),
                                                      in1=bfb[:, :].unsqueeze(1).to_broadcast([128, NT, 8]), op=ALU.add),
                         [pf[6].r, bfb.r], [zz.r])
                    P.op("act", "activation", dict(out=zz[:, :, :], in_=zz[:, :, :], func=AF.Exp, scale=-1.0), [zz.r], [zz.r])
                    P.op("act", "activation", dict(out=zz[:, :, :], in_=zz[:, :, :], func=AF.Ln, bias=1.0, scale=1.0), [zz.r], [zz.r])
                    P.op("pe", "matmul", dict(out=pf[6][:, 0:128], lhsT=tri[:, :], rhs=zz[:, :, :].rearrange("p a b -> p (a b)"),
                                              start=True, stop=True), [tri.r, zz.r], [pf[6].r])
                    P.op("dve", "tensor_copy", dict(out=csb[:, :, :], in_=pf[6][:, 0:128].rearrange("p (a b) -> p a b", b=8)), [pf[6].r], [csb.r])
                    P.op("pe", "matmul", dict(out=pf[6][:, 128:256], lhsT=sel127[:, :], rhs=csb[:, :, :].rearrange("p a b -> p (a b)"),
                                              start=True, stop=True), [sel127.r, csb.r], [pf[6].r])
                    P.op("dve", "tensor_copy", dict(out=bcs[:, :, :], in_=pf[6][:, 128:256].rearrange("p (a b) -> p a b", b=8)), [pf[6].r], [bcs.r])
                    P.op("dve", "memset", dict(ap=off[:, 0, :], constant=0.0), [], [off.r])
                    for tt in range(NT):
                        P.op("dve", "tensor_tensor", dict(out=off[:, tt + 1, :], in0=off[:, tt, :], in1=bcs[:, tt, :], op=ALU.add),
                             [off.r, bcs.r], [off.r])
                    P.op("dve", "tensor_tensor", dict(out=csb[:, :, :], in0=csb[:, :, :], in1=off[:, 0:NT, :], op=ALU.add), [csb.r, off.r], [csb.r])
                    P.op("dve", "tensor_tensor", dict(out=fb[:, :, :, :], in0=csb[:, :, :].unsqueeze(2).to_broadcast([128, NT, NT, 8]),
                                                      in1=off[:, 1:NT + 1, :].unsqueeze(1).to_broadcast([128, NT, NT, 8]), op=ALU.subtract),
                         [csb.r, off.r], [fb.r])
                    for half in range(2):
                        with ExitStack() as s3:
                            fqT = sb(s3, "fqT", [128, 2, S], BF16, nsub=4)
                            fkT = sb(s3, "fkT", [128, 2, S], BF16, nsub=4)
                            fva = sb(s3, "fva", [128, NT, 4, 65], BF16)
                            P.op("pool", "memset", dict(ap=fva[:, :, :, 64:65], constant=1.0), [], [fva.r])
                            w1 = wload(1536 + half * 256)
                            w2 = wload(2048 + half * 256)
                            proj_fm(w1, 2, hT, fqT, 0)
                            w3 = wload(2560 + half * 256)
                            proj_fm(w2, 2, hT, fkT, 0)
                            proj_tm(w3, hT, fva, 4, 64)
                            for pair in range(2):
                                for qc in range(4):
                                    for hp in range(2):
                                        hl = pair * 2 + hp
                                        h = half * 4 + hl
                                        attn_core(fkT, fqT, pair, slice(hp * 64, hp * 64 + 64), fva, hl, 64, qc, "f", h, fb)
                                        for j in range(4):
                                            acc = pf[j]
                                            sc = ysc[j]
                                            P.op("dve", "reciprocal", dict(out=sc[:, 0:1], in_=acc[:, 64:65]), [acc.r], [sc.r])
                                            P.op("dve", "tensor_scalar", dict(out=ytk[:, j, hp * 64:hp * 64 + 64], in0=acc[:, 0:64], scalar1=sc[:, 0:1],
                                                                              scalar2=None, op0=ALU.mult), [acc.r, sc.r], [ytk.r])
                                    for j in range(4):
                                        P.op("pe", "transpose", dict(out=pb16[:, j * 128:(j + 1) * 128], in_=ytk[:, j, :], identity=identb[:, :]),
                                             [ytk.r, identb.r], [pb16.r])
                                    evac(yfT[:, half * 2 + pair, qc * 512:(qc + 1) * 512], pb16[:, 0:512], [pb16.r], [yfT.rs[qc]])
                        P.barrier()

                P.barrier()
                sx.close()
                mT = sb(sa, "mT", [128, 8, S], BF16, nsub=4)
                with ExitStack() as s4:
                    wga = [sb(s4, "wga%d" % i, [128, 8, 128], BF16) for i in range(2)]
                    wgb = [sb(s4, "wgb%d" % i, [128, 8, 128], BF16) for i in range(2)]
                    wdo = [sb(s4, "wdo%d" % i, [128, 4, 128], BF16) for i in range(2)]
                    wfo = [sb(s4, "wfo%d" % i, [128, 4, 128], BF16) for i in range(2)]
                    semg4 = [P.dma_sem() for _ in range(2)]
                    sga = [sb(s4, "sga%d" % i, [128, 512], F32) for i in range(2)]
                    sgbb = [sb(s4, "sgbb%d" % i, [128, 512], F32) for i in range(2)]
                    m1 = [sb(s4, "m1_%d" % i, [128, 512], F32) for i in range(2)]

                    def load_c(c):
                        i = c % 2
                        wcast(wga[i][:, :, :], wcols(3080 + c * 128, 128), semg4[i], wga[i].r)
                        wcast(wgb[i][:, :, :], wcols(4104 + c * 128, 128), semg4[i], wgb[i].r)
                        wcast(wdo[i][:, :, :], wdo_d[layer, :, c * 128:(c + 1) * 128].rearrange("(kc p) n -> p kc n", p=128), semg4[i], wdo[i].r)
                        wcast(wfo[i][:, :, :], wfo_d[layer, :, c * 128:(c + 1) * 128].rearrange("(kc p) n -> p kc n", p=128), semg4[i], wfo[i].r)
                        for t_ in (wga[i], wgb[i], wdo[i], wfo[i]):
                            t_.r.w = (semg4[i], P.cnt[semg4[i]])
                    load_c(0)
                    it = 0
                    for c in range(8):
                        if c + 1 < 8:
                            load_c(c + 1)
                        i = c % 2
                        for tb in range(4):
                            k = it % 2
                            it += 1
                            tsl = slice(tb * 512, (tb + 1) * 512)
                            for kc in range(8):
                                P.op("pe", "matmul", dict(out=pf[4][:, :], lhsT=wga[i][:, kc, :], rhs=hT[:, kc, tsl], start=(kc == 0), stop=(kc == 7)),
                                     [wga[i].r, hT.rs[tb]], [pf[4].r])
                            P.op("act", "activation", dict(out=sga[k][:, :], in_=pf[4][:, :], func=AF.Sigmoid), [pf[4].r], [sga[k].r])
                            for kc in range(8):
                                P.op("pe", "matmul", dict(out=pf[5][:, :], lhsT=wgb[i][:, kc, :], rhs=hT[:, kc, tsl], start=(kc == 0), stop=(kc == 7)),
                                     [wgb[i].r, hT.rs[tb]], [pf[5].r])
                            P.op("act", "activation", dict(out=sgbb[k][:, :], in_=pf[5][:, :], func=AF.Sigmoid), [pf[5].r], [sgbb[k].r])
                            bd = pf[0 + 2 * k]
                            bf_ = pf[1 + 2 * k]
                            for hh in range(4):
                                P.op("pe", "matmul", dict(out=bd[:, :], lhsT=wdo[i][:, hh, :], rhs=ydT[:, hh, tsl], start=(hh == 0), stop=(hh == 3)),
                                     [wdo[i].r, ydT.rs[tb]], [bd.r])
                            for hh in range(4):
                                P.op("pe", "matmul", dict(out=bf_[:, :], lhsT=wfo[i][:, hh, :], rhs=yfT[:, hh, tsl], start=(hh == 0), stop=(hh == 3)),
                                     [wfo[i].r, yfT.rs[tb]], [bf_.r])
                            P.op("dve", "tensor_tensor", dict(out=m1[k][:, :], in0=bd[:, :], in1=sga[k][:, :], op=ALU.mult), [bd.r, sga[k].r], [m1[k].r])
                            P.op("dve", "tensor_tensor", dict(out=sgbb[k][:, :], in0=bf_[:, :], in1=sgbb[k][:, :], op=ALU.mult), [bf_.r, sgbb[k].r], [sgbb[k].r])
                            P.op("dve", "tensor_tensor", dict(out=mT[:, c, tsl], in0=m1[k][:, :], in1=sgbb[k][:, :], op=ALU.add),
                                 [m1[k].r, sgbb[k].r], [mT.rs[tb]])
                P.barrier()
                with ExitStack() as s5:
                    wo = sb(s5, "wo", [128, 8, D], BF16)
                    semo5 = P.dma_sem()
                    wsrc = wout_d[layer].rearrange("(kc p) n -> p kc n", p=128)
                    for q4 in range(4):
                        wcast(wo[:, 2 * q4:2 * q4 + 2, :], wsrc[:, 2 * q4:2 * q4 + 2, :], semo5, wo.r)
                    wo.r.w = (semo5, P.cnt[semo5])
                    for tt in range(NT):
                        for hf in range(2):
                            bank = pf[(2 * tt + hf) % 4]
                            for kc in range(8):
                                P.op("pe", "matmul", dict(out=bank[:, :], lhsT=mT[:, kc, tt * 128:(tt + 1) * 128], rhs=wo[:, kc, hf * 512:(hf + 1) * 512],
                                                          start=(kc == 0), stop=(kc == 7)), [mT.rs[tt // 4], wo.r], [bank.r])
                            P.op("dve", "tensor_tensor", dict(out=xres[:, tt, hf * 512:(hf + 1) * 512], in0=xres[:, tt, hf * 512:(hf + 1) * 512],
                                                              in1=bank[:, :], op=ALU.add), [xres.rs[tt], bank.r], [xres.rs[tt]])
            P.barrier()

        def phase_peer(layer):
            with ExitStack() as sp_:
                h2T = sb(sp_, "h2T", [128, 8, S], BF16, nsub=4)
                load_gbc(n2_d[layer:layer + 1, :])
                with ExitStack() as sh:
                    hn = [sb(sh, "hn%d" % i, [128, D], BF16) for i in range(2)]
                    norm_T(h2T, hn)
                P.barrier()
                with ExitStack() as s1:
                    KTf = sb(s1, "KTf", [128, 2, 128], F32)
                    KTc = sb(s1, "KTc", [128, 2, 128], BF16)
                    KT = sb(s1, "KT", [128, 2, 128], BF16)
                    qTb = sb(s1, "qTb", [128, 16, 512], BF16)
                    wqb = sb(s1, "wqb", [128, 8, 512], BF16)
                    s_sb = sb(s1, "s_sb", [128, 2048], F32)
                    wk = sb(s1, "wk", [128, 2048], F32)
                    vt = sb(s1, "vt", [128, 256], F32)
                    best = sb(s1, "best", [128, 128], F32)
                    eb = sb(s1, "eb", [128, 128], F32)
                    ev0 = sb(s1, "ev0", [128, 8, 16], F32)
                    zs = sb(s1, "zs", [128, 16], F32)
                    tm = sb(s1, "tm", [128, 3, 128], F32)
                    tT = sb(s1, "tT", [128, 3, 512], F32)
                    e1 = [sb(s1, "e1_%d" % i, [128, 4, 128], F32) for i in range(2)]
                    Fb = [sb(s1, "Fb%d" % i, [128, 4, 128], BF16) for i in range(2)]
                    E0b = [sb(s1, "E0b%d" % i, [128, 4, 128], BF16) for i in range(2)]
                    stg = [sb(s1, "stg%d" % i, [128, 128, 64], BF16) for i in range(2)]
                    qrep = [sb(s1, "qrep%d" % i, [128, 2, 4, 128], BF16, nsub=2) for i in range(2)]
                    semk = P.dma_sem()
                    semq = P.dma_sem()
                    semst = [P.dma_sem() for _ in range(2)]
                    P.dma("sp", KTf[:, :, :], sk_d[layer].rearrange("p n d -> n p d"), semk, writes=[KTf.r])
                    P.op("dve", "tensor_copy", dict(out=KTc[:, :, :], in_=KTf[:, :, :]), [KTf.r], [KTc.r])
                    for p in range(2):
                        P.op("pe", "transpose", dict(out=pb16[:, p * 128:(p + 1) * 128], in_=KTc[:, p, :], identity=identb[:, :]),
                             [KTc.r, identb.r], [pb16.r])
                    evac(KT[:, :, :], pb16[:, 0:256].rearrange("p (a b) -> p a b", b=128), [pb16.r], [KT.r])
                    sv = s_sb[:, :].rearrange("p (o n) -> p o n", n=128)
                    wv = wk[:, :].rearrange("p (o n) -> p o n", n=128)
                    cv4 = s_sb[:, :].rearrange("p (h a b) -> p h a b", a=16, b=16)
                    cv = s_sb[:, :].rearrange("p (h c) -> p h c", c=256)
                    wv2 = wk[:, :].rearrange("p (h c) -> p h c", c=256)
                    vv = vt[:, :].rearrange("p (h q a) -> p h q a", q=2, a=16)
                    bv = best[:, :].rearrange("p (h a) -> p h a", a=16)
                    S0 = [pf[0], pf[1]]
                    S1 = [pf[2], pf[3]]
                    GT = [pf[4], pf[5]]
                    for tb in range(4):
                        tsl = slice(tb * 512, (tb + 1) * 512)
                        for g in range(4):
                            wcast(wqb[:, :, :], wq_d[layer, :, g * 512:(g + 1) * 512].rearrange("(kc p) c -> p kc c", p=128), semq, wqb.r)
                            for oc in range(4):
                                bank = pf[4 + (oc % 2)]
                                for kc in range(8):
                                    P.op("pe", "matmul", dict(out=bank[:, :], lhsT=wqb[:, kc, oc * 128:(oc + 1) * 128], rhs=h2T[:, kc, tsl],
                                                              start=(kc == 0), stop=(kc == 7)), [wqb.r, h2T.rs[tb]], [bank.r])
                                evac(qTb[:, g * 4 + oc, :], bank[:, :], [bank.r], [qTb.r])
                        for ti in range(4):
                            tcol = slice(ti * 128, (ti + 1) * 128)
                            for oc in range(16):
                                P.op("pe", "matmul", dict(out=pf[oc // 4][:, (oc % 4) * 128:(oc % 4 + 1) * 128], lhsT=qTb[:, oc, tcol], rhs=KT[:, oc % 2, :],
                                                          start=True, stop=True), [qTb.r, KT.r], [pf[oc // 4].r])
                            for b4 in range(4):
                                P.op("act", "copy", dict(out=s_sb[:, b4 * 512:(b4 + 1) * 512], in_=pf[b4][:, :]), [pf[b4].r], [s_sb.r])
                            for oc in range(16):
                                P.op("dve", "max", dict(out=vt[:, oc * 16:oc * 16 + 8], in_=sv[:, oc, :]), [s_sb.r], [vt.r])
                                P.op("dve", "match_replace", dict(out=wv[:, oc, :], in_to_replace=vt[:, oc * 16:oc * 16 + 8], in_values=sv[:, oc, :],
                                                                  imm_value=-1e30), [vt.r, s_sb.r], [wk.r])
                                P.op("dve", "max", dict(out=vt[:, oc * 16 + 8:oc * 16 + 16], in_=wv[:, oc, :]), [wk.r], [vt.r])
                            P.op("dve", "tensor_tensor", dict(out=cv4, in0=vv[:, :, 0, :].unsqueeze(3).to_broadcast([128, 8, 16, 16]),
                                                              in1=vv[:, :, 1, :].unsqueeze(2).to_broadcast([128, 8, 16, 16]), op=ALU.add),
                                 [vt.r], [s_sb.r])
                            for h in range(8):
                                P.op("dve", "max", dict(out=best[:, h * 16:h * 16 + 8], in_=cv[:, h, :]), [s_sb.r], [best.r])
                                P.op("dve", "match_replace", dict(out=wv2[:, h, :], in_to_replace=best[:, h * 16:h * 16 + 8], in_values=cv[:, h, :],
                                                                  imm_value=-1e30), [best.r, s_sb.r], [wk.r])
                                P.op("dve", "max", dict(out=best[:, h * 16 + 8:h * 16 + 16], in_=wv2[:, h, :]), [wk.r], [best.r])
                            P.op("act", "activation", dict(out=eb[:, :], in_=best[:, :], func=AF.Exp), [best.r], [eb.r])
                            P.op("dve", "reduce_sum", dict(out=zs[:, 0:8], in_=eb[:, :].rearrange("p (h a) -> p h a", a=16), axis=AX.X), [eb.r], [zs.r])
                            P.op("dve", "reciprocal", dict(out=zs[:, 8:16], in_=zs[:, 0:8]), [zs.r], [zs.r])
                            P.op("act", "activation", dict(out=ev0[:, :, :], in_=vv[:, :, 0, :], func=AF.Exp), [vt.r], [ev0.r])
                            P.op("dve", "tensor_tensor", dict(out=tm[:, 0, :].rearrange("p (h a) -> p h a", a=16), in0=bv[:, :, 15:16].to_broadcast([128, 8, 16]),
                                                              in1=vv[:, :, 0, :], op=ALU.subtract), [best.r, vt.r], [tm.r])
                            P.op("dve", "tensor_copy", dict(out=tm[:, 1, :].rearrange("p (h a) -> p h a", a=16), in_=vv[:, :, 0, :]), [vt.r], [tm.r])
                            P.op("dve", "tensor_tensor", dict(out=tm[:, 2, :].rearrange("p (h a) -> p h a", a=16), in0=ev0[:, :, :],
                                                              in1=zs[:, 8:16].unsqueeze(2).to_broadcast([128, 8, 16]), op=ALU.mult), [ev0.r, zs.r], [tm.r])
                            for k3 in range(3):
                                P.op("pe", "transpose", dict(out=pf[6][:, k3 * 128:(k3 + 1) * 128], in_=tm[:, k3, :], identity=identf[:, :]),
                                     [tm.r, identf.r], [pf[6].r])
                            evac(tT[:, :, tcol], pf[6][:, 0:384].rearrange("p (a b) -> p a b", b=128), [pf[6].r], [tT.r])

                        def s_stage(nb):
                            k = nb % 2
                            for p in range(2):
                                P.op("act" if p == 0 else "dve", "copy" if p == 0 else "tensor_copy",
                                     dict(out=qrep[k][:, p, :, :].rearrange("d t (h a) -> d t h a", a=16),
                                          in_=qTb[:, p:16:2, nb * 4:nb * 4 + 4].rearrange("d h t -> d t h").unsqueeze(3).to_broadcast([128, 4, 8, 16])),
                                     [qTb.r], [qrep[k].rs[p]])
                            for u in range(4):
                                tl = nb * 4 + u
                                us = slice(u * 128, (u + 1) * 128)
                                P.op("pe", "matmul", dict(out=S0[k][:, us], lhsT=qrep[k][:, 0, u, :], rhs=KT[:, 0, :],
                                                          start=True, stop=True), [qrep[k].rs[0], KT.r], [S0[k].r])
                                P.op("pe", "matmul", dict(out=S1[k][:, us], lhsT=qrep[k][:, 1, u, :], rhs=KT[:, 1, :],
                                                          start=True, stop=True), [qrep[k].rs[1], KT.r], [S1[k].r])
                            P.op("act", "activation", dict(out=e1[k][:, :, :], in_=S1[k][:, :].rearrange("p (a b) -> p a b", b=128), func=AF.Exp),
                                 [S1[k].r], [e1[k].r])
                            for u in range(4):
                                tl = nb * 4 + u
                                us = slice(u * 128, (u + 1) * 128)
                                P.op("dve", "scalar_tensor_tensor", dict(out=Fb[k][:, u, :], in0=S1[k][:, us], scalar=tT[:, 0, tl:tl + 1], in1=e1[k][:, u, :],
                                                                         op0=ALU.is_ge, op1=ALU.mult), [S1[k].r, tT.r, e1[k].r], [Fb[k].r])
                                P.op("dve", "tensor_scalar", dict(out=E0b[k][:, u, :], in0=S0[k][:, us], scalar1=tT[:, 1, tl:tl + 1], scalar2=tT[:, 2, tl:tl + 1],
                                                                  op0=ALU.is_equal, op1=ALU.mult), [S0[k].r, tT.r], [E0b[k].r])

                        def g_stage(nb):
                            k = nb % 2
                            for u in range(4):
                                us = slice(u * 128, (u + 1) * 128)
                                P.op("pe", "matmul", dict(out=GT[k][:, us], lhsT=Fb[k][:, u, :], rhs=E0b[k][:, u, :], start=True, stop=True),
                                     [Fb[k].r, E0b[k].r], [GT[k].r])
                            tg = tb * 512 + nb * 4
                            ss = (tg // 64) % 2
                            t64 = tg % 64
                            P.op("act", "copy", dict(out=stg[ss][:, :, t64:t64 + 4].rearrange("j i t -> j t i"),
                                                     in_=GT[k][:, :].rearrange("p (a b) -> p a b", b=128)), [GT[k].r], [stg[ss].r])
                            if t64 + 4 == 64:
                                t0 = tg + 4 - 64
                                for i4 in range(4):
                                    P.dma("sp", gscr[i4 * 32:(i4 + 1) * 32, :, t0:t0 + 64].rearrange("i j t -> j i t"), stg[ss][:, i4 * 32:(i4 + 1) * 32, :],
                                          semst[ss], reads=[stg[ss].r])

                        for nb in range(128):
                            s_stage(nb)
                            if nb >= 1:
                                g_stage(nb - 1)
                        g_stage(127)
                P.barrier()
                with ExitStack() as s2:
                    Ub = [sb(s2, "Ub%d" % i, [128, 4, D], BF16) for i in range(2)]
                    Vb = [sb(s2, "Vb%d" % i, [128, 4, D], BF16) for i in range(2)]
                    UT = [sb(s2, "UT%d" % i, [128, 4, 8, 128], BF16) for i in range(2)]
                    Gb = [sb(s2, "Gb%d" % i, [128, 4, 256], BF16) for i in range(2)]
                    gl = [sb(s2, "gl%d" % i, [128, 256], BF16) for i in range(2)]
                    PTb = [sb(s2, "PTb%d" % i, [128, 256], BF16) for i in range(2)]
                    semU = [P.dma_sem() for _ in range(2)]
                    semV = [P.dma_sem() for _ in range(2)]
                    semG = [P.dma_sem() for _ in range(2)]
                    NEG_ = 32

                    def load_eg(eg):
                        i = eg % 2
                        wcast(Ub[i][:, :, :], eu_d[layer, eg * 512:(eg + 1) * 512, :].rearrange("(c p) d -> p c d", p=128), semU[i], Ub[i].r)
                        wcast(Vb[i][:, :, :], ev_d[layer, eg * 512:(eg + 1) * 512, :].rearrange("(c p) d -> p c d", p=128), semV[i], Vb[i].r)

                    def prep_eg(eg):
                        i = eg % 2
                        for ci in range(4):
                            for dc in range(8):
                                P.op("pe", "transpose", dict(out=pb16[:, dc * 128:(dc + 1) * 128], in_=Ub[i][:, ci, dc * 128:(dc + 1) * 128],
                                                             identity=identb[:, :]), [Ub[i].r, identb.r], [pb16.r])
                            evac(UT[i][:, ci, :, :], pb16[:, :].rearrange("p (a b) -> p a b", b=128), [pb16.r], [UT[i].r])

                    gcount = [0]

                    def load_g(eg, tb):
                        i = gcount[0] % 2
                        gcount[0] += 1
                        P.dma("sp", Gb[i][:, :, :], gscr[eg * 4:(eg + 1) * 4, :, tb * 256:(tb + 1) * 256].rearrange("i j t -> j i t"), semG[i], writes=[Gb[i].r])
                        return Gb[i]

                    def pv_stage(item):
                        eg, tb, ci, n, gbuf = item
                        i = eg % 2
                        for t2 in range(2):
                            for hf in range(2):
                                acc = pf[t2 * 2 + hf]
                                P.op("pe", "matmul", dict(out=acc[:, :], lhsT=PTb[n % 2][:, t2 * 128:(t2 + 1) * 128], rhs=Vb[i][:, ci, hf * 512:(hf + 1) * 512],
                                                          start=(ci == 0), stop=(ci == 3)), [PTb[n % 2].r, Vb[i].r], [acc.r])
                        if ci == 3:
                            for t2 in range(2):
                                tt = tb * 2 + t2
                                for hf in range(2):
                                    acc = pf[t2 * 2 + hf]
                                    P.op("dve", "tensor_tensor", dict(out=xres[:, tt, hf * 512:(hf + 1) * 512], in0=xres[:, tt, hf * 512:(hf + 1) * 512],
                                                                      in1=acc[:, :], op=ALU.add), [xres.rs[tt], acc.r], [xres.rs[tt]])

                    load_eg(0)
                    prev = None
                    n = 0
                    gnext = load_g(0, 0)
                    for eg in range(NEG_):
                        if eg + 1 < NEG_:
                            load_eg(eg + 1)
                        prep_eg(eg)
                        i = eg % 2
                        for tb in range(8):
                            gbuf = gnext
                            if tb + 1 < 8:
                                gnext = load_g(eg, tb + 1)
                            elif eg + 1 < NEG_:
                                gnext = load_g(eg + 1, 0)
                            for ci in range(4):
                                bank = pf[4 + (n % 2)]
                                for dc in range(8):
                                    P.op("pe", "matmul", dict(out=bank[:, 0:256], lhsT=UT[i][:, ci, dc, :], rhs=h2T[:, dc, tb * 256:(tb + 1) * 256],
                                                              start=(dc == 0), stop=(dc == 7)), [UT[i].r, h2T.rs[tb // 2]], [bank.r])
                                P.op("act", "activation", dict(out=gl[n % 2][:, :], in_=bank[:, 0:256], func=AF.Gelu), [bank.r], [gl[n % 2].r])
                                P.op("dve", "tensor_tensor", dict(out=PTb[n % 2][:, :], in0=gl[n % 2][:, :], in1=gbuf[:, ci, :], op=ALU.mult),
                                     [gl[n % 2].r, gbuf.r], [PTb[n % 2].r])
                                if prev is not None:
                                    pv_stage(prev)
                                prev = (eg, tb, ci, n, gbuf)
                                n += 1
                        pv_stage(prev)
                        prev = None
            P.barrier()

        for layer in range(depth):
            phase_attention(layer)
            if debug == "A" and layer == depth - 1:
                dump_x(dbg_d)
                break
            phase_peer(layer)
            if debug == "P" and layer == depth - 1:
                dump_x(dbg_d)
        with ExitStack() as sfin:
            ostg = [sb(sfin, "ostg%d" % i, [128, D], F32) for i in range(2)]
            semo = [P.dma_sem() for _ in range(2)]
            load_gbc(nf_d[0:1, :])
            for tt in range(NT):
                o = ostg[tt % 2]
                rms_tile(tt, o[:, :], o.r)
                P.dma("sp", out_d[tt * 128:(tt + 1) * 128, :], o[:, :], semo[tt % 2], reads=[o.r])
            P.barrier()
        P.emit()
    return nc


_CONSTS = None


def _consts():
    global _CONSTS
    if _CONSTS is None:
        p = np.arange(128)
        cm = np.where(p[None, :] >= p[:, None], 0.0, NEG).astype(np.float32)
        ident = np.eye(128, dtype=np.float32)
        s127 = np.zeros((128, 128), np.float32)
        s127[127, :] = 1.0
        bidx = np.stack([t5_bucket_np(dl * 128 + p[None, :] - p[:, None]) for dl in range(2)], 0)
        _CONSTS = (cm, ident, s127, bidx)
    return _CONSTS


def kernel(**inputs):
    cm, ident, s127, bidx = _consts()
    f = lambda a: np.ascontiguousarray(np.asarray(a, dtype=np.float32))
    rel_bias = f(inputs["rel_bias"])
    bt = np.ascontiguousarray(np.transpose(rel_bias[bidx], (1, 0, 3, 2)))
    shared = {k: f(inputs[k]) for k in ("norm1_g", "w_in", "b_forget", "diff_lambda", "diff_subln_g", "w_diff_o", "w_fox_o",
                                        "w_out", "norm2_g", "w_query", "sub_keys", "expert_u", "expert_v")}
    shared["rel_bias"] = rel_bias
    shared["final_norm_g"] = f(inputs["final_norm_g"]).reshape(1, D)
    shared["bias_tiles"] = bt
    shared["cmask"] = cm
    shared["ident"] = ident
    shared["sel127"] = s127
    x = f(inputs["x"])
    nb = x.shape[0]
    nc = build_program()
    in_maps = []
    for b in range(nb):
        m = dict(shared)
        m["x"] = np.ascontiguousarray(x[b])
        in_maps.append(m)
    res = run_bass_kernel_spmd(nc, in_maps, core_ids=list(range(nb)))
    return np.stack([np.asarray(r["out"], dtype=np.float32) for r in res.results], axis=0)
```

```python
import math
import numpy as np
from contextlib import ExitStack
import concourse.bass as bass
import concourse.mybir as mybir
from concourse.bass_utils import run_bass_kernel_spmd

F32 = mybir.dt.float32
BF16 = mybir.dt.bfloat16
AF = mybir.ActivationFunctionType
ALU = mybir.AluOpType
AX = mybir.AxisListType

S = 2048
D = 1024
NT = 16
DEPTH = 2
INW = 5128
EPS = 1e-6
NEG = -30000.0
ENGS = ("pe", "act", "dve", "pool", "sp")


class R:
    __slots__ = ("name", "w", "rd")

    def __init__(self, name=""):
        self.name = name
        self.w = None
        self.rd = {}


class Prog:
    def __init__(self, nc, stack):
        self.nc = nc
        self.stack = stack
        self.ops = {e: [] for e in ENGS}
        self.sems = {}
        self.cnt = {}
        self.seen = {e: {} for e in ENGS}
        for e in ENGS:
            self._mksem(e)
        self.n_dma_sem = 0

    CH = 16000

    def _mksem(self, key):
        self.sems[key] = self.stack.enter_context(self.nc.semaphore(str(key)))
        self.cnt[key] = 0

    def _semval(self, k, v):
        if k in ENGS:
            c = (v - 1) // self.CH
            key = (k, c)
            if key not in self.sems:
                self.sems[key] = self.stack.enter_context(self.nc.semaphore("%s_%d" % (k, c)))
            return self.sems[key], (v - 1) % self.CH + 1
        return self.sems[k], v

    def dma_sem(self):
        key = "dma%d" % self.n_dma_sem
        self.n_dma_sem += 1
        self._mksem(key)
        return key

    def _deps(self, eng, reads, writes):
        deps = {}

        def add(d):
            if d is None:
                return
            k, v = d
            if deps.get(k, 0) < v:
                deps[k] = v
        for r in reads:
            add(r.w)
        for r in writes:
            add(r.w)
            for k, v in r.rd.items():
                add((k, v))
        waits = []
        for k, v in deps.items():
            if k == eng and eng == "pe":
                continue
            if self.seen[eng].get(k, 0) >= v:
                continue
            self.seen[eng][k] = v
            waits.append(self._semval(k, v))
        return waits

    def op(self, eng, name, kw, reads=(), writes=()):
        wl = self._deps(eng, reads, writes)
        self.cnt[eng] += 1
        seq = self.cnt[eng]
        self.ops[eng].append((wl, name, kw, self._semval(eng, seq)[0], 1))
        for r in writes:
            r.w = (eng, seq)
            r.rd = {}
        for r in reads:
            r.rd[eng] = seq

    def dma(self, eng, out, in_, semkey, reads=(), writes=()):
        wl = self._deps(eng, reads, writes)
        self.cnt[semkey] += 16
        val = self.cnt[semkey]
        self.ops[eng].append((wl, "dma_start", dict(out=out, in_=in_), self.sems[semkey], 16))
        for r in writes:
            r.w = (semkey, val)
            r.rd = {}
        for r in reads:
            r.rd[semkey] = val

    def barrier(self):
        for e in ENGS:
            wl = []
            for k in self.cnt:
                if k == e:
                    continue
                v = self.cnt[k]
                if v > self.seen[e].get(k, 0):
                    self.seen[e][k] = v
                    wl.append(self._semval(k, v))
            if wl:
                self.ops[e].append((wl, None, None, None, 0))

    def emit(self):
        nc = self.nc

        def play(e, lst):
            for wl, name, kw, sem, inc in lst:
                for s, v in wl:
                    e.wait_ge(s, v)
                if name is not None:
                    getattr(e, name)(**kw).then_inc(sem, inc)

        with nc.Block() as block:
            @block.tensor
            def _(e):
                play(e, self.ops["pe"])

            @block.scalar
            def _(e):
                play(e, self.ops["act"])

            @block.vector
            def _(e):
                play(e, self.ops["dve"])

            @block.gpsimd
            def _(e):
                play(e, self.ops["pool"])

            @block.sync
            def _(e):
                play(e, self.ops["sp"])


class T:
    def __init__(self, t, nsub=0, name=""):
        self.t = t
        self.r = R(name)
        self.rs = [R("%s%d" % (name, i)) for i in range(nsub)]

    def __getitem__(self, k):
        return self.t[k]


def t5_bucket_np(dist):
    n = np.maximum(dist, 0)
    me = 16
    nf = np.maximum(n, 1).astype(np.float32)
    large = me + (np.log(nf / me) / math.log(128 / me) * (32 - me)).astype(np.int32)
    large = np.minimum(large, 31)
    return np.where(n < me, n, large)


def build_program(debug=None, depth=DEPTH):
    nc = bass.Bass("TRN2", target_bir_lowering=False)
    dr = {}

    def din(name, shape):
        dr[name] = nc.dram_tensor(name, list(shape), F32, kind="ExternalInput").ap()
        return dr[name]

    x_d = din("x", [S, D])
    n1_d = din("norm1_g", [DEPTH, D])
    win_d = din("w_in", [DEPTH, D, INW])
    bf_d = din("b_forget", [DEPTH, 8])
    lam_d = din("diff_lambda", [DEPTH, 4, 64])
    sg_d = din("diff_subln_g", [DEPTH, 128])
    wdo_d = din("w_diff_o", [DEPTH, 512, D])
    wfo_d = din("w_fox_o", [DEPTH, 512, D])
    wout_d = din("w_out", [DEPTH, D, D])
    n2_d = din("norm2_g", [DEPTH, D])
    wq_d = din("w_query", [DEPTH, D, 2048])
    sk_d = din("sub_keys", [DEPTH, 2, 128, 128])
    eu_d = din("expert_u", [DEPTH, 16384, D])
    ev_d = din("expert_v", [DEPTH, 16384, D])
    rb_d = din("rel_bias", [32, 4])
    nf_d = din("final_norm_g", [1, D])
    bt_d = din("bias_tiles", [128, 2, 4, 128])
    cm_d = din("cmask", [128, 128])
    id_d = din("ident", [128, 128])
    s127_d = din("sel127", [128, 128])
    out_d = nc.dram_tensor("out", [S, D], F32, kind="ExternalOutput").ap()
    gscr = nc.dram_tensor("gscr", [128, 128, S], BF16).ap()
    dbg_d = None
    if debug is not None:
        dbg_d = nc.dram_tensor("dbg", [S, D], F32, kind="ExternalOutput").ap()

    with ExitStack() as st:
        P = Prog(nc, st)

        uid = [0]

        def sb(stk, name, shape, dt, nsub=0):
            uid[0] += 1
            name = "%s_%d" % (name, uid[0])
            return T(stk.enter_context(nc.sbuf_tensor(name, list(shape), dt)), nsub, name)

        xres = sb(st, "xres", [128, NT, D], F32, nsub=NT)
        pf = [T(st.enter_context(nc.psum_tensor("pf%d" % i, [128, 512], F32)), 0, "pf%d" % i) for i in range(7)]
        pb16 = T(st.enter_context(nc.psum_tensor("pb16", [128, 1024], BF16)), 0, "pb16")
        identf = sb(st, "identf", [128, 128], F32)
        identb = sb(st, "identb", [128, 128], BF16)
        cmask = sb(st, "cmask_sb", [128, 128], F32)
        tri = sb(st, "tri", [128, 128], F32)
        sel127 = sb(st, "sel127_sb", [128, 128], F32)
        btile = sb(st, "btile", [128, 2, 4, 128], F32)
        rb31 = sb(st, "rb31", [128, 4], F32)
        gbc = sb(st, "gbc", [128, D], F32)
        junk = sb(st, "junk", [128, D], F32)
        nsc = [sb(st, "nsc%d" % i, [128, 4], F32) for i in range(2)]
        sem_c = P.dma_sem()
        sem_x = P.dma_sem()
        sem_g = P.dma_sem()

        P.dma("sp", identf[:, :], id_d, sem_c, writes=[identf.r])
        P.dma("sp", cmask[:, :], cm_d, sem_c, writes=[cmask.r])
        P.dma("sp", sel127[:, :], s127_d, sem_c, writes=[sel127.r])
        P.dma("sp", btile[:, :, :, :], bt_d, sem_c, writes=[btile.r])
        P.dma("sp", rb31[:, :], rb_d[31:32, :].partition_broadcast(128).rearrange("p a b -> p (a b)"), sem_c, writes=[rb31.r])
        for r_ in (identf.r, cmask.r, sel127.r, btile.r, rb31.r):
            r_.w = (sem_c, P.cnt[sem_c])
        for tt in range(NT):
            P.dma("sp", xres[:, tt, :], x_d[tt * 128:(tt + 1) * 128, :], sem_x, writes=[xres.rs[tt]])
        for tt in range(NT):
            xres.rs[tt].w = (sem_x, P.cnt[sem_x])
        P.op("dve", "tensor_copy", dict(out=identb[:, :], in_=identf[:, :]), [identf.r], [identb.r])
        P.op("dve", "tensor_scalar", dict(out=tri[:, :], in0=cmask[:, :], scalar1=0.0, scalar2=None, op0=ALU.is_equal),
             [cmask.r], [tri.r])
        for h in range(4):
            P.op("dve", "tensor_tensor", dict(out=btile[:, 0, h, :], in0=btile[:, 0, h, :], in1=cmask[:, :], op=ALU.add),
                 [btile.r, cmask.r], [btile.r])

        evac_flip = [0]

        def evac(out_ap, in_ap, reads, writes, eng=None):
            if eng is None:
                eng = "act" if evac_flip[0] % 2 == 0 else "dve"
                evac_flip[0] += 1
            if eng == "act":
                P.op("act", "copy", dict(out=out_ap, in_=in_ap), reads, writes)
            else:
                P.op("dve", "tensor_copy", dict(out=out_ap, in_=in_ap), reads, writes)

        def load_gbc(src_row_ap):
            P.dma("sp", gbc[:, :], src_row_ap.partition_broadcast(128).rearrange("p a b -> p (a b)"), sem_g, writes=[gbc.r])

        def rms_tile(tt, dst_ap, dst_r):
            sc = nsc[tt % 2]
            P.op("act", "activation", dict(out=junk[:, :], in_=xres[:, tt, :], func=AF.Square), [xres.rs[tt]], [junk.r])
            P.op("dve", "reduce_sum", dict(out=sc[:, 0:1], in_=junk[:, :], axis=AX.X), [junk.r], [sc.r])
            P.op("act", "activation", dict(out=sc[:, 1:2], in_=sc[:, 0:1], func=AF.Sqrt, bias=EPS, scale=1.0 / D), [sc.r], [sc.r])
            P.op("dve", "reciprocal", dict(out=sc[:, 2:3], in_=sc[:, 1:2]), [sc.r], [sc.r])
            P.op("dve", "scalar_tensor_tensor", dict(out=dst_ap, in0=xres[:, tt, :], scalar=sc[:, 2:3], in1=gbc[:, :],
                                                     op0=ALU.mult, op1=ALU.mult), [xres.rs[tt], sc.r, gbc.r], [dst_r])

        def norm_T(hT, hn):
            for tt in range(NT):
                hb = hn[tt % 2]
                rms_tile(tt, hb[:, :], hb.r)
                for dc in range(8):
                    P.op("pe", "transpose", dict(out=pb16[:, dc * 128:(dc + 1) * 128], in_=hb[:, dc * 128:(dc + 1) * 128],
                                                 identity=identb[:, :]), [hb.r, identb.r], [pb16.r])
                evac(hT[:, :, tt * 128:(tt + 1) * 128], pb16[:, :].rearrange("p (a b) -> p a b", b=128),
                     [pb16.r], [hT.rs[tt // 4]])

        def load_w(dst, src_ap, semkey):
            P.dma("pool", dst_ap_full(dst), src_ap, semkey, writes=[dst.r])

        def dst_ap_full(t):
            nd = len(t.t.shape)
            return t[tuple([slice(None)] * nd)]

        def proj_fm(w, ncol_chunks, hT, dst, dst_chunk0, kch=8, rows=128, act_func=None):
            for oc in range(ncol_chunks):
                for tb in range(4):
                    bank = pf[4 + (tb % 2)]
                    for kc in range(kch):
                        P.op("pe", "matmul", dict(out=bank[:, :], lhsT=w[:, kc, oc * 128:(oc + 1) * 128],
                                                  rhs=hT[:, kc, tb * 512:(tb + 1) * 512], start=(kc == 0), stop=(kc == kch - 1)),
                             [w.r, hT.rs[tb]], [bank.r])
                    evac(dst[:, dst_chunk0 + oc, tb * 512:(tb + 1) * 512], bank[:, :], [bank.r], [dst.rs[tb]])

        def wcast(dst_ap, src_ap, semkey, dst_r):
            P.dma("pool", dst_ap, src_ap, semkey, writes=[dst_r])

        def proj_fm(w, nchunk, hT, dst, c0):
            for oc in range(nchunk):
                for tb in range(4):
                    bank = pf[4 + (tb % 2)]
                    for kc in range(8):
                        P.op("pe", "matmul", dict(out=bank[:, :], lhsT=w[:, kc, oc * 128:(oc + 1) * 128],
                                                  rhs=hT[:, kc, tb * 512:(tb + 1) * 512], start=(kc == 0), stop=(kc == 7)),
                             [w.r, hT.rs[tb]], [bank.r])
                    evac(dst[:, c0 + oc, tb * 512:(tb + 1) * 512], bank[:, :], [bank.r], [dst.rs[tb]])

        def proj_tm(w, hT, va, nh, dv):
            for tt in range(NT):
                bank = pf[4 + (tt % 2)]
                for kc in range(8):
                    P.op("pe", "matmul", dict(out=bank[:, 0:256], lhsT=hT[:, kc, tt * 128:(tt + 1) * 128], rhs=w[:, kc, :],
                                              start=(kc == 0), stop=(kc == 7)), [hT.rs[tt // 4], w.r], [bank.r])
                evac(va[:, tt, :, 0:dv], bank[:, 0:256].rearrange("p (a b) -> p a b", b=dv), [bank.r], [va.r])

        def dump_x(dst):
            semd = P.dma_sem()
            for tt in range(NT):
                P.dma("sp", dst[tt * 128:(tt + 1) * 128, :], xres[:, tt, :], semd, reads=[xres.rs[tt]])

        def phase_attention(layer):
            lam_init = 0.8 - 0.6 * math.exp(-0.3 * layer)
            with ExitStack() as sa:
                hT = sb(sa, "hT", [128, 8, S], BF16, nsub=4)
                ydT = sb(sa, "ydT", [128, 4, S], BF16, nsub=4)
                yfT = sb(sa, "yfT", [128, 4, S], BF16, nsub=4)
                lamb = sb(sa, "lamb", [128, 4, 64], F32)
                lsc = sb(sa, "lsc", [128, 8], F32)
                sgb = sb(sa, "sgb", [128, 128], F32)
                bfb = sb(sa, "bfb", [128, 8], F32)
                sx = ExitStack()
                wbuf = [sb(sx, "wbuf%d" % i, [128, 8, 256], BF16) for i in range(2)]
                semw = [P.dma_sem() for _ in range(2)]
                PT = [sb(sx, "PT%d" % i, [128, 512], BF16, nsub=4) for i in range(4)]
                dtmp = [sb(sx, "dtmp%d" % i, [128, 128], F32) for i in range(2)]
                ytk = sb(sx, "ytk", [128, 4, 128], BF16)
                y0 = [sb(sx, "y0_%d" % i, [128, 128], F32) for i in range(4)]
                ysc = [sb(sx, "ysc%d" % i, [128, 8], F32) for i in range(4)]
                sem_s = P.dma_sem()
                wslot = [0]

                def wcols(c0, n):
                    return win_d[layer, :, c0:c0 + n].rearrange("(kc p) c -> p kc c", p=128)

                def wload(c0):
                    i = wslot[0] % 2
                    wslot[0] += 1
                    wcast(wbuf[i][:, :, :], wcols(c0, 256), semw[i], wbuf[i].r)
                    return wbuf[i]

                load_gbc(n1_d[layer:layer + 1, :])
                P.dma("sp", lamb[:, :, :], lam_d[layer].partition_broadcast(128), sem_s, writes=[lamb.r])
                P.dma("sp", sgb[:, :], sg_d[layer:layer + 1, :].partition_broadcast(128).rearrange("p a b -> p (a b)"), sem_s, writes=[sgb.r])
                P.dma("sp", bfb[:, :], bf_d[layer:layer + 1, :].partition_broadcast(128).rearrange("p a b -> p (a b)"), sem_s, writes=[bfb.r])
                for r_ in (lamb.r, sgb.r, bfb.r):
                    r_.w = (sem_s, P.cnt[sem_s])
                with ExitStack() as sh:
                    hn = [sb(sh, "hn%d" % i, [128, D], BF16) for i in range(2)]
                    norm_T(hT, hn)
                P.barrier()

                P.op("dve", "tensor_tensor", dict(out=lamb[:, 0, :], in0=lamb[:, 0, :], in1=lamb[:, 1, :], op=ALU.mult), [lamb.r], [lamb.r])
                P.op("dve", "tensor_tensor", dict(out=lamb[:, 2, :], in0=lamb[:, 2, :], in1=lamb[:, 3, :], op=ALU.mult), [lamb.r], [lamb.r])
                P.op("dve", "reduce_sum", dict(out=lsc[:, 0:1], in_=lamb[:, 0, :], axis=AX.X), [lamb.r], [lsc.r])
                P.op("dve", "reduce_sum", dict(out=lsc[:, 1:2], in_=lamb[:, 2, :], axis=AX.X), [lamb.r], [lsc.r])
                P.op("act", "activation", dict(out=lsc[:, 2:4], in_=lsc[:, 0:2], func=AF.Exp), [lsc.r], [lsc.r])
                P.op("dve", "tensor_tensor", dict(out=lsc[:, 4:5], in0=lsc[:, 3:4], in1=lsc[:, 2:3], op=ALU.subtract), [lsc.r], [lsc.r])
                P.op("dve", "tensor_scalar", dict(out=lsc[:, 4:5], in0=lsc[:, 4:5], scalar1=-lam_init, scalar2=None, op0=ALU.add), [lsc.r], [lsc.r])
                P.op("dve", "tensor_scalar", dict(out=sgb[:, :], in0=sgb[:, :], scalar1=(1.0 - lam_init), scalar2=None, op0=ALU.mult), [sgb.r], [sgb.r])

                sflip = [0]

                def attn_core(kT, qT, ch, rows, va, hloc, dv, qc, kind, hglob, fb):
                    pend = []

                    def pv(kt, pt, qt0, nq):
                        for j in range(nq):
                            qt = qt0 + j
                            acc = pf[qt - 4 * qc]
                            P.op("pe", "matmul", dict(out=acc[:, 0:dv + 1], lhsT=pt[:, j * 128:(j + 1) * 128], rhs=va[:, kt, hloc, :],
                                                      start=(kt == 0), stop=(kt == qt)), [pt.rs[j], va.r], [acc.r])

                    for kt in range(4 * qc + 4):
                        q_lo = max(qc * 512, kt * 128)
                        n = (qc + 1) * 512 - q_lo
                        sbk = pf[4 + (sflip[0] % 3)]
                        pt = PT[sflip[0] % 4]
                        sflip[0] += 1
                        P.op("pe", "matmul", dict(out=sbk[:, 0:n], lhsT=kT[rows, ch, kt * 128:(kt + 1) * 128],
                                                  rhs=qT[rows, ch, q_lo:q_lo + n], start=True, stop=True),
                             [kT.rs[kt // 4], qT.rs[qc]], [sbk.r])
                        qt0 = q_lo // 128
                        nq = n // 128
                        if kind == "d":
                            col = 0
                            for j in range(nq):
                                dl = qt0 + j - kt
                                if dl <= 1:
                                    tmp = dtmp[(qt0 + j) % 2]
                                    P.op("dve", "scalar_tensor_tensor", dict(out=tmp[:, :], in0=sbk[:, j * 128:(j + 1) * 128], scalar=0.125,
                                                                             in1=btile[:, dl, hglob, :], op0=ALU.mult, op1=ALU.add),
                                         [sbk.r, btile.r], [tmp.r])
                                    P.op("act", "activation", dict(out=pt[:, j * 128:(j + 1) * 128], in_=tmp[:, :], func=AF.Exp),
                                         [tmp.r], [pt.rs[j]])
                                    col = (j + 1) * 128
                            if col < n:
                                P.op("act", "activation", dict(out=pt[:, col:n], in_=sbk[:, col:n], func=AF.Exp,
                                                               bias=rb31[:, hglob:hglob + 1], scale=0.125), [sbk.r, rb31.r],
                                     [pt.rs[j] for j in range(col // 128, nq)])
                        else:
                            for j in range(nq):
                                qt = qt0 + j
                                if qt == kt:
                                    tmp = dtmp[qt % 2]
                                    P.op("dve", "scalar_tensor_tensor", dict(out=tmp[:, :], in0=sbk[:, j * 128:(j + 1) * 128], scalar=0.125,
                                                                             in1=cmask[:, :], op0=ALU.mult, op1=ALU.add),
                                         [sbk.r, cmask.r], [tmp.r])
                                    P.op("act", "activation", dict(out=pt[:, j * 128:(j + 1) * 128], in_=tmp[:, :], func=AF.Exp,
                                                                   bias=fb[:, kt, qt, hglob:hglob + 1], scale=1.0), [tmp.r, fb.r], [pt.rs[j]])
                                else:
                                    P.op("act", "activation", dict(out=pt[:, j * 128:(j + 1) * 128], in_=sbk[:, j * 128:(j + 1) * 128], func=AF.Exp,
                                                                   bias=fb[:, kt, qt, hglob:hglob + 1], scale=0.125), [sbk.r, fb.r], [pt.rs[j]])
                        pend.append((kt, pt, qt0, nq))
                        if len(pend) > 2:
                            pv(*pend.pop(0))
                    while pend:
                        pv(*pend.pop(0))

                for half in range(2):
                    with ExitStack() as s2:
                        dqT = sb(s2, "dqT", [128, 2, S], BF16, nsub=4)
                        dkT = sb(s2, "dkT", [128, 2, S], BF16, nsub=4)
                        dva = sb(s2, "dva", [128, NT, 2, 129], BF16)
                        P.op("pool", "memset", dict(ap=dva[:, :, :, 128:129], constant=1.0), [], [dva.r])
                        w1 = wload(0 + half * 256)
                        w2 = wload(512 + half * 256)
                        proj_fm(w1, 2, hT, dqT, 0)
                        w3 = wload(1024 + half * 256)
                        proj_fm(w2, 2, hT, dkT, 0)
                        proj_tm(w3, hT, dva, 2, 128)
                        for hl in range(2):
                            h = half * 2 + hl
                            for qc in range(4):
                                for m in range(2):
                                    attn_core(dkT, dqT, hl, slice(m * 64, (m + 1) * 64), dva, hl, 128, qc, "d", h, None)
                                    for j in range(4):
                                        acc = pf[j]
                                        sc = ysc[j]
                                        if m == 0:
                                            P.op("dve", "reciprocal", dict(out=sc[:, 0:1], in_=acc[:, 128:129]), [acc.r], [sc.r])
                                            P.op("dve", "tensor_scalar", dict(out=y0[j][:, :], in0=acc[:, 0:128], scalar1=sc[:, 0:1], scalar2=None,
                                                                              op0=ALU.mult), [acc.r, sc.r], [y0[j].r])
                                        else:
                                            P.op("dve", "reciprocal", dict(out=sc[:, 1:2], in_=acc[:, 128:129]), [acc.r], [sc.r])
                                            P.op("dve", "tensor_tensor", dict(out=sc[:, 2:3], in0=sc[:, 1:2], in1=lsc[:, 4:5], op=ALU.mult),
                                                 [sc.r, lsc.r], [sc.r])
                                            P.op("dve", "scalar_tensor_tensor", dict(out=y0[j][:, :], in0=acc[:, 0:128], scalar=sc[:, 2:3],
                                                                                     in1=y0[j][:, :], op0=ALU.mult, op1=ALU.add),
                                                 [acc.r, sc.r, y0[j].r], [y0[j].r])
                                for j in range(4):
                                    sc = ysc[j]
                                    tmp = dtmp[j % 2]
                                    P.op("act", "activation", dict(out=tmp[:, :], in_=y0[j][:, :], func=AF.Square), [y0[j].r], [tmp.r])
                                    P.op("dve", "reduce_sum", dict(out=sc[:, 3:4], in_=tmp[:, :], axis=AX.X), [tmp.r], [sc.r])
                                    P.op("act", "activation", dict(out=sc[:, 4:5], in_=sc[:, 3:4], func=AF.Sqrt, bias=EPS, scale=1.0 / 128),
                                         [sc.r], [sc.r])
                                    P.op("dve", "reciprocal", dict(out=sc[:, 5:6], in_=sc[:, 4:5]), [sc.r], [sc.r])
                                    P.op("dve", "scalar_tensor_tensor", dict(out=ytk[:, j, :], in0=y0[j][:, :], scalar=sc[:, 5:6], in1=sgb[:, :],
                                                                             op0=ALU.mult, op1=ALU.mult), [y0[j].r, sc.r, sgb.r], [ytk.r])
                                    P.op("pe", "transpose", dict(out=pb16[:, j * 128:(j + 1) * 128], in_=ytk[:, j, :], identity=identb[:, :]),
                                         [ytk.r, identb.r], [pb16.r])
                                evac(ydT[:, h, qc * 512:(qc + 1) * 512], pb16[:, 0:512], [pb16.r], [ydT.rs[qc]])
                    P.barrier()

                with ExitStack() as sf:
                    wff = sb(sf, "wff", [128, 8, 8], BF16)
                    zz = sb(sf, "zz", [128, NT, 8], F32)
                    csb = sb(sf, "csb", [128, NT, 8], F32)
                    bcs = sb(sf, "bcs", [128, NT, 8], F32)
                    off = sb(sf, "off", [128, NT + 1, 8], F32)
                    fb = sb(sf, "fb", [128, NT, NT, 8], F32)
                    semf = P.dma_sem()
                    wcast(wff[:, :, :], wcols(3072, 8), semf, wff.r)
                    for tt in range(NT):
                        for kc in range(8):
                            P.op("pe", "matmul", dict(out=pf[6][:, tt * 8:(tt + 1) * 8], lhsT=hT[:, kc, tt * 128:(tt + 1) * 128], rhs=wff[:, kc, :],
                                                      start=(kc == 0), stop=(kc == 7)), [hT.rs[tt // 4], wff.r], [pf[6].r])
                    P.op("dve", "tensor_tensor", dict(out=zz[:, :, :], in0=pf[6][:, 0:128].rearrange("p (a b) -> p a b", b=8),
                                                      in1=bfb[:, :].unsqueeze(1).to_broadcast([128, NT, 8]), op=ALU.add),
                         [pf[6].r, bfb.r], [zz.r])
                    P.op("act", "activation", dict(out=zz[:, :, :], in_=zz[:, :, :], func=AF.Exp, scale=-1.0), [zz.r], [zz.r])
                    P.op("act", "activation", dict(out=zz[:, :, :], in_=zz[:, :, :], func=AF.Ln, bias=1.0, scale=1.0), [zz.r], [zz.r])
                    P.op("pe", "matmul", dict(out=pf[6][:, 0:128], lhsT=tri[:, :], rhs=zz[:, :, :].rearrange("p a b -> p (a b)"),
                                              start=True, stop=True), [tri.r, zz.r], [pf[6].r])
                    P.op("dve", "tensor_copy", dict(out=csb[:, :, :], in_=pf[6][:, 0:128].rearrange("p (a b) -> p a b", b=8)), [pf[6].r], [csb.r])
                    P.op("pe", "matmul", dict(out=pf[6][:, 128:256], lhsT=sel127[:, :], rhs=csb[:, :, :].rearrange("p a b -> p (a b)"),
                                              start=True, stop=True), [sel127.r, csb.r], [pf[6].r])
                    P.op("dve", "tensor_copy", dict(out=bcs[:, :, :], in_=pf[6][:, 128:256].rearrange("p (a b) -> p a b", b=8)), [pf[6].r], [bcs.r])
                    P.op("dve", "memset", dict(ap=off[:, 0, :], constant=0.0), [], [off.r])
                    for tt in range(NT):
                        P.op("dve", "tensor_tensor", dict(out=off[:, tt + 1, :], in0=off[:, tt, :], in1=bcs[:, tt, :], op=ALU.add),
                             [off.r, bcs.r], [off.r])
                    P.op("dve", "tensor_tensor", dict(out=csb[:, :, :], in0=csb[:, :, :], in1=off[:, 0:NT, :], op=ALU.add), [csb.r, off.r], [csb.r])
                    P.op("dve", "tensor_tensor", dict(out=fb[:, :, :, :], in0=csb[:, :, :].unsqueeze(2).to_broadcast([128, NT, NT, 8]),
                                                      in1=off[:, 1:NT + 1, :].unsqueeze(1).to_broadcast([128, NT, NT, 8]), op=ALU.subtract),
                         [csb.r, off.r], [fb.r])
                    for half in range(2):
                        with ExitStack() as s3:
                            fqT = sb(s3, "fqT", [128, 2, S], BF16, nsub=4)
                            fkT = sb(s3, "fkT", [128, 2, S], BF16, nsub=4)
                            fva = sb(s3, "fva", [128, NT, 4, 65], BF16)
                            P.op("pool", "memset", dict(ap=fva[:, :, :, 64:65], constant=1.0), [], [fva.r])
                            w1 = wload(1536 + half * 256)
                            w2 = wload(2048 + half * 256)
                            proj_fm(w1, 2, hT, fqT, 0)
                            w3 = wload(2560 + half * 256)
                            proj_fm(w2, 2, hT, fkT, 0)
                            proj_tm(w3, hT, fva, 4, 64)
                            for pair in range(2):
                                for qc in range(4):
                                    for hp in range(2):
                                        hl = pair * 2 + hp
                                        h = half * 4 + hl
                                        attn_core(fkT, fqT, pair, slice(hp * 64, hp * 64 + 64), fva, hl, 64, qc, "f", h, fb)
                                        for j in range(4):
                                            acc = pf[j]
                                            sc = ysc[j]
                                            P.op("dve", "reciprocal", dict(out=sc[:, 0:1], in_=acc[:, 64:65]), [acc.r], [sc.r])
                                            P.op("dve", "tensor_scalar", dict(out=ytk[:, j, hp * 64:hp * 64 + 64], in0=acc[:, 0:64], scalar1=sc[:, 0:1],
                                                                              scalar2=None, op0=ALU.mult), [acc.r, sc.r], [ytk.r])
                                    for j in range(4):
                                        P.op("pe", "transpose", dict(out=pb16[:, j * 128:(j + 1) * 128], in_=ytk[:, j, :], identity=identb[:, :]),
                                             [ytk.r, identb.r], [pb16.r])
                                    evac(yfT[:, half * 2 + pair, qc * 512:(qc + 1) * 512], pb16[:, 0:512], [pb16.r], [yfT.rs[qc]])
                        P.barrier()

                P.barrier()
                sx.close()
                mT = sb(sa, "mT", [128, 8, S], BF16, nsub=4)
                with ExitStack() as s4:
                    wga = [sb(s4, "wga%d" % i, [128, 8, 128], BF16) for i in range(2)]
                    wgb = [sb(s4, "wgb%d" % i, [128, 8, 128], BF16) for i in range(2)]
                    wdo = [sb(s4, "wdo%d" % i, [128, 4, 128], BF16) for i in range(2)]
                    wfo = [sb(s4, "wfo%d" % i, [128, 4, 128], BF16) for i in range(2)]
                    semg4 = [P.dma_sem() for _ in range(2)]
                    sga = [sb(s4, "sga%d" % i, [128, 512], F32) for i in range(2)]
                    sgbb = [sb(s4, "sgbb%d" % i, [128, 512], F32) for i in range(2)]
                    m1 = [sb(s4, "m1_%d" % i, [128, 512], F32) for i in range(2)]

                    def load_c(c):
                        i = c % 2
                        wcast(wga[i][:, :, :], wcols(3080 + c * 128, 128), semg4[i], wga[i].r)
                        wcast(wgb[i][:, :, :], wcols(4104 + c * 128, 128), semg4[i], wgb[i].r)
                        wcast(wdo[i][:, :, :], wdo_d[layer, :, c * 128:(c + 1) * 128].rearrange("(kc p) n -> p kc n", p=128), semg4[i], wdo[i].r)
                        wcast(wfo[i][:, :, :], wfo_d[layer, :, c * 128:(c + 1) * 128].rearrange("(kc p) n -> p kc n", p=128), semg4[i], wfo[i].r)
                        for t_ in (wga[i], wgb[i], wdo[i], wfo[i]):
                            t_.r.w = (semg4[i], P.cnt[semg4[i]])
                    load_c(0)
                    it = 0
                    for c in range(8):
                        if c + 1 < 8:
                            load_c(c + 1)
                        i = c % 2
                        for tb in range(4):
                            k = it % 2
                            it += 1
                            tsl = slice(tb * 512, (tb + 1) * 512)
                            for kc in range(8):
                                P.op("pe", "matmul", dict(out=pf[4][:, :], lhsT=wga[i][:, kc, :], rhs=hT[:, kc, tsl], start=(kc == 0), stop=(kc == 7)),
                                     [wga[i].r, hT.rs[tb]], [pf[4].r])
                            P.op("act", "activation", dict(out=sga[k][:, :], in_=pf[4][:, :], func=AF.Sigmoid), [pf[4].r], [sga[k].r])
                            for kc in range(8):
                                P.op("pe", "matmul", dict(out=pf[5][:, :], lhsT=wgb[i][:, kc, :], rhs=hT[:, kc, tsl], start=(kc == 0), stop=(kc == 7)),
                                     [wgb[i].r, hT.rs[tb]], [pf[5].r])
                            P.op("act", "activation", dict(out=sgbb[k][:, :], in_=pf[5][:, :], func=AF.Sigmoid), [pf[5].r], [sgbb[k].r])
                            bd = pf[0 + 2 * k]
                            bf_ = pf[1 + 2 * k]
                            for hh in range(4):
                                P.op("pe", "matmul", dict(out=bd[:, :], lhsT=wdo[i][:, hh, :], rhs=ydT[:, hh, tsl], start=(hh == 0), stop=(hh == 3)),
                                     [wdo[i].r, ydT.rs[tb]], [bd.r])
                            for hh in range(4):
                                P.op("pe", "matmul", dict(out=bf_[:, :], lhsT=wfo[i][:, hh, :], rhs=yfT[:, hh, tsl], start=(hh == 0), stop=(hh == 3)),
                                     [wfo[i].r, yfT.rs[tb]], [bf_.r])
                            P.op("dve", "tensor_tensor", dict(out=m1[k][:, :], in0=bd[:, :], in1=sga[k][:, :], op=ALU.mult), [bd.r, sga[k].r], [m1[k].r])
                            P.op("dve", "tensor_tensor", dict(out=sgbb[k][:, :], in0=bf_[:, :], in1=sgbb[k][:, :], op=ALU.mult), [bf_.r, sgbb[k].r], [sgbb[k].r])
                            P.op("dve", "tensor_tensor", dict(out=mT[:, c, tsl], in0=m1[k][:, :], in1=sgbb[k][:, :], op=ALU.add),
                                 [m1[k].r, sgbb[k].r], [mT.rs[tb]])
                P.barrier()
                with ExitStack() as s5:
                    wo = sb(s5, "wo", [128, 8, D], BF16)
                    semo5 = P.dma_sem()
                    wsrc = wout_d[layer].rearrange("(kc p) n -> p kc n", p=128)
                    for q4 in range(4):
                        wcast(wo[:, 2 * q4:2 * q4 + 2, :], wsrc[:, 2 * q4:2 * q4 + 2, :], semo5, wo.r)
                    wo.r.w = (semo5, P.cnt[semo5])
                    for tt in range(NT):
                        for hf in range(2):
                            bank = pf[(2 * tt + hf) % 4]
                            for kc in range(8):
                                P.op("pe", "matmul", dict(out=bank[:, :], lhsT=mT[:, kc, tt * 128:(tt + 1) * 128], rhs=wo[:, kc, hf * 512:(hf + 1) * 512],
                                                          start=(kc == 0), stop=(kc == 7)), [mT.rs[tt // 4], wo.r], [bank.r])
                            P.op("dve", "tensor_tensor", dict(out=xres[:, tt, hf * 512:(hf + 1) * 512], in0=xres[:, tt, hf * 512:(hf + 1) * 512],
                                                              in1=bank[:, :], op=ALU.add), [xres.rs[tt], bank.r], [xres.rs[tt]])
            P.barrier()

        def phase_peer(layer):
            with ExitStack() as sp_:
                h2T = sb(sp_, "h2T", [128, 8, S], BF16, nsub=4)
                load_gbc(n2_d[layer:layer + 1, :])
                with ExitStack() as sh:
                    hn = [sb(sh, "hn%d" % i, [128, D], BF16) for i in range(2)]
                    norm_T(h2T, hn)
                P.barrier()
                with ExitStack() as s1:
                    KTf = sb(s1, "KTf", [128, 2, 128], F32)
                    KTc = sb(s1, "KTc", [128, 2, 128], BF16)
                    KT = sb(s1, "KT", [128, 2, 128], BF16)
                    qTb = sb(s1, "qTb", [128, 16, 512], BF16)
                    wqb = sb(s1, "wqb", [128, 8, 512], BF16)
                    s_sb = sb(s1, "s_sb", [128, 2048], F32)
                    wk = sb(s1, "wk", [128, 2048], F32, nsub=16)
                    vt = sb(s1, "vt", [128, 256], F32, nsub=32)
                    best = sb(s1, "best", [128, 128], F32, nsub=16)
                    eb = sb(s1, "eb", [128, 128], F32)
                    ev0 = sb(s1, "ev0", [128, 8, 16], F32)
                    zs = sb(s1, "zs", [128, 16], F32)
                    tm = sb(s1, "tm", [128, 3, 128], F32)
                    tT = sb(s1, "tT", [128, 3, 512], F32)
                    e1 = [sb(s1, "e1_%d" % i, [128, 4, 128], F32) for i in range(2)]
                    Fb = [sb(s1, "Fb%d" % i, [128, 4, 128], BF16, nsub=4) for i in range(2)]
                    E0b = [sb(s1, "E0b%d" % i, [128, 4, 128], BF16, nsub=4) for i in range(2)]
                    stg = [sb(s1, "stg%d" % i, [128, 128, 64], BF16) for i in range(2)]
                    qrep = [sb(s1, "qrep%d" % i, [128, 2, 4, 128], BF16, nsub=2) for i in range(2)]
                    semk = P.dma_sem()
                    semq = P.dma_sem()
                    semst = [P.dma_sem() for _ in range(2)]
                    P.dma("sp", KTf[:, :, :], sk_d[layer].rearrange("p n d -> n p d"), semk, writes=[KTf.r])
                    P.op("dve", "tensor_copy", dict(out=KTc[:, :, :], in_=KTf[:, :, :]), [KTf.r], [KTc.r])
                    for p in range(2):
                        P.op("pe", "transpose", dict(out=pb16[:, p * 128:(p + 1) * 128], in_=KTc[:, p, :], identity=identb[:, :]),
                             [KTc.r, identb.r], [pb16.r])
                    evac(KT[:, :, :], pb16[:, 0:256].rearrange("p (a b) -> p a b", b=128), [pb16.r], [KT.r])
                    sv = s_sb[:, :].rearrange("p (o n) -> p o n", n=128)
                    wv = wk[:, :].rearrange("p (o n) -> p o n", n=128)
                    cv4 = s_sb[:, :].rearrange("p (h a b) -> p h a b", a=16, b=16)
                    cv = s_sb[:, :].rearrange("p (h c) -> p h c", c=256)
                    wv2 = wk[:, :].rearrange("p (h c) -> p h c", c=256)
                    vv = vt[:, :].rearrange("p (h q a) -> p h q a", q=2, a=16)
                    bv = best[:, :].rearrange("p (h a) -> p h a", a=16)
                    S0 = [pf[0], pf[1]]
                    S1 = [pf[2], pf[3]]
                    GT = [pf[4], pf[5]]
                    for tb in range(4):
                        tsl = slice(tb * 512, (tb + 1) * 512)
                        for g in range(4):
                            wcast(wqb[:, :, :], wq_d[layer, :, g * 512:(g + 1) * 512].rearrange("(kc p) c -> p kc c", p=128), semq, wqb.r)
                            for oc in range(4):
                                bank = pf[4 + (oc % 2)]
                                for kc in range(8):
                                    P.op("pe", "matmul", dict(out=bank[:, :], lhsT=wqb[:, kc, oc * 128:(oc + 1) * 128], rhs=h2T[:, kc, tsl],
                                                              start=(kc == 0), stop=(kc == 7)), [wqb.r, h2T.rs[tb]], [bank.r])
                                evac(qTb[:, g * 4 + oc, :], bank[:, :], [bank.r], [qTb.r])
                        for ti in range(4):
                            tcol = slice(ti * 128, (ti + 1) * 128)
                            for oc in range(16):
                                P.op("pe", "matmul", dict(out=pf[oc // 4][:, (oc % 4) * 128:(oc % 4 + 1) * 128], lhsT=qTb[:, oc, tcol], rhs=KT[:, oc % 2, :],
                                                          start=True, stop=True), [qTb.r, KT.r], [pf[oc // 4].r])
                            for b4 in range(4):
                                P.op("act", "copy", dict(out=s_sb[:, b4 * 512:(b4 + 1) * 512], in_=pf[b4][:, :]), [pf[b4].r], [s_sb.r])
                            for oc in range(16):
                                P.op("dve", "max", dict(out=vt[:, oc * 16:oc * 16 + 8], in_=sv[:, oc, :]), [s_sb.r], [vt.rs[oc]])
                            for oc in range(16):
                                P.op("dve", "match_replace", dict(out=wv[:, oc, :], in_to_replace=vt[:, oc * 16:oc * 16 + 8], in_values=sv[:, oc, :],
                                                                  imm_value=-1e30), [vt.rs[oc], s_sb.r], [wk.rs[oc]])
                            for oc in range(16):
                                P.op("dve", "max", dict(out=vt[:, oc * 16 + 8:oc * 16 + 16], in_=wv[:, oc, :]), [wk.rs[oc]], [vt.rs[16 + oc]])
                            P.op("dve", "tensor_tensor", dict(out=cv4, in0=vv[:, :, 0, :].unsqueeze(3).to_broadcast([128, 8, 16, 16]),
                                                              in1=vv[:, :, 1, :].unsqueeze(2).to_broadcast([128, 8, 16, 16]), op=ALU.add),
                                 vt.rs, [s_sb.r])
                            for h in range(8):
                                P.op("dve", "max", dict(out=best[:, h * 16:h * 16 + 8], in_=cv[:, h, :]), [s_sb.r], [best.rs[h]])
                            for h in range(8):
                                P.op("dve", "match_replace", dict(out=wv2[:, h, :], in_to_replace=best[:, h * 16:h * 16 + 8], in_values=cv[:, h, :],
                                                                  imm_value=-1e30), [best.rs[h], s_sb.r], [wk.rs[2 * h], wk.rs[2 * h + 1]])
                            for h in range(8):
                                P.op("dve", "max", dict(out=best[:, h * 16 + 8:h * 16 + 16], in_=wv2[:, h, :]), [wk.rs[2 * h], wk.rs[2 * h + 1]], [best.rs[8 + h]])
                            P.op("act", "activation", dict(out=eb[:, :], in_=best[:, :], func=AF.Exp), best.rs, [eb.r])
                            P.op("dve", "reduce_sum", dict(out=zs[:, 0:8], in_=eb[:, :].rearrange("p (h a) -> p h a", a=16), axis=AX.X), [eb.r], [zs.r])
                            P.op("dve", "reciprocal", dict(out=zs[:, 8:16], in_=zs[:, 0:8]), [zs.r], [zs.r])
                            P.op("act", "activation", dict(out=ev0[:, :, :], in_=vv[:, :, 0, :], func=AF.Exp), vt.rs, [ev0.r])
                            P.op("dve", "tensor_tensor", dict(out=tm[:, 0, :].rearrange("p (h a) -> p h a", a=16), in0=bv[:, :, 15:16].to_broadcast([128, 8, 16]),
                                                              in1=vv[:, :, 0, :], op=ALU.subtract), best.rs + vt.rs, [tm.r])
                            P.op("dve", "tensor_copy", dict(out=tm[:, 1, :].rearrange("p (h a) -> p h a", a=16), in_=vv[:, :, 0, :]), vt.rs, [tm.r])
                            P.op("dve", "tensor_tensor", dict(out=tm[:, 2, :].rearrange("p (h a) -> p h a", a=16), in0=ev0[:, :, :],
                                                              in1=zs[:, 8:16].unsqueeze(2).to_broadcast([128, 8, 16]), op=ALU.mult), [ev0.r, zs.r], [tm.r])
                            for k3 in range(3):
                                P.op("pe", "transpose", dict(out=pf[6][:, k3 * 128:(k3 + 1) * 128], in_=tm[:, k3, :], identity=identf[:, :]),
                                     [tm.r, identf.r], [pf[6].r])
                            evac(tT[:, :, tcol], pf[6][:, 0:384].rearrange("p (a b) -> p a b", b=128), [pf[6].r], [tT.r])

                        def stA(nb):
                            k = nb % 2
                            for p in range(2):
                                P.op("act" if p == 0 else "dve", "copy" if p == 0 else "tensor_copy",
                                     dict(out=qrep[k][:, p, :, :].rearrange("d t (h a) -> d t h a", a=16),
                                          in_=qTb[:, p:16:2, nb * 4:nb * 4 + 4].rearrange("d h t -> d t h").unsqueeze(3).to_broadcast([128, 4, 8, 16])),
                                     [qTb.r], [qrep[k].rs[p]])

                        def stB(nb):
                            k = nb % 2
                            for u in range(4):
                                us = slice(u * 128, (u + 1) * 128)
                                P.op("pe", "matmul", dict(out=S0[k][:, us], lhsT=qrep[k][:, 0, u, :], rhs=KT[:, 0, :],
                                                          start=True, stop=True), [qrep[k].rs[0], KT.r], [S0[k].r])
                                P.op("pe", "matmul", dict(out=S1[k][:, us], lhsT=qrep[k][:, 1, u, :], rhs=KT[:, 1, :],
                                                          start=True, stop=True), [qrep[k].rs[1], KT.r], [S1[k].r])
                            P.op("act", "activation", dict(out=e1[k][:, :, :], in_=S1[k][:, :].rearrange("p (a b) -> p a b", b=128), func=AF.Exp),
                                 [S1[k].r], [e1[k].r])

                        def stC(nb):
                            k = nb % 2
                            for u in range(4):
                                tl = nb * 4 + u
                                us = slice(u * 128, (u + 1) * 128)
                                P.op("dve", "scalar_tensor_tensor", dict(out=Fb[k][:, u, :], in0=S1[k][:, us], scalar=tT[:, 0, tl:tl + 1], in1=e1[k][:, u, :],
                                                                         op0=ALU.is_ge, op1=ALU.mult), [S1[k].r, tT.r, e1[k].r], [Fb[k].rs[u]])
                                P.op("dve", "tensor_scalar", dict(out=E0b[k][:, u, :], in0=S0[k][:, us], scalar1=tT[:, 1, tl:tl + 1], scalar2=tT[:, 2, tl:tl + 1],
                                                                  op0=ALU.is_equal, op1=ALU.mult), [S0[k].r, tT.r], [E0b[k].rs[u]])

                        def stD(nb):
                            k = nb % 2
                            for u in range(4):
                                us = slice(u * 128, (u + 1) * 128)
                                P.op("pe", "matmul", dict(out=GT[k][:, us], lhsT=Fb[k][:, u, :], rhs=E0b[k][:, u, :], start=True, stop=True),
                                     [Fb[k].rs[u], E0b[k].rs[u]], [GT[k].r])
                            tg = tb * 512 + nb * 4
                            ss = (tg // 64) % 2
                            t64 = tg % 64
                            P.op("act", "copy", dict(out=stg[ss][:, :, t64:t64 + 4].rearrange("j i t -> j t i"),
                                                     in_=GT[k][:, :].rearrange("p (a b) -> p a b", b=128)), [GT[k].r], [stg[ss].r])
                            if t64 + 4 == 64:
                                t0 = tg + 4 - 64
                                for i4 in range(4):
                                    P.dma("sp", gscr[i4 * 32:(i4 + 1) * 32, :, t0:t0 + 64].rearrange("i j t -> j i t"), stg[ss][:, i4 * 32:(i4 + 1) * 32, :],
                                          semst[ss], reads=[stg[ss].r])

                        NB = 128
                        for it in range(-2, NB + 1):
                            if 0 <= it + 2 < NB:
                                stA(it + 2)
                            if 0 <= it + 1 < NB:
                                stB(it + 1)
                            if 0 <= it < NB:
                                stC(it)
                            if 0 <= it - 1 < NB:
                                stD(it - 1)
                P.barrier()
                with ExitStack() as s2:
                    Ub = [sb(s2, "Ub%d" % i, [128, 4, D], BF16) for i in range(2)]
                    Vb = [sb(s2, "Vb%d" % i, [128, 4, D], BF16) for i in range(2)]
                    UT = [sb(s2, "UT%d" % i, [128, 4, 8, 128], BF16) for i in range(2)]
                    Gb = [sb(s2, "Gb%d" % i, [128, 4, 256], BF16) for i in range(2)]
                    gl = [sb(s2, "gl%d" % i, [128, 256], BF16) for i in range(3)]
                    PTb = [sb(s2, "PTb%d" % i, [128, 256], BF16) for i in range(3)]
                    semU = [P.dma_sem() for _ in range(2)]
                    semV = [P.dma_sem() for _ in range(2)]
                    semG = [P.dma_sem() for _ in range(2)]
                    NEG_ = 32

                    def load_eg(eg):
                        i = eg % 2
                        wcast(Ub[i][:, :, :], eu_d[layer, eg * 512:(eg + 1) * 512, :].rearrange("(c p) d -> p c d", p=128), semU[i], Ub[i].r)
                        wcast(Vb[i][:, :, :], ev_d[layer, eg * 512:(eg + 1) * 512, :].rearrange("(c p) d -> p c d", p=128), semV[i], Vb[i].r)

                    def prep_eg(eg):
                        i = eg % 2
                        for ci in range(4):
                            for dc in range(8):
                                P.op("pe", "transpose", dict(out=pb16[:, dc * 128:(dc + 1) * 128], in_=Ub[i][:, ci, dc * 128:(dc + 1) * 128],
                                                             identity=identb[:, :]), [Ub[i].r, identb.r], [pb16.r])
                            evac(UT[i][:, ci, :, :], pb16[:, :].rearrange("p (a b) -> p a b", b=128), [pb16.r], [UT[i].r])

                    gcount = [0]

                    def load_g(eg, tb):
                        i = gcount[0] % 2
                        gcount[0] += 1
                        P.dma("sp", Gb[i][:, :, :], gscr[eg * 4:(eg + 1) * 4, :, tb * 256:(tb + 1) * 256].rearrange("i j t -> j i t"), semG[i], writes=[Gb[i].r])
                        return Gb[i]

                    def pv_stage(item):
                        eg, tb, ci, n, gbuf = item
                        i = eg % 2
                        for t2 in range(2):
                            for hf in range(2):
                                acc = pf[t2 * 2 + hf]
                                P.op("pe", "matmul", dict(out=acc[:, :], lhsT=PTb[n % 3][:, t2 * 128:(t2 + 1) * 128], rhs=Vb[i][:, ci, hf * 512:(hf + 1) * 512],
                                                          start=(ci == 0), stop=(ci == 3)), [PTb[n % 3].r, Vb[i].r], [acc.r])
                        if ci == 3:
                            for t2 in range(2):
                                tt = tb * 2 + t2
                                for hf in range(2):
                                    acc = pf[t2 * 2 + hf]
                                    P.op("dve", "tensor_tensor", dict(out=xres[:, tt, hf * 512:(hf + 1) * 512], in0=xres[:, tt, hf * 512:(hf + 1) * 512],
                                                                      in1=acc[:, :], op=ALU.add), [xres.rs[tt], acc.r], [xres.rs[tt]])

                    load_eg(0)
                    pendq = []
                    n = 0
                    gnext = load_g(0, 0)
                    for eg in range(NEG_):
                        if eg + 1 < NEG_:
                            load_eg(eg + 1)
                        prep_eg(eg)
                        i = eg % 2
                        for tb in range(8):
                            gbuf = gnext
                            if tb + 1 < 8:
                                gnext = load_g(eg, tb + 1)
                            elif eg + 1 < NEG_:
                                gnext = load_g(eg + 1, 0)
                            for ci in range(4):
                                bank = pf[4 + (n % 3)]
                                for dc in range(8):
                                    P.op("pe", "matmul", dict(out=bank[:, 0:256], lhsT=UT[i][:, ci, dc, :], rhs=h2T[:, dc, tb * 256:(tb + 1) * 256],
                                                              start=(dc == 0), stop=(dc == 7)), [UT[i].r, h2T.rs[tb // 2]], [bank.r])
                                P.op("act", "activation", dict(out=gl[n % 3][:, :], in_=bank[:, 0:256], func=AF.Gelu), [bank.r], [gl[n % 3].r])
                                P.op("dve", "tensor_tensor", dict(out=PTb[n % 3][:, :], in0=gl[n % 3][:, :], in1=gbuf[:, ci, :], op=ALU.mult),
                                     [gl[n % 3].r, gbuf.r], [PTb[n % 3].r])
                                pendq.append((eg, tb, ci, n, gbuf))
                                if len(pendq) > 2:
                                    pv_stage(pendq.pop(0))
                                n += 1
                        while pendq:
                            pv_stage(pendq.pop(0))
            P.barrier()

        for layer in range(depth):
            phase_attention(layer)
            if debug == "A" and layer == depth - 1:
                dump_x(dbg_d)
                break
            phase_peer(layer)
            if debug == "P" and layer == depth - 1:
                dump_x(dbg_d)
        with ExitStack() as sfin:
            ostg = [sb(sfin, "ostg%d" % i, [128, D], F32) for i in range(2)]
            semo = [P.dma_sem() for _ in range(2)]
            load_gbc(nf_d[0:1, :])
            for tt in range(NT):
                o = ostg[tt % 2]
                rms_tile(tt, o[:, :], o.r)
                P.dma("sp", out_d[tt * 128:(tt + 1) * 128, :], o[:, :], semo[tt % 2], reads=[o.r])
            P.barrier()
        P.emit()
    return nc


_CONSTS = None


def _consts():
    global _CONSTS
    if _CONSTS is None:
        p = np.arange(128)
        cm = np.where(p[None, :] >= p[:, None], 0.0, NEG).astype(np.float32)
        ident = np.eye(128, dtype=np.float32)
        s127 = np.zeros((128, 128), np.float32)
        s127[127, :] = 1.0
        bidx = np.stack([t5_bucket_np(dl * 128 + p[None, :] - p[:, None]) for dl in range(2)], 0)
        _CONSTS = (cm, ident, s127, bidx)
    return _CONSTS


def kernel(**inputs):
    cm, ident, s127, bidx = _consts()
    f = lambda a: np.ascontiguousarray(np.asarray(a, dtype=np.float32))
    rel_bias = f(inputs["rel_bias"])
    bt = np.ascontiguousarray(np.transpose(rel_bias[bidx], (1, 0, 3, 2)))
    shared = {k: f(inputs[k]) for k in ("norm1_g", "w_in", "b_forget", "diff_lambda", "diff_subln_g", "w_diff_o", "w_fox_o",
                                        "w_out", "norm2_g", "w_query", "sub_keys", "expert_u", "expert_v")}
    shared["rel_bias"] = rel_bias
    shared["final_norm_g"] = f(inputs["final_norm_g"]).reshape(1, D)
    shared["bias_tiles"] = bt
    shared["cmask"] = cm
    shared["ident"] = ident
    shared["sel127"] = s127
    x = f(inputs["x"])
    nb = x.shape[0]
    nc = build_program()
    in_maps = []
    for b in range(nb):
        m = dict(shared)
        m["x"] = np.ascontiguousarray(x[b])
        in_maps.append(m)
    res = run_bass_kernel_spmd(nc, in_maps, core_ids=list(range(nb)))
    return np.stack([np.asarray(r["out"], dtype=np.float32) for r in res.results], axis=0)
```

```python
import math
import numpy as np
from contextlib import ExitStack
import concourse.bass as bass
import concourse.mybir as mybir
from concourse.bass_utils import run_bass_kernel_spmd

F32 = mybir.dt.float32
BF16 = mybir.dt.bfloat16
AF = mybir.ActivationFunctionType
ALU = mybir.AluOpType
AX = mybir.AxisListType

S = 2048
D = 1024
NT = 16
DEPTH = 2
INW = 5128
EPS = 1e-6
NEG = -30000.0
ENGS = ("pe", "act", "dve", "pool", "sp")


class R:
    __slots__ = ("name", "w", "rd")

    def __init__(self, name=""):
        self.name = name
        self.w = None
        self.rd = {}


class Prog:
    def __init__(self, nc, stack):
        self.nc = nc
        self.stack = stack
        self.ops = {e: [] for e in ENGS}
        self.sems = {}
        self.cnt = {}
        self.seen = {e: {} for e in ENGS}
        for e in ENGS:
            self._mksem(e)
        self.n_dma_sem = 0

    CH = 16000

    def _mksem(self, key):
        self.sems[key] = self.stack.enter_context(self.nc.semaphore(str(key)))
        self.cnt[key] = 0

    def _semval(self, k, v):
        if k in ENGS:
            c = (v - 1) // self.CH
            key = (k, c)
            if key not in self.sems:
                self.sems[key] = self.stack.enter_context(self.nc.semaphore("%s_%d" % (k, c)))
            return self.sems[key], (v - 1) % self.CH + 1
        return self.sems[k], v

    def dma_sem(self):
        key = "dma%d" % self.n_dma_sem
        self.n_dma_sem += 1
        self._mksem(key)
        return key

    def _deps(self, eng, reads, writes):
        deps = {}

        def add(d):
            if d is None:
                return
            k, v = d
            if deps.get(k, 0) < v:
                deps[k] = v
        for r in reads:
            add(r.w)
        for r in writes:
            add(r.w)
            for k, v in r.rd.items():
                add((k, v))
        waits = []
        for k, v in deps.items():
            if k == eng and eng == "pe":
                continue
            if self.seen[eng].get(k, 0) >= v:
                continue
            self.seen[eng][k] = v
            waits.append(self._semval(k, v))
        return waits

    def op(self, eng, name, kw, reads=(), writes=()):
        wl = self._deps(eng, reads, writes)
        self.cnt[eng] += 1
        seq = self.cnt[eng]
        self.ops[eng].append((wl, name, kw, self._semval(eng, seq)[0], 1))
        for r in writes:
            r.w = (eng, seq)
            r.rd = {}
        for r in reads:
            r.rd[eng] = seq

    def dma(self, eng, out, in_, semkey, reads=(), writes=()):
        wl = self._deps(eng, reads, writes)
        self.cnt[semkey] += 16
        val = self.cnt[semkey]
        self.ops[eng].append((wl, "dma_start", dict(out=out, in_=in_), self.sems[semkey], 16))
        for r in writes:
            r.w = (semkey, val)
            r.rd = {}
        for r in reads:
            r.rd[semkey] = val

    def barrier(self):
        for e in ENGS:
            wl = []
            for k in self.cnt:
                if k == e:
                    continue
                v = self.cnt[k]
                if v > self.seen[e].get(k, 0):
                    self.seen[e][k] = v
                    wl.append(self._semval(k, v))
            if wl:
                self.ops[e].append((wl, None, None, None, 0))

    def emit(self):
        nc = self.nc

        def play(e, lst):
            for wl, name, kw, sem, inc in lst:
                for s, v in wl:
                    e.wait_ge(s, v)
                if name is not None:
                    getattr(e, name)(**kw).then_inc(sem, inc)

        with nc.Block() as block:
            @block.tensor
            def _(e):
                play(e, self.ops["pe"])

            @block.scalar
            def _(e):
                play(e, self.ops["act"])

            @block.vector
            def _(e):
                play(e, self.ops["dve"])

            @block.gpsimd
            def _(e):
                play(e, self.ops["pool"])

            @block.sync
            def _(e):
                play(e, self.ops["sp"])


class T:
    def __init__(self, t, nsub=0, name=""):
        self.t = t
        self.r = R(name)
        self.rs = [R("%s%d" % (name, i)) for i in range(nsub)]

    def __getitem__(self, k):
        return self.t[k]


def t5_bucket_np(dist):
    n = np.maximum(dist, 0)
    me = 16
    nf = np.maximum(n, 1).astype(np.float32)
    large = me + (np.log(nf / me) / math.log(128 / me) * (32 - me)).astype(np.int32)
    large = np.minimum(large, 31)
    return np.where(n < me, n, large)


def build_program(debug=None, depth=DEPTH):
    nc = bass.Bass("TRN2", target_bir_lowering=False)
    dr = {}

    def din(name, shape):
        dr[name] = nc.dram_tensor(name, list(shape), F32, kind="ExternalInput").ap()
        return dr[name]

    x_d = din("x", [S, D])
    n1_d = din("norm1_g", [DEPTH, D])
    win_d = din("w_in", [DEPTH, D, INW])
    bf_d = din("b_forget", [DEPTH, 8])
    lam_d = din("diff_lambda", [DEPTH, 4, 64])
    sg_d = din("diff_subln_g", [DEPTH, 128])
    wdo_d = din("w_diff_o", [DEPTH, 512, D])
    wfo_d = din("w_fox_o", [DEPTH, 512, D])
    wout_d = din("w_out", [DEPTH, D, D])
    n2_d = din("norm2_g", [DEPTH, D])
    wq_d = din("w_query", [DEPTH, D, 2048])
    sk_d = din("sub_keys", [DEPTH, 2, 128, 128])
    eu_d = din("expert_u", [DEPTH, 16384, D])
    ev_d = din("expert_v", [DEPTH, 16384, D])
    rb_d = din("rel_bias", [32, 4])
    nf_d = din("final_norm_g", [1, D])
    bt_d = din("bias_tiles", [128, 2, 4, 128])
    cm_d = din("cmask", [128, 128])
    id_d = din("ident", [128, 128])
    s127_d = din("sel127", [128, 128])
    out_d = nc.dram_tensor("out", [S, D], F32, kind="ExternalOutput").ap()
    gscr = nc.dram_tensor("gscr", [128, 128, S], BF16).ap()
    dbg_d = None
    if debug is not None:
        dbg_d = nc.dram_tensor("dbg", [S, D], F32, kind="ExternalOutput").ap()

    with ExitStack() as st:
        P = Prog(nc, st)

        uid = [0]

        def sb(stk, name, shape, dt, nsub=0):
            uid[0] += 1
            name = "%s_%d" % (name, uid[0])
            return T(stk.enter_context(nc.sbuf_tensor(name, list(shape), dt)), nsub, name)

        xres = sb(st, "xres", [128, NT, D], F32, nsub=NT)
        pf = [T(st.enter_context(nc.psum_tensor("pf%d" % i, [128, 512], F32)), 0, "pf%d" % i) for i in range(7)]
        pb16 = T(st.enter_context(nc.psum_tensor("pb16", [128, 1024], BF16)), 0, "pb16")
        identf = sb(st, "identf", [128, 128], F32)
        identb = sb(st, "identb", [128, 128], BF16)
        cmask = sb(st, "cmask_sb", [128, 128], F32)
        tri = sb(st, "tri", [128, 128], F32)
        sel127 = sb(st, "sel127_sb", [128, 128], F32)
        btile = sb(st, "btile", [128, 2, 4, 128], F32)
        rb31 = sb(st, "rb31", [128, 4], F32)
        gbc = sb(st, "gbc", [128, D], F32)
        junk = sb(st, "junk", [128, D], F32)
        nsc = [sb(st, "nsc%d" % i, [128, 4], F32) for i in range(2)]
        sem_c = P.dma_sem()
        sem_x = P.dma_sem()
        sem_g = P.dma_sem()

        P.dma("sp", identf[:, :], id_d, sem_c, writes=[identf.r])
        P.dma("sp", cmask[:, :], cm_d, sem_c, writes=[cmask.r])
        P.dma("sp", sel127[:, :], s127_d, sem_c, writes=[sel127.r])
        P.dma("sp", btile[:, :, :, :], bt_d, sem_c, writes=[btile.r])
        P.dma("sp", rb31[:, :], rb_d[31:32, :].partition_broadcast(128).rearrange("p a b -> p (a b)"), sem_c, writes=[rb31.r])
        for r_ in (identf.r, cmask.r, sel127.r, btile.r, rb31.r):
            r_.w = (sem_c, P.cnt[sem_c])
        for tt in range(NT):
            P.dma("sp", xres[:, tt, :], x_d[tt * 128:(tt + 1) * 128, :], sem_x, writes=[xres.rs[tt]])
        for tt in range(NT):
            xres.rs[tt].w = (sem_x, P.cnt[sem_x])
        P.op("dve", "tensor_copy", dict(out=identb[:, :], in_=identf[:, :]), [identf.r], [identb.r])
        P.op("dve", "tensor_scalar", dict(out=tri[:, :], in0=cmask[:, :], scalar1=0.0, scalar2=None, op0=ALU.is_equal),
             [cmask.r], [tri.r])
        for h in range(4):
            P.op("dve", "tensor_tensor", dict(out=btile[:, 0, h, :], in0=btile[:, 0, h, :], in1=cmask[:, :], op=ALU.add),
                 [btile.r, cmask.r], [btile.r])

        evac_flip = [0]

        def evac(out_ap, in_ap, reads, writes, eng=None):
            if eng is None:
                eng = "act" if evac_flip[0] % 2 == 0 else "dve"
                evac_flip[0] += 1
            if eng == "act":
                P.op("act", "copy", dict(out=out_ap, in_=in_ap), reads, writes)
            else:
                P.op("dve", "tensor_copy", dict(out=out_ap, in_=in_ap), reads, writes)

        def load_gbc(src_row_ap):
            P.dma("sp", gbc[:, :], src_row_ap.partition_broadcast(128).rearrange("p a b -> p (a b)"), sem_g, writes=[gbc.r])

        def rms_tile(tt, dst_ap, dst_r):
            sc = nsc[tt % 2]
            P.op("act", "activation", dict(out=junk[:, :], in_=xres[:, tt, :], func=AF.Square), [xres.rs[tt]], [junk.r])
            P.op("dve", "reduce_sum", dict(out=sc[:, 0:1], in_=junk[:, :], axis=AX.X), [junk.r], [sc.r])
            P.op("act", "activation", dict(out=sc[:, 1:2], in_=sc[:, 0:1], func=AF.Sqrt, bias=EPS, scale=1.0 / D), [sc.r], [sc.r])
            P.op("dve", "reciprocal", dict(out=sc[:, 2:3], in_=sc[:, 1:2]), [sc.r], [sc.r])
            P.op("dve", "scalar_tensor_tensor", dict(out=dst_ap, in0=xres[:, tt, :], scalar=sc[:, 2:3], in1=gbc[:, :],
                                                     op0=ALU.mult, op1=ALU.mult), [xres.rs[tt], sc.r, gbc.r], [dst_r])

        def norm_T(hT, hn):
            for tt in range(NT):
                hb = hn[tt % 2]
                rms_tile(tt, hb[:, :], hb.r)
                for dc in range(8):
                    P.op("pe", "transpose", dict(out=pb16[:, dc * 128:(dc + 1) * 128], in_=hb[:, dc * 128:(dc + 1) * 128],
                                                 identity=identb[:, :]), [hb.r, identb.r], [pb16.r])
                evac(hT[:, :, tt * 128:(tt + 1) * 128], pb16[:, :].rearrange("p (a b) -> p a b", b=128),
                     [pb16.r], [hT.rs[tt // 4]])

        def load_w(dst, src_ap, semkey):
            P.dma("pool", dst_ap_full(dst), src_ap, semkey, writes=[dst.r])

        def dst_ap_full(t):
            nd = len(t.t.shape)
            return t[tuple([slice(None)] * nd)]

        def proj_fm(w, ncol_chunks, hT, dst, dst_chunk0, kch=8, rows=128, act_func=None):
            for oc in range(ncol_chunks):
                for tb in range(4):
                    bank = pf[4 + (tb % 2)]
                    for kc in range(kch):
                        P.op("pe", "matmul", dict(out=bank[:, :], lhsT=w[:, kc, oc * 128:(oc + 1) * 128],
                                                  rhs=hT[:, kc, tb * 512:(tb + 1) * 512], start=(kc == 0), stop=(kc == kch - 1)),
                             [w.r, hT.rs[tb]], [bank.r])
                    evac(dst[:, dst_chunk0 + oc, tb * 512:(tb + 1) * 512], bank[:, :], [bank.r], [dst.rs[tb]])

        def wcast(dst_ap, src_ap, semkey, dst_r):
            P.dma("pool", dst_ap, src_ap, semkey, writes=[dst_r])

        def proj_fm(w, nchunk, hT, dst, c0):
            for oc in range(nchunk):
                for tb in range(4):
                    bank = pf[4 + (tb % 2)]
                    for kc in range(8):
                        P.op("pe", "matmul", dict(out=bank[:, :], lhsT=w[:, kc, oc * 128:(oc + 1) * 128],
                                                  rhs=hT[:, kc, tb * 512:(tb + 1) * 512], start=(kc == 0), stop=(kc == 7)),
                             [w.r, hT.rs[tb]], [bank.r])
                    evac(dst[:, c0 + oc, tb * 512:(tb + 1) * 512], bank[:, :], [bank.r], [dst.rs[tb]])

        def proj_tm(w, hT, va, nh, dv):
            for tt in range(NT):
                bank = pf[4 + (tt % 2)]
                for kc in range(8):
                    P.op("pe", "matmul", dict(out=bank[:, 0:256], lhsT=hT[:, kc, tt * 128:(tt + 1) * 128], rhs=w[:, kc, :],
                                              start=(kc == 0), stop=(kc == 7)), [hT.rs[tt // 4], w.r], [bank.r])
                evac(va[:, tt, :, 0:dv], bank[:, 0:256].rearrange("p (a b) -> p a b", b=dv), [bank.r], [va.r])

        def dump_x(dst):
            semd = P.dma_sem()
            for tt in range(NT):
                P.dma("sp", dst[tt * 128:(tt + 1) * 128, :], xres[:, tt, :], semd, reads=[xres.rs[tt]])

        def phase_attention(layer):
            lam_init = 0.8 - 0.6 * math.exp(-0.3 * layer)
            with ExitStack() as sa:
                hT = sb(sa, "hT", [128, 8, S], BF16, nsub=4)
                ydT = sb(sa, "ydT", [128, 4, S], BF16, nsub=4)
                yfT = sb(sa, "yfT", [128, 4, S], BF16, nsub=4)
                lamb = sb(sa, "lamb", [128, 4, 64], F32)
                lsc = sb(sa, "lsc", [128, 8], F32)
                sgb = sb(sa, "sgb", [128, 128], F32)
                bfb = sb(sa, "bfb", [128, 8], F32)
                sx = ExitStack()
                wbuf = [sb(sx, "wbuf%d" % i, [128, 8, 256], BF16) for i in range(2)]
                semw = [P.dma_sem() for _ in range(2)]
                PT = [sb(sx, "PT%d" % i, [128, 512], BF16, nsub=4) for i in range(4)]
                dtmp = [sb(sx, "dtmp%d" % i, [128, 128], F32) for i in range(2)]
                ytk = sb(sx, "ytk", [128, 4, 128], BF16)
                y0 = [sb(sx, "y0_%d" % i, [128, 128], F32) for i in range(4)]
                ysc = [sb(sx, "ysc%d" % i, [128, 8], F32) for i in range(4)]
                sem_s = P.dma_sem()
                wslot = [0]

                def wcols(c0, n):
                    return win_d[layer, :, c0:c0 + n].rearrange("(kc p) c -> p kc c", p=128)

                def wload(c0):
                    i = wslot[0] % 2
                    wslot[0] += 1
                    wcast(wbuf[i][:, :, :], wcols(c0, 256), semw[i], wbuf[i].r)
                    return wbuf[i]

                load_gbc(n1_d[layer:layer + 1, :])
                P.dma("sp", lamb[:, :, :], lam_d[layer].partition_broadcast(128), sem_s, writes=[lamb.r])
                P.dma("sp", sgb[:, :], sg_d[layer:layer + 1, :].partition_broadcast(128).rearrange("p a b -> p (a b)"), sem_s, writes=[sgb.r])
                P.dma("sp", bfb[:, :], bf_d[layer:layer + 1, :].partition_broadcast(128).rearrange("p a b -> p (a b)"), sem_s, writes=[bfb.r])
                for r_ in (lamb.r, sgb.r, bfb.r):
                    r_.w = (sem_s, P.cnt[sem_s])
                with ExitStack() as sh:
                    hn = [sb(sh, "hn%d" % i, [128, D], BF16) for i in range(2)]
                    norm_T(hT, hn)
                P.barrier()

                P.op("dve", "tensor_tensor", dict(out=lamb[:, 0, :], in0=lamb[:, 0, :], in1=lamb[:, 1, :], op=ALU.mult), [lamb.r], [lamb.r])
                P.op("dve", "tensor_tensor", dict(out=lamb[:, 2, :], in0=lamb[:, 2, :], in1=lamb[:, 3, :], op=ALU.mult), [lamb.r], [lamb.r])
                P.op("dve", "reduce_sum", dict(out=lsc[:, 0:1], in_=lamb[:, 0, :], axis=AX.X), [lamb.r], [lsc.r])
                P.op("dve", "reduce_sum", dict(out=lsc[:, 1:2], in_=lamb[:, 2, :], axis=AX.X), [lamb.r], [lsc.r])
                P.op("act", "activation", dict(out=lsc[:, 2:4], in_=lsc[:, 0:2], func=AF.Exp), [lsc.r], [lsc.r])
                P.op("dve", "tensor_tensor", dict(out=lsc[:, 4:5], in0=lsc[:, 3:4], in1=lsc[:, 2:3], op=ALU.subtract), [lsc.r], [lsc.r])
                P.op("dve", "tensor_scalar", dict(out=lsc[:, 4:5], in0=lsc[:, 4:5], scalar1=-lam_init, scalar2=None, op0=ALU.add), [lsc.r], [lsc.r])
                P.op("dve", "tensor_scalar", dict(out=sgb[:, :], in0=sgb[:, :], scalar1=(1.0 - lam_init), scalar2=None, op0=ALU.mult), [sgb.r], [sgb.r])

                sflip = [0]

                def attn_core(kT, qT, ch, rows, va, hloc, dv, qc, kind, hglob, fb):
                    pend = []

                    def pv(kt, pt, qt0, nq):
                        for j in range(nq):
                            qt = qt0 + j
                            acc = pf[qt - 4 * qc]
                            P.op("pe", "matmul", dict(out=acc[:, 0:dv + 1], lhsT=pt[:, j * 128:(j + 1) * 128], rhs=va[:, kt, hloc, :],
                                                      start=(kt == 0), stop=(kt == qt)), [pt.rs[j], va.r], [acc.r])

                    for kt in range(4 * qc + 4):
                        q_lo = max(qc * 512, kt * 128)
                        n = (qc + 1) * 512 - q_lo
                        sbk = pf[4 + (sflip[0] % 3)]
                        pt = PT[sflip[0] % 4]
                        sflip[0] += 1
                        P.op("pe", "matmul", dict(out=sbk[:, 0:n], lhsT=kT[rows, ch, kt * 128:(kt + 1) * 128],
                                                  rhs=qT[rows, ch, q_lo:q_lo + n], start=True, stop=True),
                             [kT.rs[kt // 4], qT.rs[qc]], [sbk.r])
                        qt0 = q_lo // 128
                        nq = n // 128
                        if kind == "d":
                            col = 0
                            for j in range(nq):
                                dl = qt0 + j - kt
                                if dl <= 1:
                                    tmp = dtmp[(qt0 + j) % 2]
                                    P.op("dve", "scalar_tensor_tensor", dict(out=tmp[:, :], in0=sbk[:, j * 128:(j + 1) * 128], scalar=0.125,
                                                                             in1=btile[:, dl, hglob, :], op0=ALU.mult, op1=ALU.add),
                                         [sbk.r, btile.r], [tmp.r])
                                    P.op("act", "activation", dict(out=pt[:, j * 128:(j + 1) * 128], in_=tmp[:, :], func=AF.Exp),
                                         [tmp.r], [pt.rs[j]])
                                    col = (j + 1) * 128
                            if col < n:
                                P.op("act", "activation", dict(out=pt[:, col:n], in_=sbk[:, col:n], func=AF.Exp,
                                                               bias=rb31[:, hglob:hglob + 1], scale=0.125), [sbk.r, rb31.r],
                                     [pt.rs[j] for j in range(col // 128, nq)])
                        else:
                            for j in range(nq):
                                qt = qt0 + j
                                if qt == kt:
                                    tmp = dtmp[qt % 2]
                                    P.op("dve", "scalar_tensor_tensor", dict(out=tmp[:, :], in0=sbk[:, j * 128:(j + 1) * 128], scalar=0.125,
                                                                             in1=cmask[:, :], op0=ALU.mult, op1=ALU.add),
                                         [sbk.r, cmask.r], [tmp.r])
                                    P.op("act", "activation", dict(out=pt[:, j * 128:(j + 1) * 128], in_=tmp[:, :], func=AF.Exp,
                                                                   bias=fb[:, kt, qt, hglob:hglob + 1], scale=1.0), [tmp.r, fb.r], [pt.rs[j]])
                                else:
                                    P.op("act", "activation", dict(out=pt[:, j * 128:(j + 1) * 128], in_=sbk[:, j * 128:(j + 1) * 128], func=AF.Exp,
                                                                   bias=fb[:, kt, qt, hglob:hglob + 1], scale=0.125), [sbk.r, fb.r], [pt.rs[j]])
                        pend.append((kt, pt, qt0, nq))
                        if len(pend) > 2:
                            pv(*pend.pop(0))
                    while pend:
                        pv(*pend.pop(0))

                for half in range(2):
                    with ExitStack() as s2:
                        dqT = sb(s2, "dqT", [128, 2, S], BF16, nsub=4)
                        dkT = sb(s2, "dkT", [128, 2, S], BF16, nsub=4)
                        dva = sb(s2, "dva", [128, NT, 2, 129], BF16)
                        P.op("pool", "memset", dict(ap=dva[:, :, :, 128:129], constant=1.0), [], [dva.r])
                        w1 = wload(0 + half * 256)
                        w2 = wload(512 + half * 256)
                        proj_fm(w1, 2, hT, dqT, 0)
                        w3 = wload(1024 + half * 256)
                        proj_fm(w2, 2, hT, dkT, 0)
                        proj_tm(w3, hT, dva, 2, 128)
                        for hl in range(2):
                            h = half * 2 + hl
                            for qc in range(4):
                                for m in range(2):
                                    attn_core(dkT, dqT, hl, slice(m * 64, (m + 1) * 64), dva, hl, 128, qc, "d", h, None)
                                    for j in range(4):
                                        acc = pf[j]
                                        sc = ysc[j]
                                        if m == 0:
                                            P.op("dve", "reciprocal", dict(out=sc[:, 0:1], in_=acc[:, 128:129]), [acc.r], [sc.r])
                                            P.op("dve", "tensor_scalar", dict(out=y0[j][:, :], in0=acc[:, 0:128], scalar1=sc[:, 0:1], scalar2=None,
                                                                              op0=ALU.mult), [acc.r, sc.r], [y0[j].r])
                                        else:
                                            P.op("dve", "reciprocal", dict(out=sc[:, 1:2], in_=acc[:, 128:129]), [acc.r], [sc.r])
                                            P.op("dve", "tensor_tensor", dict(out=sc[:, 2:3], in0=sc[:, 1:2], in1=lsc[:, 4:5], op=ALU.mult),
                                                 [sc.r, lsc.r], [sc.r])
                                            P.op("dve", "scalar_tensor_tensor", dict(out=y0[j][:, :], in0=acc[:, 0:128], scalar=sc[:, 2:3],
                                                                                     in1=y0[j][:, :], op0=ALU.mult, op1=ALU.add),
                                                 [acc.r, sc.r, y0[j].r], [y0[j].r])
                                for j in range(4):
                                    sc = ysc[j]
                                    tmp = dtmp[j % 2]
                                    P.op("act", "activation", dict(out=tmp[:, :], in_=y0[j][:, :], func=AF.Square), [y0[j].r], [tmp.r])
                                    P.op("dve", "reduce_sum", dict(out=sc[:, 3:4], in_=tmp[:, :], axis=AX.X), [tmp.r], [sc.r])
                                    P.op("act", "activation", dict(out=sc[:, 4:5], in_=sc[:, 3:4], func=AF.Sqrt, bias=EPS, scale=1.0 / 128),
                                         [sc.r], [sc.r])
                                    P.op("dve", "reciprocal", dict(out=sc[:, 5:6], in_=sc[:, 4:5]), [sc.r], [sc.r])
                                    P.op("dve", "scalar_tensor_tensor", dict(out=ytk[:, j, :], in0=y0[j][:, :], scalar=sc[:, 5:6], in1=sgb[:, :],
                                                                             op0=ALU.mult, op1=ALU.mult), [y0[j].r, sc.r, sgb.r], [ytk.r])
                                    P.op("pe", "transpose", dict(out=pb16[:, j * 128:(j + 1) * 128], in_=ytk[:, j, :], identity=identb[:, :]),
                                         [ytk.r, identb.r], [pb16.r])
                                evac(ydT[:, h, qc * 512:(qc + 1) * 512], pb16[:, 0:512], [pb16.r], [ydT.rs[qc]])
                    P.barrier()

                with ExitStack() as sf:
                    wff = sb(sf, "wff", [128, 8, 8], BF16)
                    zz = sb(sf, "zz", [128, NT, 8], F32)
                    csb = sb(sf, "csb", [128, NT, 8], F32)
                    bcs = sb(sf, "bcs", [128, NT, 8], F32)
                    off = sb(sf, "off", [128, NT + 1, 8], F32)
                    fb = sb(sf, "fb", [128, NT, NT, 8], F32)
                    semf = P.dma_sem()
                    wcast(wff[:, :, :], wcols(3072, 8), semf, wff.r)
                    for tt in range(NT):
                        for kc in range(8):
                            P.op("pe", "matmul", dict(out=pf[6][:, tt * 8:(tt + 1) * 8], lhsT=hT[:, kc, tt * 128:(tt + 1) * 128], rhs=wff[:, kc, :],
                                                      start=(kc == 0), stop=(kc == 7)), [hT.rs[tt // 4], wff.r], [pf[6].r])
                    P.op("dve", "tensor_tensor", dict(out=zz[:, :, :], in0=pf[6][:, 0:128].rearrange("p (a b) -> p a b", b=8),
                                                      in1=bfb[:, :].unsqueeze(1).to_broadcast([128, NT, 8]), op=ALU.add),
                         [pf[6].r, bfb.r], [zz.r])
                    P.op("act", "activation", dict(out=zz[:, :, :], in_=zz[:, :, :], func=AF.Exp, scale=-1.0), [zz.r], [zz.r])
                    P.op("act", "activation", dict(out=zz[:, :, :], in_=zz[:, :, :], func=AF.Ln, bias=1.0, scale=1.0), [zz.r], [zz.r])
                    P.op("pe", "matmul", dict(out=pf[6][:, 0:128], lhsT=tri[:, :], rhs=zz[:, :, :].rearrange("p a b -> p (a b)"),
                                              start=True, stop=True), [tri.r, zz.r], [pf[6].r])
                    P.op("dve", "tensor_copy", dict(out=csb[:, :, :], in_=pf[6][:, 0:128].rearrange("p (a b) -> p a b", b=8)), [pf[6].r], [csb.r])
                    P.op("pe", "matmul", dict(out=pf[6][:, 128:256], lhsT=sel127[:, :], rhs=csb[:, :, :].rearrange("p a b -> p (a b)"),
                                              start=True, stop=True), [sel127.r, csb.r], [pf[6].r])
                    P.op("dve", "tensor_copy", dict(out=bcs[:, :, :], in_=pf[6][:, 128:256].rearrange("p (a b) -> p a b", b=8)), [pf[6].r], [bcs.r])
                    P.op("dve", "memset", dict(ap=off[:, 0, :], constant=0.0), [], [off.r])
                    for tt in range(NT):
                        P.op("dve", "tensor_tensor", dict(out=off[:, tt + 1, :], in0=off[:, tt, :], in1=bcs[:, tt, :], op=ALU.add),
                             [off.r, bcs.r], [off.r])
                    P.op("dve", "tensor_tensor", dict(out=csb[:, :, :], in0=csb[:, :, :], in1=off[:, 0:NT, :], op=ALU.add), [csb.r, off.r], [csb.r])
                    P.op("dve", "tensor_tensor", dict(out=fb[:, :, :, :], in0=csb[:, :, :].unsqueeze(2).to_broadcast([128, NT, NT, 8]),
                                                      in1=off[:, 1:NT + 1, :].unsqueeze(1).to_broadcast([128, NT, NT, 8]), op=ALU.subtract),
                         [csb.r, off.r], [fb.r])
                    for half in range(2):
                        with ExitStack() as s3:
                            fqT = sb(s3, "fqT", [128, 2, S], BF16, nsub=4)
                            fkT = sb(s3, "fkT", [128, 2, S], BF16, nsub=4)
                            fva = sb(s3, "fva", [128, NT, 4, 65], BF16)
                            P.op("pool", "memset", dict(ap=fva[:, :, :, 64:65], constant=1.0), [], [fva.r])
                            w1 = wload(1536 + half * 256)
                            w2 = wload(2048 + half * 256)
                            proj_fm(w1, 2, hT, fqT, 0)
                            w3 = wload(2560 + half * 256)
                            proj_fm(w2, 2, hT, fkT, 0)
                            proj_tm(w3, hT, fva, 4, 64)
                            for pair in range(2):
                                for qc in range(4):
                                    for hp in range(2):
                                        hl = pair * 2 + hp
                                        h = half * 4 + hl
                                        attn_core(fkT, fqT, pair, slice(hp * 64, hp * 64 + 64), fva, hl, 64, qc, "f", h, fb)
                                        for j in range(4):
                                            acc = pf[j]
                                            sc = ysc[j]
                                            P.op("dve", "reciprocal", dict(out=sc[:, 0:1], in_=acc[:, 64:65]), [acc.r], [sc.r])
                                            P.op("dve", "tensor_scalar", dict(out=ytk[:, j, hp * 64:hp * 64 + 64], in0=acc[:, 0:64], scalar1=sc[:, 0:1],
                                                                              scalar2=None, op0=ALU.mult), [acc.r, sc.r], [ytk.r])
                                    for j in range(4):
                                        P.op("pe", "transpose", dict(out=pb16[:, j * 128:(j + 1) * 128], in_=ytk[:, j, :], identity=identb[:, :]),
                                             [ytk.r, identb.r], [pb16.r])
                                    evac(yfT[:, half * 2 + pair, qc * 512:(qc + 1) * 512], pb16[:, 0:512], [pb16.r], [yfT.rs[qc]])
                        P.barrier()

                P.barrier()
                sx.close()
                mT = sb(sa, "mT", [128, 8, S], BF16, nsub=4)
                with ExitStack() as s4:
                    wga = [sb(s4, "wga%d" % i, [128, 8, 128], BF16) for i in range(2)]
                    wgb = [sb(s4, "wgb%d" % i, [128, 8, 128], BF16) for i in range(2)]
                    wdo = [sb(s4, "wdo%d" % i, [128, 4, 128], BF16) for i in range(2)]
                    wfo = [sb(s4, "wfo%d" % i, [128, 4, 128], BF16) for i in range(2)]
                    semg4 = [P.dma_sem() for _ in range(2)]
                    sga = [sb(s4, "sga%d" % i, [128, 512], F32) for i in range(2)]
                    sgbb = [sb(s4, "sgbb%d" % i, [128, 512], F32) for i in range(2)]
                    m1 = [sb(s4, "m1_%d" % i, [128, 512], F32) for i in range(2)]

                    def load_c(c):
                        i = c % 2
                        wcast(wga[i][:, :, :], wcols(3080 + c * 128, 128), semg4[i], wga[i].r)
                        wcast(wgb[i][:, :, :], wcols(4104 + c * 128, 128), semg4[i], wgb[i].r)
                        wcast(wdo[i][:, :, :], wdo_d[layer, :, c * 128:(c + 1) * 128].rearrange("(kc p) n -> p kc n", p=128), semg4[i], wdo[i].r)
                        wcast(wfo[i][:, :, :], wfo_d[layer, :, c * 128:(c + 1) * 128].rearrange("(kc p) n -> p kc n", p=128), semg4[i], wfo[i].r)
                        for t_ in (wga[i], wgb[i], wdo[i], wfo[i]):
                            t_.r.w = (semg4[i], P.cnt[semg4[i]])
                    load_c(0)
                    it = 0
                    for c in range(8):
                        if c + 1 < 8:
                            load_c(c + 1)
                        i = c % 2
                        for tb in range(4):
                            k = it % 2
                            it += 1
                            tsl = slice(tb * 512, (tb + 1) * 512)
                            for kc in range(8):
                                P.op("pe", "matmul", dict(out=pf[4][:, :], lhsT=wga[i][:, kc, :], rhs=hT[:, kc, tsl], start=(kc == 0), stop=(kc == 7)),
                                     [wga[i].r, hT.rs[tb]], [pf[4].r])
                            P.op("act", "activation", dict(out=sga[k][:, :], in_=pf[4][:, :], func=AF.Sigmoid), [pf[4].r], [sga[k].r])
                            for kc in range(8):
                                P.op("pe", "matmul", dict(out=pf[5][:, :], lhsT=wgb[i][:, kc, :], rhs=hT[:, kc, tsl], start=(kc == 0), stop=(kc == 7)),
                                     [wgb[i].r, hT.rs[tb]], [pf[5].r])
                            P.op("act", "activation", dict(out=sgbb[k][:, :], in_=pf[5][:, :], func=AF.Sigmoid), [pf[5].r], [sgbb[k].r])
                            bd = pf[0 + 2 * k]
                            bf_ = pf[1 + 2 * k]
                            for hh in range(4):
                                P.op("pe", "matmul", dict(out=bd[:, :], lhsT=wdo[i][:, hh, :], rhs=ydT[:, hh, tsl], start=(hh == 0), stop=(hh == 3)),
                                     [wdo[i].r, ydT.rs[tb]], [bd.r])
                            for hh in range(4):
                                P.op("pe", "matmul", dict(out=bf_[:, :], lhsT=wfo[i][:, hh, :], rhs=yfT[:, hh, tsl], start=(hh == 0), stop=(hh == 3)),
                                     [wfo[i].r, yfT.rs[tb]], [bf_.r])
                            P.op("dve", "tensor_tensor", dict(out=m1[k][:, :], in0=bd[:, :], in1=sga[k][:, :], op=ALU.mult), [bd.r, sga[k].r], [m1[k].r])
                            P.op("dve", "tensor_tensor", dict(out=sgbb[k][:, :], in0=bf_[:, :], in1=sgbb[k][:, :], op=ALU.mult), [bf_.r, sgbb[k].r], [sgbb[k].r])
                            P.op("dve", "tensor_tensor", dict(out=mT[:, c, tsl], in0=m1[k][:, :], in1=sgbb[k][:, :], op=ALU.add),
                                 [m1[k].r, sgbb[k].r], [mT.rs[tb]])
                P.barrier()
                with ExitStack() as s5:
                    wo = sb(s5, "wo", [128, 8, D], BF16)
                    semo5 = P.dma_sem()
                    wsrc = wout_d[layer].rearrange("(kc p) n -> p kc n", p=128)
                    for q4 in range(4):
                        wcast(wo[:, 2 * q4:2 * q4 + 2, :], wsrc[:, 2 * q4:2 * q4 + 2, :], semo5, wo.r)
                    wo.r.w = (semo5, P.cnt[semo5])
                    for tt in range(NT):
                        for hf in range(2):
                            bank = pf[(2 * tt + hf) % 4]
                            for kc in range(8):
                                P.op("pe", "matmul", dict(out=bank[:, :], lhsT=mT[:, kc, tt * 128:(tt + 1) * 128], rhs=wo[:, kc, hf * 512:(hf + 1) * 512],
                                                          start=(kc == 0), stop=(kc == 7)), [mT.rs[tt // 4], wo.r], [bank.r])
                            P.op("dve", "tensor_tensor", dict(out=xres[:, tt, hf * 512:(hf + 1) * 512], in0=xres[:, tt, hf * 512:(hf + 1) * 512],
                                                              in1=bank[:, :], op=ALU.add), [xres.rs[tt], bank.r], [xres.rs[tt]])
            P.barrier()

        def phase_peer(layer):
            with ExitStack() as sp_:
                h2T = sb(sp_, "h2T", [128, 8, S], BF16, nsub=4)
                load_gbc(n2_d[layer:layer + 1, :])
                with ExitStack() as sh:
                    hn = [sb(sh, "hn%d" % i, [128, D], BF16) for i in range(2)]
                    norm_T(h2T, hn)
                P.barrier()
                with ExitStack() as s1:
                    KTf = sb(s1, "KTf", [128, 2, 128], F32)
                    KTc = sb(s1, "KTc", [128, 2, 128], BF16)
                    KT = sb(s1, "KT", [128, 2, 128], BF16)
                    qTb = sb(s1, "qTb", [128, 16, 512], BF16)
                    wqb = sb(s1, "wqb", [128, 8, 512], BF16)
                    s_sb = sb(s1, "s_sb", [128, 2048], F32)
                    wk = sb(s1, "wk", [128, 2048], F32, nsub=16)
                    vt = sb(s1, "vt", [128, 256], F32, nsub=32)
                    best = sb(s1, "best", [128, 128], F32, nsub=16)
                    eb = sb(s1, "eb", [128, 128], F32)
                    ev0 = sb(s1, "ev0", [128, 8, 16], F32)
                    zs = sb(s1, "zs", [128, 16], F32)
                    tm = sb(s1, "tm", [128, 3, 128], F32)
                    tT = sb(s1, "tT", [128, 3, 512], F32)
                    e1 = [sb(s1, "e1_%d" % i, [128, 4, 128], F32) for i in range(2)]
                    Fb = [sb(s1, "Fb%d" % i, [128, 4, 128], BF16, nsub=4) for i in range(2)]
                    E0b = [sb(s1, "E0b%d" % i, [128, 4, 128], BF16, nsub=4) for i in range(2)]
                    stg = [sb(s1, "stg%d" % i, [128, 128, 64], BF16) for i in range(2)]
                    qrep = [sb(s1, "qrep%d" % i, [128, 2, 4, 128], BF16, nsub=2) for i in range(2)]
                    semk = P.dma_sem()
                    semq = P.dma_sem()
                    semst = [P.dma_sem() for _ in range(2)]
                    P.dma("sp", KTf[:, :, :], sk_d[layer].rearrange("p n d -> n p d"), semk, writes=[KTf.r])
                    P.op("dve", "tensor_copy", dict(out=KTc[:, :, :], in_=KTf[:, :, :]), [KTf.r], [KTc.r])
                    for p in range(2):
                        P.op("pe", "transpose", dict(out=pb16[:, p * 128:(p + 1) * 128], in_=KTc[:, p, :], identity=identb[:, :]),
                             [KTc.r, identb.r], [pb16.r])
                    evac(KT[:, :, :], pb16[:, 0:256].rearrange("p (a b) -> p a b", b=128), [pb16.r], [KT.r])
                    sv = s_sb[:, :].rearrange("p (o n) -> p o n", n=128)
                    wv = wk[:, :].rearrange("p (o n) -> p o n", n=128)
                    cv4 = s_sb[:, :].rearrange("p (h a b) -> p h a b", a=16, b=16)
                    cv = s_sb[:, :].rearrange("p (h c) -> p h c", c=256)
                    wv2 = wk[:, :].rearrange("p (h c) -> p h c", c=256)
                    vv = vt[:, :].rearrange("p (h q a) -> p h q a", q=2, a=16)
                    bv = best[:, :].rearrange("p (h a) -> p h a", a=16)
                    S0 = [pf[0], pf[1]]
                    S1 = [pf[2], pf[3]]
                    GT = [pf[4], pf[5]]
                    for tb in range(4):
                        tsl = slice(tb * 512, (tb + 1) * 512)
                        for g in range(4):
                            wcast(wqb[:, :, :], wq_d[layer, :, g * 512:(g + 1) * 512].rearrange("(kc p) c -> p kc c", p=128), semq, wqb.r)
                            for oc in range(4):
                                bank = pf[4 + (oc % 2)]
                                for kc in range(8):
                                    P.op("pe", "matmul", dict(out=bank[:, :], lhsT=wqb[:, kc, oc * 128:(oc + 1) * 128], rhs=h2T[:, kc, tsl],
                                                              start=(kc == 0), stop=(kc == 7)), [wqb.r, h2T.rs[tb]], [bank.r])
                                evac(qTb[:, g * 4 + oc, :], bank[:, :], [bank.r], [qTb.r])
                        for ti in range(4):
                            tcol = slice(ti * 128, (ti + 1) * 128)
                            for oc in range(16):
                                P.op("pe", "matmul", dict(out=pf[oc // 4][:, (oc % 4) * 128:(oc % 4 + 1) * 128], lhsT=qTb[:, oc, tcol], rhs=KT[:, oc % 2, :],
                                                          start=True, stop=True), [qTb.r, KT.r], [pf[oc // 4].r])
                            for b4 in range(4):
                                P.op("act", "copy", dict(out=s_sb[:, b4 * 512:(b4 + 1) * 512], in_=pf[b4][:, :]), [pf[b4].r], [s_sb.r])
                            for oc in range(16):
                                P.op("dve", "max", dict(out=vt[:, oc * 16:oc * 16 + 8], in_=sv[:, oc, :]), [s_sb.r], [vt.rs[oc]])
                            for oc in range(16):
                                P.op("dve", "match_replace", dict(out=wv[:, oc, :], in_to_replace=vt[:, oc * 16:oc * 16 + 8], in_values=sv[:, oc, :],
                                                                  imm_value=-1e30), [vt.rs[oc], s_sb.r], [wk.rs[oc]])
                            for oc in range(16):
                                P.op("dve", "max", dict(out=vt[:, oc * 16 + 8:oc * 16 + 16], in_=wv[:, oc, :]), [wk.rs[oc]], [vt.rs[16 + oc]])
                            P.op("dve", "tensor_tensor", dict(out=cv4, in0=vv[:, :, 0, :].unsqueeze(3).to_broadcast([128, 8, 16, 16]),
                                                              in1=vv[:, :, 1, :].unsqueeze(2).to_broadcast([128, 8, 16, 16]), op=ALU.add),
                                 vt.rs, [s_sb.r])
                            for h in range(8):
                                P.op("dve", "max", dict(out=best[:, h * 16:h * 16 + 8], in_=cv[:, h, :]), [s_sb.r], [best.rs[h]])
                            for h in range(8):
                                P.op("dve", "match_replace", dict(out=wv2[:, h, :], in_to_replace=best[:, h * 16:h * 16 + 8], in_values=cv[:, h, :],
                                                                  imm_value=-1e30), [best.rs[h], s_sb.r], [wk.rs[2 * h], wk.rs[2 * h + 1]])
                            for h in range(8):
                                P.op("dve", "max", dict(out=best[:, h * 16 + 8:h * 16 + 16], in_=wv2[:, h, :]), [wk.rs[2 * h], wk.rs[2 * h + 1]], [best.rs[8 + h]])
                            P.op("act", "activation", dict(out=eb[:, :], in_=best[:, :], func=AF.Exp), best.rs, [eb.r])
                            P.op("dve", "reduce_sum", dict(out=zs[:, 0:8], in_=eb[:, :].rearrange("p (h a) -> p h a", a=16), axis=AX.X), [eb.r], [zs.r])
                            P.op("dve", "reciprocal", dict(out=zs[:, 8:16], in_=zs[:, 0:8]), [zs.r], [zs.r])
                            P.op("act", "activation", dict(out=ev0[:, :, :], in_=vv[:, :, 0, :], func=AF.Exp), vt.rs, [ev0.r])
                            P.op("dve", "tensor_tensor", dict(out=tm[:, 0, :].rearrange("p (h a) -> p h a", a=16), in0=bv[:, :, 15:16].to_broadcast([128, 8, 16]),
                                                              in1=vv[:, :, 0, :], op=ALU.subtract), best.rs + vt.rs, [tm.r])
                            P.op("dve", "tensor_copy", dict(out=tm[:, 1, :].rearrange("p (h a) -> p h a", a=16), in_=vv[:, :, 0, :]), vt.rs, [tm.r])
                            P.op("dve", "tensor_tensor", dict(out=tm[:, 2, :].rearrange("p (h a) -> p h a", a=16), in0=ev0[:, :, :],
                                                              in1=zs[:, 8:16].unsqueeze(2).to_broadcast([128, 8, 16]), op=ALU.mult), [ev0.r, zs.r], [tm.r])
                            for k3 in range(3):
                                P.op("pe", "transpose", dict(out=pf[6][:, k3 * 128:(k3 + 1) * 128], in_=tm[:, k3, :], identity=identf[:, :]),
                                     [tm.r, identf.r], [pf[6].r])
                            evac(tT[:, :, tcol], pf[6][:, 0:384].rearrange("p (a b) -> p a b", b=128), [pf[6].r], [tT.r])

                        def stA(nb):
                            k = nb % 2
                            for p in range(2):
                                P.op("pool" if p == 0 else "dve", "tensor_copy",
                                     dict(out=qrep[k][:, p, :, :].rearrange("d t (h a) -> d t h a", a=16),
                                          in_=qTb[:, p:16:2, nb * 4:nb * 4 + 4].rearrange("d h t -> d t h").unsqueeze(3).to_broadcast([128, 4, 8, 16])),
                                     [qTb.r], [qrep[k].rs[p]])

                        def stB(nb):
                            k = nb % 2
                            for u in range(4):
                                us = slice(u * 128, (u + 1) * 128)
                                P.op("pe", "matmul", dict(out=S0[k][:, us], lhsT=qrep[k][:, 0, u, :], rhs=KT[:, 0, :],
                                                          start=True, stop=True), [qrep[k].rs[0], KT.r], [S0[k].r])
                                P.op("pe", "matmul", dict(out=S1[k][:, us], lhsT=qrep[k][:, 1, u, :], rhs=KT[:, 1, :],
                                                          start=True, stop=True), [qrep[k].rs[1], KT.r], [S1[k].r])
                            P.op("act", "activation", dict(out=e1[k][:, :, :], in_=S1[k][:, :].rearrange("p (a b) -> p a b", b=128), func=AF.Exp),
                                 [S1[k].r], [e1[k].r])

                        def stC(nb):
                            k = nb % 2
                            for u in range(4):
                                tl = nb * 4 + u
                                us = slice(u * 128, (u + 1) * 128)
                                P.op("dve", "scalar_tensor_tensor", dict(out=Fb[k][:, u, :], in0=S1[k][:, us], scalar=tT[:, 0, tl:tl + 1], in1=e1[k][:, u, :],
                                                                         op0=ALU.is_ge, op1=ALU.mult), [S1[k].r, tT.r, e1[k].r], [Fb[k].rs[u]])
                                P.op("dve", "tensor_scalar", dict(out=E0b[k][:, u, :], in0=S0[k][:, us], scalar1=tT[:, 1, tl:tl + 1], scalar2=tT[:, 2, tl:tl + 1],
                                                                  op0=ALU.is_equal, op1=ALU.mult), [S0[k].r, tT.r], [E0b[k].rs[u]])

                        def stD(nb):
                            k = nb % 2
                            for u in range(4):
                                us = slice(u * 128, (u + 1) * 128)
                                P.op("pe", "matmul", dict(out=GT[k][:, us], lhsT=Fb[k][:, u, :], rhs=E0b[k][:, u, :], start=True, stop=True),
                                     [Fb[k].rs[u], E0b[k].rs[u]], [GT[k].r])
                            tg = tb * 512 + nb * 4
                            ss = (tg // 64) % 2
                            t64 = tg % 64
                            P.op("act", "copy", dict(out=stg[ss][:, :, t64:t64 + 4].rearrange("j i t -> j t i"),
                                                     in_=GT[k][:, :].rearrange("p (a b) -> p a b", b=128)), [GT[k].r], [stg[ss].r])
                            if t64 + 4 == 64:
                                t0 = tg + 4 - 64
                                for i4 in range(4):
                                    P.dma("sp", gscr[i4 * 32:(i4 + 1) * 32, :, t0:t0 + 64].rearrange("i j t -> j i t"), stg[ss][:, i4 * 32:(i4 + 1) * 32, :],
                                          semst[ss], reads=[stg[ss].r])

                        NB = 128
                        for it in range(-2, NB + 1):
                            if 0 <= it + 2 < NB:
                                stA(it + 2)
                            if 0 <= it + 1 < NB:
                                stB(it + 1)
                            if 0 <= it < NB:
                                stC(it)
                            if 0 <= it - 1 < NB:
                                stD(it - 1)
                P.barrier()
                with ExitStack() as s2:
                    Ub = [sb(s2, "Ub%d" % i, [128, 4, D], BF16) for i in range(2)]
                    Vb = [sb(s2, "Vb%d" % i, [128, 4, D], BF16) for i in range(2)]
                    UT = [sb(s2, "UT%d" % i, [128, 4, 8, 128], BF16) for i in range(2)]
                    Gb = [sb(s2, "Gb%d" % i, [128, 4, 256], BF16) for i in range(2)]
                    gl = [sb(s2, "gl%d" % i, [128, 256], BF16) for i in range(3)]
                    PTb = [sb(s2, "PTb%d" % i, [128, 256], BF16) for i in range(3)]
                    semU = [P.dma_sem() for _ in range(2)]
                    semV = [P.dma_sem() for _ in range(2)]
                    semG = [P.dma_sem() for _ in range(2)]
                    NEG_ = 32

                    def load_eg(eg):
                        i = eg % 2
                        wcast(Ub[i][:, :, :], eu_d[layer, eg * 512:(eg + 1) * 512, :].rearrange("(c p) d -> p c d", p=128), semU[i], Ub[i].r)
                        wcast(Vb[i][:, :, :], ev_d[layer, eg * 512:(eg + 1) * 512, :].rearrange("(c p) d -> p c d", p=128), semV[i], Vb[i].r)

                    def prep_eg(eg):
                        i = eg % 2
                        for ci in range(4):
                            for dc in range(8):
                                P.op("pe", "transpose", dict(out=pb16[:, dc * 128:(dc + 1) * 128], in_=Ub[i][:, ci, dc * 128:(dc + 1) * 128],
                                                             identity=identb[:, :]), [Ub[i].r, identb.r], [pb16.r])
                            evac(UT[i][:, ci, :, :], pb16[:, :].rearrange("p (a b) -> p a b", b=128), [pb16.r], [UT[i].r])

                    gcount = [0]

                    def load_g(eg, tb):
                        i = gcount[0] % 2
                        gcount[0] += 1
                        P.dma("sp", Gb[i][:, :, :], gscr[eg * 4:(eg + 1) * 4, :, tb * 256:(tb + 1) * 256].rearrange("i j t -> j i t"), semG[i], writes=[Gb[i].r])
                        return Gb[i]

                    def pv_stage(item):
                        eg, tb, ci, n, gbuf = item
                        i = eg % 2
                        for t2 in range(2):
                            for hf in range(2):
                                acc = pf[t2 * 2 + hf]
                                P.op("pe", "matmul", dict(out=acc[:, :], lhsT=PTb[n % 3][:, t2 * 128:(t2 + 1) * 128], rhs=Vb[i][:, ci, hf * 512:(hf + 1) * 512],
                                                          start=(ci == 0), stop=(ci == 3)), [PTb[n % 3].r, Vb[i].r], [acc.r])
                        if ci == 3:
                            for t2 in range(2):
                                tt = tb * 2 + t2
                                for hf in range(2):
                                    acc = pf[t2 * 2 + hf]
                                    P.op("dve", "tensor_tensor", dict(out=xres[:, tt, hf * 512:(hf + 1) * 512], in0=xres[:, tt, hf * 512:(hf + 1) * 512],
                                                                      in1=acc[:, :], op=ALU.add), [xres.rs[tt], acc.r], [xres.rs[tt]])

                    load_eg(0)
                    pendq = []
                    n = 0
                    gnext = load_g(0, 0)
                    for eg in range(NEG_):
                        if eg + 1 < NEG_:
                            load_eg(eg + 1)
                        prep_eg(eg)
                        i = eg % 2
                        for tb in range(8):
                            gbuf = gnext
                            if tb + 1 < 8:
                                gnext = load_g(eg, tb + 1)
                            elif eg + 1 < NEG_:
                                gnext = load_g(eg + 1, 0)
                            for ci in range(4):
                                bank = pf[4 + (n % 3)]
                                for dc in range(8):
                                    P.op("pe", "matmul", dict(out=bank[:, 0:256], lhsT=UT[i][:, ci, dc, :], rhs=h2T[:, dc, tb * 256:(tb + 1) * 256],
                                                              start=(dc == 0), stop=(dc == 7)), [UT[i].r, h2T.rs[tb // 2]], [bank.r])
                                P.op("act", "activation", dict(out=gl[n % 3][:, :], in_=bank[:, 0:256], func=AF.Gelu), [bank.r], [gl[n % 3].r])
                                P.op("dve", "tensor_tensor", dict(out=PTb[n % 3][:, :], in0=gl[n % 3][:, :], in1=gbuf[:, ci, :], op=ALU.mult),
                                     [gl[n % 3].r, gbuf.r], [PTb[n % 3].r])
                                pendq.append((eg, tb, ci, n, gbuf))
                                if len(pendq) > 2:
                                    pv_stage(pendq.pop(0))
                                n += 1
                        while pendq:
                            pv_stage(pendq.pop(0))
            P.barrier()

        for layer in range(depth):
            phase_attention(layer)
            if debug == "A" and layer == depth - 1:
                dump_x(dbg_d)
                break
            phase_peer(layer)
            if debug == "P" and layer == depth - 1:
                dump_x(dbg_d)
        with ExitStack() as sfin:
            ostg = [sb(sfin, "ostg%d" % i, [128, D], F32) for i in range(2)]
            semo = [P.dma_sem() for _ in range(2)]
            load_gbc(nf_d[0:1, :])
            for tt in range(NT):
                o = ostg[tt % 2]
                rms_tile(tt, o[:, :], o.r)
                P.dma("sp", out_d[tt * 128:(tt + 1) * 128, :], o[:, :], semo[tt % 2], reads=[o.r])
            P.barrier()
        P.emit()
    return nc


_CONSTS = None


def _consts():
    global _CONSTS
    if _CONSTS is None:
        p = np.arange(128)
        cm = np.where(p[None, :] >= p[:, None], 0.0, NEG).astype(np.float32)
        ident = np.eye(128, dtype=np.float32)
        s127 = np.zeros((128, 128), np.float32)
        s127[127, :] = 1.0
        bidx = np.stack([t5_bucket_np(dl * 128 + p[None, :] - p[:, None]) for dl in range(2)], 0)
        _CONSTS = (cm, ident, s127, bidx)
    return _CONSTS


def kernel(**inputs):
    cm, ident, s127, bidx = _consts()
    f = lambda a: np.ascontiguousarray(np.asarray(a, dtype=np.float32))
    rel_bias = f(inputs["rel_bias"])
    bt = np.ascontiguousarray(np.transpose(rel_bias[bidx], (1, 0, 3, 2)))
    shared = {k: f(inputs[k]) for k in ("norm1_g", "w_in", "b_forget", "diff_lambda", "diff_subln_g", "w_diff_o", "w_fox_o",
                                        "w_out", "norm2_g", "w_query", "sub_keys", "expert_u", "expert_v")}
    shared["rel_bias"] = rel_bias
    shared["final_norm_g"] = f(inputs["final_norm_g"]).reshape(1, D)
    shared["bias_tiles"] = bt
    shared["cmask"] = cm
    shared["ident"] = ident
    shared["sel127"] = s127
    x = f(inputs["x"])
    nb = x.shape[0]
    nc = build_program()
    in_maps = []
    for b in range(nb):
        m = dict(shared)
        m["x"] = np.ascontiguousarray(x[b])
        in_maps.append(m)
    res = run_bass_kernel_spmd(nc, in_maps, core_ids=list(range(nb)))
    return np.stack([np.asarray(r["out"], dtype=np.float32) for r in res.results], axis=0)
```
